# Optimizing a Trainium2 kernel written in Bass

```python
import jax
import jax.numpy as jnp
from jax import lax
import numpy as np


D_MODEL = 1024
BATCH = 8
SEQ = 4096
DEPTH = 2

GRID_W = 64
CTX_LEN = 256
HEAD_DIM = 64
FOURIER_GROUPS = 4
FOURIER_GROUP_DIM = 64
FOURIER_WIDTH = FOURIER_GROUPS * FOURIER_GROUP_DIM
SWA_HEADS = 6
SWA_KV_HEADS = 2
SWA_Q_WIDTH = SWA_HEADS * HEAD_DIM
SWA_KV_WIDTH = SWA_KV_HEADS * HEAD_DIM
WINDOW = 128
BLOCK = 128
MLA_HEADS = 6
MLA_NOPE_DIM = 64
MLA_ROPE_DIM = 32
MLA_V_DIM = 64
MLA_Q_RANK = 256
MLA_KV_RANK = 128
MLA_SCALE = (MLA_NOPE_DIM + MLA_ROPE_DIM) ** -0.5
D_MIX = FOURIER_WIDTH + SWA_Q_WIDTH + MLA_HEADS * MLA_V_DIM
OFF_SWA_Q = FOURIER_WIDTH
OFF_SWA_K = OFF_SWA_Q + SWA_Q_WIDTH
OFF_SWA_V = OFF_SWA_K + SWA_KV_WIDTH
OFF_MLA_CQ = OFF_SWA_V + SWA_KV_WIDTH
OFF_MLA_CKV = OFF_MLA_CQ + MLA_Q_RANK
OFF_MLA_KR = OFF_MLA_CKV + MLA_KV_RANK
D_IN = OFF_MLA_KR + MLA_ROPE_DIM
IN_SPLITS = (OFF_SWA_Q, OFF_SWA_K, OFF_SWA_V, OFF_MLA_CQ, OFF_MLA_CKV, OFF_MLA_KR)
D_FF = 4 * D_MODEL
ROPE_THETA = 10000.0
NORM_EPS = 1e-6
NEG_INF = -1e30

kernel_name = 'hybrid_dit_fourier_swa_mla'


def rms_norm(x, g):
    xf = x.astype(jnp.float32)
    y = xf * lax.rsqrt(jnp.mean(xf * xf, axis=-1, keepdims=True) + NORM_EPS)
    return (y * g.astype(jnp.float32)).astype(x.dtype)


def modulate(h, shift, scale):
    return h * (1 + scale) + shift


def axial_rope_tables(rows, dim):
    r, col = jnp.meshgrid(jnp.arange(rows, dtype=jnp.float32), jnp.arange(GRID_W, dtype=jnp.float32), indexing='ij')
    r = r.reshape(-1)
    col = col.reshape(-1)
    n_freq = dim // 4
    inv_freq = ROPE_THETA ** (-jnp.arange(n_freq, dtype=jnp.float32) / n_freq)
    ang = jnp.concatenate([r[:, None] * inv_freq[None, :], col[:, None] * inv_freq[None, :]], axis=-1)
    return jnp.cos(ang), jnp.sin(ang)


def apply_rope(x, cos, sin):
    half = x.shape[-1] // 2
    xf = x.astype(jnp.float32)
    x1, x2 = xf[..., :half], xf[..., half:]
    c, s = cos[:, None, :], sin[:, None, :]
    return jnp.concatenate([x1 * c - x2 * s, x1 * s + x2 * c], axis=-1).astype(x.dtype)


def fourier_mix(f, w_f):
    B_, S_, _ = f.shape
    fg = f.reshape(B_, S_, FOURIER_GROUPS, FOURIER_GROUP_DIM).astype(jnp.float32)
    spec = jnp.fft.fft2(fg, axes=(1, 3), norm='ortho').real.astype(f.dtype)
    out = jnp.einsum('bsgc,gcd->bsgd', spec, w_f)
    return out.reshape(B_, S_, FOURIER_WIDTH)


def softmax_with_sink(logits, sink_kg):
    sk = jnp.broadcast_to(sink_kg.astype(jnp.float32)[:, :, None, None], logits.shape[:-1] + (1,))
    p = jax.nn.softmax(jnp.concatenate([logits, sk], axis=-1), axis=-1)
    return p[..., :-1]


def swa_latent(q, k, v, k_ctx, v_ctx, sink):
    B_, S_, H, d = q.shape
    G = H // SWA_KV_HEADS
    nb = S_ // BLOCK
    qb = q.reshape(B_, nb, BLOCK, SWA_KV_HEADS, G, d)
    pad = ((0, 0), (BLOCK, BLOCK), (0, 0), (0, 0))
    kp = jnp.pad(k, pad).reshape(B_, nb + 2, BLOCK, SWA_KV_HEADS, d)
    vp = jnp.pad(v, pad).reshape(B_, nb + 2, BLOCK, SWA_KV_HEADS, d)
    kb = jnp.concatenate([kp[:, :-2], kp[:, 1:-1], kp[:, 2:]], axis=2)
    vb = jnp.concatenate([vp[:, :-2], vp[:, 1:-1], vp[:, 2:]], axis=2)
    scale = d ** -0.5
    s_loc = jnp.einsum('bnqkgd,bnjkd->bnkgqj', qb, kb).astype(jnp.float32) * scale
    s_ctx = jnp.einsum('bnqkgd,bckd->bnkgqc', qb, k_ctx).astype(jnp.float32) * scale
    qpos = jnp.arange(nb)[:, None] * BLOCK + jnp.arange(BLOCK)[None, :]
    kpos = jnp.arange(nb)[:, None] * BLOCK - BLOCK + jnp.arange(3 * BLOCK)[None, :]
    rel = kpos[:, None, :] - qpos[:, :, None]
    valid = (jnp.abs(rel) <= WINDOW) & (kpos[:, None, :] >= 0) & (kpos[:, None, :] < S_)
    s_loc = jnp.where(valid[None, :, None, None, :, :], s_loc, NEG_INF)
    p = softmax_with_sink(jnp.concatenate([s_loc, s_ctx], axis=-1), sink.reshape(SWA_KV_HEADS, G)).astype(v.dtype)
    o = (jnp.einsum('bnkgqj,bnjkd->bnqkgd', p[..., :3 * BLOCK], vb)
         + jnp.einsum('bnkgqc,bckd->bnqkgd', p[..., 3 * BLOCK:], v_ctx))
    return o.reshape(B_, S_, H * d)


def swa_context(q, k, v, sink):
    B_, L, H, d = q.shape
    G = H // SWA_KV_HEADS
    qg = q.reshape(B_, L, SWA_KV_HEADS, G, d)
    s = jnp.einsum('bqkgd,bckd->bkgqc', qg, k).astype(jnp.float32) * (d ** -0.5)
    p = softmax_with_sink(s, sink.reshape(SWA_KV_HEADS, G)).astype(v.dtype)
    o = jnp.einsum('bkgqc,bckd->bqkgd', p, v)
    return o.reshape(B_, L, H * d)


def mla_queries(cq, q_norm, w_uq):
    B_, S_, _ = cq.shape
    q = (rms_norm(cq, q_norm) @ w_uq).reshape(B_, S_, MLA_HEADS, MLA_NOPE_DIM + MLA_ROPE_DIM)
    return q[..., :MLA_NOPE_DIM], q[..., MLA_NOPE_DIM:]


def mla_keys_values(ckv, kv_norm, w_ukv):
    B_, S_, _ = ckv.shape
    kv = (rms_norm(ckv, kv_norm) @ w_ukv).reshape(B_, S_, MLA_HEADS, MLA_NOPE_DIM + MLA_V_DIM)
    return kv[..., :MLA_NOPE_DIM], kv[..., MLA_NOPE_DIM:]


def mla_latent(qn, qr, kn, kr, v, kn_c, kr_c, v_c):
    B_, S_, H, _ = qn.shape
    nb = S_ // BLOCK

    def blocks(t):
        return t.reshape(B_, nb, BLOCK, *t.shape[2:]).swapaxes(0, 1)

    def one_block(qs):
        qn_i, qr_i = qs
        s_lat = jnp.einsum('bqhd,bkhd->bhqk', qn_i, kn) + jnp.einsum('bqhr,bkr->bhqk', qr_i, kr)
        s_ctx = jnp.einsum('bqhd,bchd->bhqc', qn_i, kn_c) + jnp.einsum('bqhr,bcr->bhqc', qr_i, kr_c)
        logits = jnp.concatenate([s_lat, s_ctx], axis=-1).astype(jnp.float32) * MLA_SCALE
        p = jax.nn.softmax(logits, axis=-1).astype(v.dtype)
        return (jnp.einsum('bhqk,bkhd->bqhd', p[..., :S_], v)
                + jnp.einsum('bhqc,bchd->bqhd', p[..., S_:], v_c))

    o = lax.map(one_block, (blocks(qn), blocks(qr)))
    return o.swapaxes(0, 1).reshape(B_, S_, H * MLA_V_DIM)


def mla_context(qn, qr, kn, kr, v):
    B_, L, H, _ = qn.shape
    s = jnp.einsum('bqhd,bkhd->bhqk', qn, kn) + jnp.einsum('bqhr,bkr->bhqk', qr, kr)
    p = jax.nn.softmax(s.astype(jnp.float32) * MLA_SCALE, axis=-1).astype(v.dtype)
    return jnp.einsum('bhqk,bkhd->bqhd', p, v).reshape(B_, L, H * MLA_V_DIM)


def squared_relu_mlp(h, w1, w2):
    return jnp.square(jax.nn.relu(h @ w1)) @ w2


def setup_inputs(seed: int = 0) -> dict:
    key = jax.random.key(seed)
    ks = jax.random.split(key, 20)

    def nrm(k, shape, scale):
        return jax.random.normal(k, shape, jnp.float32) * scale

    return {
        'x': nrm(ks[0], (BATCH, SEQ, D_MODEL), 1.0),
        'c': nrm(ks[1], (BATCH, D_MODEL), 1.0),
        'ctx': nrm(ks[2], (BATCH, CTX_LEN, D_MODEL), 1.0),
        'c_ctx': nrm(ks[3], (D_MODEL,), 1.0),
        'w_ada': nrm(ks[4], (DEPTH, D_MODEL, 6 * D_MODEL), 0.5 * D_MODEL ** -0.5),
        'b_ada': nrm(ks[5], (DEPTH, 6 * D_MODEL), 0.02),
        'norm1_g': 1.0 + nrm(ks[6], (DEPTH, D_MODEL), 0.05),
        'norm2_g': 1.0 + nrm(ks[7], (DEPTH, D_MODEL), 0.05),
        'w_in': nrm(ks[8], (DEPTH, D_MODEL, D_IN), D_MODEL ** -0.5),
        'w_fourier': nrm(ks[9], (DEPTH, FOURIER_GROUPS, FOURIER_GROUP_DIM, FOURIER_GROUP_DIM), FOURIER_GROUP_DIM ** -0.5),
        'swa_sink': nrm(ks[10], (DEPTH, SWA_HEADS), 0.5),
        'mla_q_norm': 1.0 + nrm(ks[11], (DEPTH, MLA_Q_RANK), 0.05),
        'w_uq': nrm(ks[12], (DEPTH, MLA_Q_RANK, MLA_HEADS * (MLA_NOPE_DIM + MLA_ROPE_DIM)), MLA_Q_RANK ** -0.5),
        'mla_kv_norm': 1.0 + nrm(ks[13], (DEPTH, MLA_KV_RANK), 0.05),
        'w_ukv': nrm(ks[14], (DEPTH, MLA_KV_RANK, MLA_HEADS * (MLA_NOPE_DIM + MLA_V_DIM)), MLA_KV_RANK ** -0.5),
        'w_out': nrm(ks[15], (DEPTH, D_MIX, D_MODEL), D_MIX ** -0.5),
        'w_mlp1': nrm(ks[16], (DEPTH, D_MODEL, D_FF), D_MODEL ** -0.5),
        'w_mlp2': nrm(ks[17], (DEPTH, D_FF, D_MODEL), D_FF ** -0.5),
        'final_norm_g': 1.0 + nrm(ks[18], (D_MODEL,), 0.05),
    }


def reference(x, c, ctx, c_ctx, w_ada, b_ada, norm1_g, norm2_g, w_in, w_fourier, swa_sink,
              mla_q_norm, w_uq, mla_kv_norm, w_ukv, w_out, w_mlp1, w_mlp2, final_norm_g):
    B_, S_, _ = x.shape
    L = ctx.shape[1]
    rows = S_ // GRID_W
    cos_h, sin_h = axial_rope_tables(rows, HEAD_DIM)
    cos_r, sin_r = axial_rope_tables(rows, MLA_ROPE_DIM)
    silu_c = jax.nn.silu(c)
    silu_cc = jax.nn.silu(c_ctx)[None, :]
    h, hc = x, ctx
    for l in range(DEPTH):
        last = l == DEPTH - 1
        mod = (silu_c @ w_ada[l] + b_ada[l])[:, None, :]
        mod_c = (silu_cc @ w_ada[l] + b_ada[l])[:, None, :]
        sh1, sc1, g1, sh2, sc2, g2 = jnp.split(mod, 6, axis=-1)
        csh1, csc1, cg1, csh2, csc2, cg2 = jnp.split(mod_c, 6, axis=-1)

        u = modulate(rms_norm(h, norm1_g[l]), sh1, sc1) @ w_in[l]
        uc = modulate(rms_norm(hc, norm1_g[l]), csh1, csc1) @ w_in[l]
        f, q, k, v, cq, ckv, kr = jnp.split(u, IN_SPLITS, axis=-1)
        fc, qc, kc, vc, cqc, ckvc, krc = jnp.split(uc, IN_SPLITS, axis=-1)

        q = apply_rope(q.reshape(B_, S_, SWA_HEADS, HEAD_DIM), cos_h, sin_h)
        k = apply_rope(k.reshape(B_, S_, SWA_KV_HEADS, HEAD_DIM), cos_h, sin_h)
        v = v.reshape(B_, S_, SWA_KV_HEADS, HEAD_DIM)
        kc = kc.reshape(B_, L, SWA_KV_HEADS, HEAD_DIM)
        vc = vc.reshape(B_, L, SWA_KV_HEADS, HEAD_DIM)
        swa_out = swa_latent(q, k, v, kc, vc, swa_sink[l])

        qn, qr = mla_queries(cq, mla_q_norm[l], w_uq[l])
        qr = apply_rope(qr, cos_r, sin_r)
        kn, mv = mla_keys_values(ckv, mla_kv_norm[l], w_ukv[l])
        kr = apply_rope(kr[:, :, None, :], cos_r, sin_r)[:, :, 0, :]
        knc, mvc = mla_keys_values(ckvc, mla_kv_norm[l], w_ukv[l])
        mla_out = mla_latent(qn, qr, kn, kr, mv, knc, krc, mvc)

        mix = jnp.concatenate([fourier_mix(f, w_fourier[l]), swa_out, mla_out], axis=-1) @ w_out[l]
        h = h + g1 * mix
        h = h + g2 * squared_relu_mlp(modulate(rms_norm(h, norm2_g[l]), sh2, sc2), w_mlp1[l], w_mlp2[l])

        if not last:
            qc = qc.reshape(B_, L, SWA_HEADS, HEAD_DIM)
            qnc, qrc = mla_queries(cqc, mla_q_norm[l], w_uq[l])
            mix_c = jnp.concatenate([fourier_mix(fc, w_fourier[l]),
                                     swa_context(qc, kc, vc, swa_sink[l]),
                                     mla_context(qnc, qrc, knc, krc, mvc)], axis=-1) @ w_out[l]
            hc = hc + cg1 * mix_c
            hc = hc + cg2 * squared_relu_mlp(modulate(rms_norm(hc, norm2_g[l]), csh2, csc2), w_mlp1[l], w_mlp2[l])
    return rms_norm(h, final_norm_g)
```

```python
import numpy as np
import ml_dtypes
from contextlib import ExitStack
import concourse.bass as bass
import concourse.mybir as mybir
from concourse.bass_utils import run_bass_kernel_spmd

F32 = mybir.dt.float32
BF16 = mybir.dt.bfloat16
AF = mybir.ActivationFunctionType
ALU = mybir.AluOpType

D = 1024
S_LAT = 4096
L_CTX = 256
T = S_LAT + L_CTX
DEPTH = 2
DFF = 4096
D_IN = 1312
EPS = 1e-6
MLA_SCALE = 96.0 ** -0.5
SWA_SCALE = 0.125
C_F, C_Q, C_K, C_CQ, C_CKV, C_KR, C_V = 0, 256, 640, 768, 1024, 1152, 1184

ENGS = ["pe", "act", "dve", "pool", "sp"]
NDSEM = 8


class _Rec:
    def __getattr__(self, name):
        def f(*a, **k):
            self.call = (name, a, k)
            return self
        return f


class Sched:
    def __init__(self, nc):
        self.nc = nc
        self.ops = {e: [] for e in ENGS}
        self.res = {}
        self.known = {e: {} for e in ENGS}
        self.count = {e: 0 for e in ENGS}
        self.dma_n = {e: 0 for e in ENGS}
        self.pending = {e: {} for e in ENGS}
        self.out_dmas = []

    def _dep(self, eng, dep, waits):
        key, val = dep[:-1], dep[-1]
        if key == ("c", "pe") and eng == "pe":
            return
        if self.known[eng].get(key, 0) >= val:
            return
        self.known[eng][key] = val
        waits[key] = max(waits.get(key, 0), val)

    def op(self, eng, fn, reads=(), writes=(), dma=False, final=False):
        pb = [r for r in reads if r.startswith("pb")]
        if pb:
            reads = [r for r in reads if not r.startswith("pb")]
            writes = list(writes) + pb
        waits = {}
        for key, val in self.pending[eng].items():
            self._dep(eng, key + (val,), waits)
        self.pending[eng] = {}
        for r in reads:
            st = self.res.get(r)
            if st and st["w"]:
                self._dep(eng, st["w"], waits)
        for r in writes:
            st = self.res.get(r)
            if st:
                if st["w"]:
                    self._dep(eng, st["w"], waits)
                for d in st["r"]:
                    self._dep(eng, d, waits)
        if dma:
            n = self.dma_n[eng]
            slot = n % NDSEM
            val = 16 * (n // NDSEM + 1)
            self.dma_n[eng] += 1
            if n >= NDSEM:
                self._dep(eng, ("d", eng, slot, val - 16), waits)
            me = ("d", eng, slot, val)
            if final:
                self.out_dmas.append(me)
        else:
            self.count[eng] += 1
            me = ("c", eng, self.count[eng])
        rec = _Rec()
        fn(rec)
        self.ops[eng].append((rec.call, waits, me))
        for r in reads:
            self.res.setdefault(r, {"w": None, "r": []})["r"].append(me)
        for r in writes:
            self.res[r] = {"w": me, "r": []}
        return me

    def barrier(self):
        deps = {}
        for e in ENGS:
            if self.count[e] > 0:
                deps[("c", e)] = self.count[e]
            n = self.dma_n[e]
            for slot in range(min(n, NDSEM)):
                last = n - 1 - ((n - 1 - slot) % NDSEM)
                deps[("d", e, slot)] = 16 * (last // NDSEM + 1)
        for e in ENGS:
            for k, v in deps.items():
                self.pending[e][k] = max(self.pending[e].get(k, 0), v)
        self.res = {}

    def emit(self):
        nc = self.nc
        with ExitStack() as st:
            csem = {e: st.enter_context(nc.semaphore("c_" + e)) for e in ENGS}
            dsem = {(e, s): st.enter_context(nc.semaphore("d_%s%d" % (e, s)))
                    for e in ("sp", "pool", "act") for s in range(NDSEM)}
            block = st.enter_context(nc.Block())

            def semof(key):
                return csem[key[1]] if key[0] == "c" else dsem[(key[1], key[2])]

            def run(engname, eng):
                for fn, waits, me in self.ops[engname]:
                    for key, val in waits.items():
                        eng.wait_ge(semof(key), val)
                    ins = getattr(eng, fn[0])(*fn[1], **fn[2])
                    if me[0] == "c":
                        ins.then_inc(csem[engname], 1)
                    else:
                        ins.then_inc(dsem[(engname, me[2])], 16)
                if engname == "sp":
                    for me in self.out_dmas:
                        eng.wait_ge(dsem[(me[1], me[2])], me[3])

            @block.tensor
            def _(e):
                run("pe", e)

            @block.scalar
            def _(e):
                run("act", e)

            @block.vector
            def _(e):
                run("dve", e)

            @block.gpsimd
            def _(e):
                run("pool", e)

            @block.sync
            def _(e):
                run("sp", e)


class Arena:
    def __init__(self, ap_f32, nwords):
        self.t = ap_f32
        self.n = nwords
        self.off = 0

    def alloc(self, shape, dt):
        n = int(np.prod(shape))
        words = n if dt == F32 else (n + 1) // 2
        words = (words + 7) // 8 * 8
        assert self.off + words <= self.n, ("arena overflow", self.off, words, self.n)
        ap = self.t[:, self.off:self.off + words]
        self.off += words
        if dt != F32:
            ap = ap.bitcast(dt)
        ap = ap[:, 0:n]
        if len(shape) == 2:
            return ap.rearrange("p (a b) -> p a b", a=shape[0])
        if len(shape) == 3:
            return ap.rearrange("p (a b c) -> p a b c", a=shape[0], b=shape[1])
        return ap


class Ring:
    def __init__(self, arena, name, n, shape, dt):
        self.bufs = [arena.alloc(shape, dt) for _ in range(n)]
        self.names = ["%s#%d" % (name, i) for i in range(n)]
        self.i = 0

    def next(self):
        k = self.i % len(self.bufs)
        self.i += 1
        return self.bufs[k], self.names[k]


def _rope_tabs(dim):
    rows = S_LAT // 64
    r, col = np.meshgrid(np.arange(rows, dtype=np.float32), np.arange(64, dtype=np.float32), indexing="ij")
    r = r.reshape(-1)
    col = col.reshape(-1)
    nf = dim // 4
    inv = (np.float32(10000.0) ** (-np.arange(nf, dtype=np.float32) / np.float32(nf))).astype(np.float32)
    ang = np.concatenate([r[:, None] * inv[None, :], col[:, None] * inv[None, :]], axis=-1).astype(np.float32)
    c = np.cos(ang).astype(np.float32)
    s = np.sin(ang).astype(np.float32)
    half = dim // 2
    C2 = np.ones((dim, T), np.float32)
    S2 = np.zeros((dim, T), np.float32)
    C2[:half, :S_LAT] = c.T
    C2[half:, :S_LAT] = c.T
    S2[:half, :S_LAT] = -s.T
    S2[half:, :S_LAT] = s.T
    return C2, S2


def _consts():
    k = {}
    C2, S2 = _rope_tabs(64)
    ropeS = np.zeros((2, 128, T), np.float32)
    ropeS[0] = np.concatenate([C2, C2], 0)
    ropeS[1] = np.concatenate([S2, S2], 0)
    C2m, S2m = _rope_tabs(32)
    ropeM = np.zeros((2, 128, T), np.float32)
    ropeM[0] = 1.0
    for base in (0, 64):
        ropeM[0, base:base + 32] = C2m
        ropeM[1, base:base + 32] = S2m
    k["ropeS"] = ropeS
    k["ropeM"] = ropeM
    cm = np.zeros((5, 128, 128), np.float32)
    cm[0] = np.eye(128)
    cm[1] = 1.0
    for b0 in (0, 64):
        for d in range(32):
            cm[2, b0 + d, b0 + d + 32] = 1.0
            cm[2, b0 + d + 32, b0 + d] = 1.0
    for b0 in (0, 64):
        for d in range(16):
            cm[3, b0 + d, b0 + d + 16] = 1.0
            cm[3, b0 + d + 16, b0 + d] = 1.0
    a = np.arange(64)
    ang = 2 * np.pi * np.outer(a, a) / 64.0
    Cr, Sr = np.cos(ang) / 8.0, np.sin(ang) / 8.0
    cm[4, 0:64, 0:64] = Cr
    cm[4, 64:128, 0:64] = Sr
    cm[4, 0:64, 64:128] = -Sr
    cm[4, 64:128, 64:128] = Cr
    k["cmat"] = np.ascontiguousarray(cm.transpose(1, 0, 2))
    cf = np.zeros((3, 128, 128), np.float32)
    cf[0] = np.eye(128)
    for g in range(2):
        cf[1, g * 64:(g + 1) * 64, g * 64:(g + 1) * 64] = Cr
        cf[2, g * 64:(g + 1) * 64, g * 64:(g + 1) * 64] = -Sr
    k["cmatf"] = np.ascontiguousarray(cf.transpose(1, 0, 2))
    c = np.arange(64)[:, None, None]
    ka = np.arange(64)[None, :, None]
    kb = np.arange(64)[None, None, :]
    th = 2 * np.pi * c * (64 * kb + ka) / 4096.0
    m2 = np.zeros((128, 64, 64), np.float32)
    m2[0:64] = np.cos(th) / 8.0
    m2[64:128] = np.sin(th) / 8.0
    k["m2"] = m2
    tok = (np.arange(2)[None, :, None] * 128 + np.arange(128)[:, None, None]).astype(np.float64)
    kk = np.arange(256)[None, None, :]
    th = 2 * np.pi * tok * kk / 256.0
    d256 = np.zeros((128, 2, 2, 256), np.float32)
    d256[:, :, 0, :] = np.cos(th) / 16.0
    d256[:, :, 1, :] = np.sin(th) / 16.0
    k["d256"] = d256
    j = np.arange(128)[:, None, None]
    off = np.arange(-1, 5)[None, :, None]
    i = np.arange(512)[None, None, :]
    k["masks"] = (np.abs(128 * off + j - i) <= 128).astype(np.float32)
    return k


def _perm_in():
    idx = list(range(0, 256))
    for c in range(3):
        for h in (c, c + 3):
            idx += list(range(256 + h * 64, 256 + (h + 1) * 64))
    idx += list(range(640, 768))
    idx += list(range(896, 1152))
    idx += list(range(1152, 1280))
    idx += list(range(1280, 1312))
    idx += list(range(768, 896))
    return np.array(idx)


def _perm_ukv():
    kn, v = [], []
    for h in range(6):
        kn += list(range(h * 128, h * 128 + 64))
        v += list(range(h * 128 + 64, h * 128 + 128))
    return np.array(kn + v)


def build(debug=()):
    nc = bass.Bass("TRN2", target_bir_lowering=False)

    def din(name, shape, dt=F32):
        return nc.dram_tensor(name, list(shape), dt, kind="ExternalInput").ap()

    def dscr(name, shape, dt):
        kind = "ExternalOutput" if name in debug else "Internal"
        return nc.dram_tensor(name, list(shape), dt, kind=kind).ap()

    x = din("x", [S_LAT, D])
    ctx = din("ctx", [L_CTX, D])
    cT = din("cT", [128, 8, 2])
    w_ada = din("w_ada", [DEPTH, D, 6 * D])
    b_adaT = din("b_adaT", [DEPTH, 128, 48])
    n1T = din("n1T", [DEPTH, 128, 8])
    n2T = din("n2T", [DEPTH, 128, 8])
    fg = din("fg", [1, D])
    w_in = din("w_in", [DEPTH, D, D_IN])
    w_f = din("w_f", [DEPTH, 4, 64, 64])
    sink = din("sink", [DEPTH, 6])
    gqT = din("gqT", [DEPTH, 128, 2])
    gkvT = din("gkvT", [DEPTH, 128, 1])
    w_uq = din("w_uq", [DEPTH, 256, 576])
    w_ukv = din("w_ukv", [DEPTH, 128, 768])
    w_out = din("w_out", [DEPTH, D, D])
    w_mlp1 = din("w_mlp1", [DEPTH, D, DFF])
    w_mlp2 = din("w_mlp2", [DEPTH, DFF, D])
    ropeS = din("ropeS", [2, 128, T])
    ropeM = din("ropeM", [2, 128, T])
    cmat = din("cmat", [128, 5, 128])
    cmatf = din("cmatf", [128, 3, 128])
    m2d = din("m2", [128, 64, 64])
    d256d = din("d256", [128, 2, 2, 256])
    masksd = din("masks", [128, 6, 512])
    out = nc.dram_tensor("out", [S_LAT, D], F32, kind="ExternalOutput").ap()

    h_scr = dscr("h_scr", [T, D], F32)
    mod_scr = dscr("mod_scr", [DEPTH, 2, 6 * D], F32)
    y_scr = dscr("y_scr", [T, 512], BF16)
    z_scr = dscr("z_scr", [128, 64, 256], BF16)
    qs_scr = dscr("qs_scr", [3, 128, T], BF16)
    ks_scr = dscr("ks_scr", [128, T], BF16)
    vs_scr = dscr("vs_scr", [T, 128], BF16)
    qm_scr = dscr("qm_scr", [6, 96, T], BF16)
    kn_scr = dscr("kn_scr", [384, T], BF16)
    kr_scr = dscr("kr_scr", [32, T], BF16)
    vm_scr = dscr("vm_scr", [T, 384], BF16)
    mixT_scr = dscr("mixT_scr", [D, T], BF16)

    S = Sched(nc)
    SBW = 52736
    with ExitStack() as st:
        arena_t = st.enter_context(nc.sbuf_tensor("arena", [128, SBW], F32))
        pall = st.enter_context(nc.psum_tensor("pall", [128, 4096], F32))

        def bank(b, w=512):
            return pall[:, b * 512:b * 512 + w]

        def bname(b):
            return "pb%d" % b

        A = Arena(arena_t, SBW)
        cm = A.alloc([5, 128], BF16)
        ident, ones, PmS, PmM, D1 = (cm[:, i, :] for i in range(5))
        cmf = A.alloc([3, 128], F32)
        identf, CCre, CCim = (cmf[:, i, :] for i in range(3))
        epst = A.alloc([1], F32)
        modT = A.alloc([96], F32)
        amod = A.alloc([2, 2, 8], F32)
        smod = A.alloc([2, 2, 8], F32)
        nT = A.alloc([2, 8], F32)
        gq = A.alloc([2], F32)
        gkv = A.alloc([1], F32)
        exps = A.alloc([6], F32)
        gbc = A.alloc([2, D], F32)
        fgbc = A.alloc([D], F32)
        PERSIST = A.off

        S.op("pool", lambda e: e.dma_start(out=cm, in_=cmat), writes=["cm"], dma=True)
        S.op("sp", lambda e: e.dma_start(out=cmf, in_=cmatf), writes=["cmf"], dma=True)
        S.op("sp", lambda e: e.dma_start(out=fgbc, in_=fg.partition_broadcast(128)), writes=["fgbc"], dma=True)
        S.op("dve", lambda e: e.memset(epst, EPS), writes=["eps"])

        def blocks(width, with_ctx):
            bl = [(t0, width) for t0 in range(0, S_LAT, width)]
            if with_ctx:
                bl += [(t0, min(width, L_CTX)) for t0 in range(S_LAT, T, min(width, L_CTX))]
            return bl

        def phase_mod(l):
            S.barrier()
            A.off = PERSIST
            cTs = A.alloc([8, 2], F32)
            sil = A.alloc([8, 2], BF16)
            bT = A.alloc([48], F32)
            mrow = A.alloc([128], F32)
            wring = Ring(A, "wada", 2, [8, 1024], BF16)
            S.op("sp", lambda e: e.dma_start(out=cTs, in_=cT), writes=["cTs"], dma=True)
            S.op("sp", lambda e: e.dma_start(out=bT, in_=b_adaT[l]), writes=["bT"], dma=True)
            S.op("sp", lambda e: e.dma_start(out=nT[:, 0, :], in_=n1T[l]), writes=["nT"], dma=True)
            S.op("sp", lambda e: e.dma_start(out=nT[:, 1, :], in_=n2T[l]), writes=["nT"], dma=True)
            S.op("sp", lambda e: e.dma_start(out=gq, in_=gqT[l]), writes=["gq"], dma=True)
            S.op("sp", lambda e: e.dma_start(out=gkv, in_=gkvT[l]), writes=["gkv"], dma=True)
            S.op("sp", lambda e: e.dma_start(out=exps, in_=sink[l:l + 1, :].partition_broadcast(128)), writes=["exps"], dma=True)
            S.op("act", lambda e: e.activation(out=sil, in_=cTs, func=AF.Silu), reads=["cTs"], writes=["sil"])
            S.op("act", lambda e: e.activation(out=exps, in_=exps, func=AF.Exp), reads=["exps"], writes=["exps"])
            wv = w_ada[l].rearrange("(k p) n -> p k n", p=128)
            pb = bank(0, 96)
            for nb in range(6):
                wb, wn = wring.next()
                for kc in range(0, 8, 4):
                    S.op("pool", lambda e, wb=wb, nb=nb, kc=kc: e.dma_start(out=wb[:, kc:kc + 4, :], in_=wv[:, kc:kc + 4, nb * 1024:(nb + 1) * 1024]),
                         writes=[wn + "_%d" % (kc // 4)], dma=True)
                for cc in range(8):
                    chunk = nb * 8 + cc
                    for kc in range(8):
                        S.op("pe", lambda e, wb=wb, cc=cc, kc=kc, chunk=chunk: e.matmul(
                            pb[:, chunk * 2:chunk * 2 + 2], lhsT=wb[:, kc, cc * 128:(cc + 1) * 128], rhs=sil[:, kc, :],
                            start=(kc == 0), stop=(kc == 7)), reads=[wn + "_%d" % (kc // 4), "sil"], writes=[bname(0)])
            pbv = pb.rearrange("p (c j) -> p j c", j=2)
            for j in range(2):
                S.op("dve", lambda e, j=j: e.tensor_tensor(out=modT[:, j * 48:(j + 1) * 48], in0=pbv[:, j, :], in1=bT, op=ALU.add),
                     reads=[bname(0), "bT"], writes=["modT"])
            for ni, (s_sh, s_sc) in enumerate(((0, 1), (3, 4))):
                for j in range(2):
                    sc = modT[:, j * 48 + s_sc * 8: j * 48 + s_sc * 8 + 8]
                    shv = modT[:, j * 48 + s_sh * 8: j * 48 + s_sh * 8 + 8]
                    S.op("dve", lambda e, ni=ni, j=j, sc=sc: e.scalar_tensor_tensor(
                        out=amod[:, ni, j, :], in0=sc, scalar=1.0, in1=nT[:, ni, :], op0=ALU.add, op1=ALU.mult),
                        reads=["modT", "nT"], writes=["amod"])
                    S.op("dve", lambda e, ni=ni, j=j, shv=shv: e.tensor_copy(out=smod[:, ni, j, :], in_=shv),
                         reads=["modT"], writes=["smod"])
            S.op("pe", lambda e: e.transpose(out=bank(1, 128)[0:96, :], in_=modT, identity=identf), reads=["modT", "cmf"], writes=[bname(1)])
            S.op("dve", lambda e: e.tensor_copy(out=mrow[0:96, :], in_=bank(1, 128)[0:96, :]), reads=[bname(1)], writes=["mrow"])
            S.op("sp", lambda e: e.dma_start(out=mod_scr[l].rearrange("j (c p) -> (j c) p", p=128), in_=mrow[0:96, :]),
                 reads=["mrow"], writes=["mod_scr"], dma=True)

        def load_gate(l, sec):
            for j in range(2):
                S.op("sp", lambda e, j=j: e.dma_start(out=gbc[:, j, :], in_=mod_scr[l, j:j + 1, sec * D:(sec + 1) * D].partition_broadcast(128)),
                     reads=["mod_scr"], writes=["gbc"], dma=True)

        def norm_mod_T(htiles, nt, ni, j, xmT, xmn, ssb, xhring, tpbanks):
            junk = xhring.bufs[-1]
            for t in range(nt):
                ha, hn = htiles[t]
                S.op("act", lambda e, ha=ha, t=t: e.activation(out=junk, in_=ha, func=AF.Square, accum_out=ssb[:, t:t + 1]),
                     reads=[hn], writes=["junk", "ssb"])
            S.op("act", lambda e: e.activation(out=ssb[:, 4:4 + nt], in_=ssb[:, 0:nt], func=AF.Ln, scale=1.0 / D, bias=epst[:, 0:1]),
                 reads=["ssb", "eps"], writes=["ssb"])
            S.op("act", lambda e: e.activation(out=ssb[:, 8:8 + nt], in_=ssb[:, 4:4 + nt], func=AF.Exp, scale=-0.5),
                 reads=["ssb"], writes=["ssb"])
            for t in range(nt):
                ha, hn = htiles[t]
                xh = xhring.bufs[t % (len(xhring.bufs) - 1)]
                xn = xhring.names[t % (len(xhring.bufs) - 1)]
                S.op("act", lambda e, ha=ha, xh=xh, t=t: e.activation(out=xh, in_=ha, func=AF.Copy, scale=ssb[:, 8 + t:9 + t]),
                     reads=[hn, "ssb"], writes=[xn])
                tb = tpbanks[t % len(tpbanks)]
                tp = bank(tb).bitcast(BF16)
                for kc in range(8):
                    S.op("pe", lambda e, xh=xh, kc=kc, tp=tp: e.transpose(out=tp[:, kc * 128:(kc + 1) * 128], in_=xh[:, kc * 128:(kc + 1) * 128], identity=ident),
                         reads=[xn, "cm"], writes=[bname(tb)])
                tp3 = tp.rearrange("p (k c) -> p k c", k=8)
                dst = xmT[:, :, t * 128:(t + 1) * 128]
                S.op("dve", lambda e, tp3=tp3, dst=dst: e.tensor_tensor(out=dst, in0=tp3, in1=amod[:, ni, j, :].unsqueeze(2).to_broadcast([128, 8, 128]), op=ALU.mult),
                     reads=[bname(tb), "amod"], writes=[xmn])
                S.op("dve", lambda e, dst=dst: e.tensor_tensor(out=dst, in0=dst, in1=smod[:, ni, j, :].unsqueeze(2).to_broadcast([128, 8, 128]), op=ALU.add),
                     reads=[xmn, "smod"], writes=[xmn])

        def phase_A(l):
            S.barrier()
            A.off = PERSIST
            last = l == DEPTH - 1
            Win = A.alloc([8, D_IN], BF16)
            wuq = A.alloc([2, 576], BF16)
            wukv = A.alloc([768], BF16)
            wfbd = A.alloc([2, 128], F32)
            Abd = A.alloc([2, 256], BF16)
            hring = Ring(A, "hA", 8, [D], F32)
            xhring = Ring(A, "xhA", 3, [D], BF16)
            xmring = Ring(A, "xmA", 2, [8, 512], BF16)
            ssb = A.alloc([12], F32)
            rtab = Ring(A, "rtab", 2, [4, 512], F32)
            fT = A.alloc([2, 512], BF16)
            ysb = Ring(A, "ysb", 2, [512], BF16)
            qb = Ring(A, "qb", 2, [512], BF16)
            t1r = Ring(A, "t1", 2, [512], F32)
            t2r = Ring(A, "t2", 2, [512], F32)
            qo = Ring(A, "qo", 3, [512], BF16)
            sq = A.alloc([2, 512], BF16)
            lnv = A.alloc([512], F32)
            rbc = A.alloc([512], F32)
            cqn = A.alloc([2, 512], BF16)
            ckvn = A.alloc([512], BF16)
            knb = Ring(A, "knb", 2, [512], BF16)
            vtb = Ring(A, "vtb", 2, [384], BF16)

            wv = w_in[l].rearrange("(k p) n -> p k n", p=128)
            for kc in range(0, 8, 2):
                S.op("pool", lambda e, kc=kc: e.dma_start(out=Win[:, kc:kc + 2, :], in_=wv[:, kc:kc + 2, :]), writes=["Win"], dma=True)
            S.op("pool", lambda e: e.dma_start(out=wuq, in_=w_uq[l].rearrange("(k p) n -> p k n", p=128)), writes=["wuq"], dma=True)
            S.op("pool", lambda e: e.dma_start(out=wukv, in_=w_ukv[l]), writes=["wukv"], dma=True)
            S.op("dve", lambda e: e.memset(wfbd, 0.0), writes=["wfbd"])
            for g in range(4):
                jc, gl = g // 2, g % 2
                S.op("sp", lambda e, g=g, jc=jc, gl=gl: e.dma_start(out=wfbd[gl * 64:(gl + 1) * 64, jc, gl * 64:(gl + 1) * 64], in_=w_f[l, g]),
                     reads=["wfbd"], writes=["wfbd%d" % g], dma=True)
            for jc in range(2):
                for ri, Cm in enumerate((CCre, CCim)):
                    S.op("pe", lambda e, jc=jc, ri=ri, Cm=Cm: e.matmul(bank(4, 256)[:, ri * 128:(ri + 1) * 128], lhsT=Cm, rhs=wfbd[:, jc, :], start=True, stop=True),
                         reads=["cmf", "wfbd", "wfbd0", "wfbd1", "wfbd2", "wfbd3"], writes=[bname(4)])
                S.op("dve", lambda e, jc=jc: e.tensor_copy(out=Abd[:, jc, :], in_=bank(4, 256)), reads=[bname(4)], writes=["Abd"])

            src_lat = x if l == 0 else h_scr[0:S_LAT]
            src_ctx = ctx if l == 0 else h_scr[S_LAT:T]
            bl = blocks(512, True)
            hloaded = {}

            def load_block(bi):
                t0, W = bl[bi]
                tiles = []
                for t in range(W // 128):
                    ha, hn = hring.next()
                    tok = t0 + t * 128
                    src = src_lat[tok:tok + 128] if tok < S_LAT else src_ctx[tok - S_LAT:tok - S_LAT + 128]
                    S.op("sp", lambda e, ha=ha, src=src: e.dma_start(out=ha, in_=src), reads=["h_scr%d" % (tok // 128)], writes=[hn], dma=True)
                    tiles.append((ha, hn))
                rt, rn = rtab.next()
                S.op("sp", lambda e, rt=rt, t0=t0, W=W: e.dma_start(out=rt[:, 0:2, 0:W], in_=ropeS[:, :, t0:t0 + W].rearrange("a p w -> p a w")), writes=[rn], dma=True)
                S.op("sp", lambda e, rt=rt, t0=t0, W=W: e.dma_start(out=rt[:, 2:4, 0:W], in_=ropeM[:, :, t0:t0 + W].rearrange("a p w -> p a w")), writes=[rn + "m"], dma=True)
                hloaded[bi] = (tiles, rt, rn)

            def rope(psb, rows, W, Ct, St, Pm, outap, outn, tabn):
                r0, r1 = rows
                q_b, q_n = qb.next()
                t1, t1n = t1r.next()
                t2, t2n = t2r.next()
                ps = bank(psb)
                S.op("act", lambda e: e.activation(out=q_b[r0:r1, 0:W], in_=ps[r0:r1, 0:W], func=AF.Copy), reads=[bname(psb)], writes=[q_n])
                S.op("pe", lambda e: e.matmul(bank(4)[r0:r1, 0:W], lhsT=Pm[r0:r1, r0:r1], rhs=q_b[r0:r1, 0:W], start=True, stop=True),
                     reads=[q_n, "cm"], writes=[bname(4)])
                S.op("dve", lambda e: e.tensor_tensor(out=t1[r0:r1, 0:W], in0=ps[r0:r1, 0:W], in1=Ct[r0:r1, 0:W], op=ALU.mult),
                     reads=[bname(psb), tabn], writes=[t1n])
                S.op("dve", lambda e: e.tensor_tensor(out=t2[r0:r1, 0:W], in0=bank(4)[r0:r1, 0:W], in1=St[r0:r1, 0:W], op=ALU.mult),
                     reads=[bname(4), tabn], writes=[t2n])
                S.op("dve", lambda e: e.tensor_tensor(out=outap[r0:r1, 0:W], in0=t1[r0:r1, 0:W], in1=t2[r0:r1, 0:W], op=ALU.add),
                     reads=[t1n, t2n], writes=[outn])

            def rms_bc(psbanks, W, nfeat):
                n = len(psbanks)
                for jj, b in enumerate(psbanks):
                    S.op("act", lambda e, jj=jj, b=b: e.activation(out=sq[:, jj, 0:W], in_=bank(b)[:, 0:W], func=AF.Square), reads=[bname(b)], writes=["sq%d" % jj])
                for jj in range(n):
                    S.op("pe", lambda e, jj=jj: e.matmul(bank(4)[:, 0:W], lhsT=ones, rhs=sq[:, jj, 0:W], start=(jj == 0), stop=(jj == n - 1)),
                         reads=["sq%d" % jj, "cm"], writes=[bname(4)])
                S.op("act", lambda e: e.activation(out=lnv[:, 0:W], in_=bank(4)[:, 0:W], func=AF.Ln, scale=1.0 / nfeat, bias=epst[:, 0:1]),
                     reads=[bname(4), "eps"], writes=["lnv"])
                S.op("act", lambda e: e.activation(out=rbc[:, 0:W], in_=lnv[:, 0:W], func=AF.Exp, scale=-0.5), reads=["lnv"], writes=["rbc"])

            ubanks = [2, 3, 5]
            ucnt = [0]

            def inproj(xm, xmn, col0, M, W, b=None):
                if b is None:
                    b = ubanks[ucnt[0] % 3]
                    ucnt[0] += 1
                for kc in range(8):
                    S.op("pe", lambda e, kc=kc: e.matmul(bank(b)[0:M, 0:W], lhsT=Win[:, kc, col0:col0 + M], rhs=xm[:, kc, 0:W], start=(kc == 0), stop=(kc == 7)),
                         reads=["Win", xmn], writes=[bname(b)])
                return b

            tokb = [6, 7]
            tokc = [0]

            def tokbank():
                b = tokb[tokc[0] % 2]
                tokc[0] += 1
                return b

            xms = {}

            def do_norm(bi):
                t0, W = bl[bi]
                tiles, rt, rn = hloaded[bi]
                xm, xmn = xmring.next()
                norm_mod_T(tiles, W // 128, 0, 0 if t0 < S_LAT else 1, xm, xmn, ssb, xhring, [0, 1])
                xms[bi] = (xm, xmn)

            def make_tasks(bi):
                t0, W = bl[bi]
                tiles, rt, rn = hloaded.pop(bi)
                nt = W // 128
                xm, xmn = xms.pop(bi)
                CS, SS, CM, SM = rt[:, 0, :], rt[:, 1, :], rt[:, 2, :], rt[:, 3, :]
                tasks = []
                bk = {}

                def mk_first(key, cols):
                    def first():
                        bk[key] = [inproj(xm, xmn, c0, M, W) for (c0, M) in cols]
                    return first

                def rest_f(jc):
                    def rest():
                        b = bk[("f", jc)][0]
                        S.op("act", lambda e: e.activation(out=fT[:, jc, 0:W], in_=bank(b)[:, 0:W], func=AF.Copy), reads=[bname(b)], writes=["fT%d" % jc])
                        if jc == 1:
                            for t in range(nt):
                                tb = tokbank()
                                for jj in range(2):
                                    for ri in range(2):
                                        o = bank(tb)[:, ri * 256 + jj * 128: ri * 256 + (jj + 1) * 128]
                                        S.op("pe", lambda e, o=o, jj=jj, t=t, ri=ri: e.matmul(o, lhsT=fT[:, jj, t * 128:(t + 1) * 128], rhs=Abd[:, jj, ri * 128:(ri + 1) * 128],
                                                                                           start=True, stop=True), reads=["fT%d" % jj, "Abd"], writes=[bname(tb)])
                                yb, yn = ysb.next()
                                S.op("act", lambda e, yb=yb, tb=tb: e.activation(out=yb, in_=bank(tb), func=AF.Copy), reads=[bname(tb)], writes=[yn])
                                S.op("sp", lambda e, yb=yb, t=t: e.dma_start(out=y_scr[t0 + t * 128:t0 + (t + 1) * 128, :], in_=yb), reads=[yn], writes=["y_scr"], dma=True)
                    return rest
                for jc in range(2):
                    tasks.append((mk_first(("f", jc), [(C_F + jc * 128, 128)]), rest_f(jc)))

                def rest_q(c):
                    def rest():
                        b = bk[("q", c)][0]
                        oa, on = qo.next()
                        rope(b, (0, 128), W, CS, SS, PmS, oa, on, rn)
                        dst = qs_scr[c, :, t0:t0 + W] if c < 3 else ks_scr[:, t0:t0 + W]
                        S.op("sp", lambda e: e.dma_start(out=dst, in_=oa[:, 0:W]), reads=[on], writes=["qks_scr"], dma=True)
                    return rest
                for c in range(4):
                    tasks.append((mk_first(("q", c), [(C_Q + c * 128, 128)]), rest_q(c)))
                    if c == 1 and bi + 1 < len(bl):
                        tasks.append((lambda: None, lambda: do_norm(bi + 1)))

                def rest_cq():
                    b0, b1 = bk["cq"]
                    rms_bc([b0, b1], W, 256)
                    for jj, b in enumerate((b0, b1)):
                        S.op("dve", lambda e, jj=jj, b=b: e.scalar_tensor_tensor(out=cqn[:, jj, 0:W], in0=bank(b)[:, 0:W], scalar=gq[:, jj:jj + 1], in1=rbc[:, 0:W],
                                                                                 op0=ALU.mult, op1=ALU.mult), reads=[bname(b), "gq", "rbc"], writes=["cqn"])
                    for hd in range(6):
                        ub_ = tokbank()
                        for jj in range(2):
                            S.op("pe", lambda e, hd=hd, jj=jj, ub_=ub_: e.matmul(bank(ub_)[0:96, 0:W], lhsT=wuq[:, jj, hd * 96:(hd + 1) * 96], rhs=cqn[:, jj, 0:W], start=(jj == 0), stop=(jj == 1)),
                                 reads=["wuq", "cqn"], writes=[bname(ub_)])
                        oa, on = qo.next()
                        S.op("act", lambda e, oa=oa, ub_=ub_: e.activation(out=oa[0:64, 0:W], in_=bank(ub_)[0:64, 0:W], func=AF.Copy), reads=[bname(ub_)], writes=[on])
                        rope(ub_, (64, 96), W, CM, SM, PmM, oa, on, rn + "m")
                        S.op("sp", lambda e, oa=oa, hd=hd: e.dma_start(out=qm_scr[hd, :, t0:t0 + W], in_=oa[0:96, 0:W]), reads=[on], writes=["qm_scr"], dma=True)
                tasks.append((mk_first("cq", [(C_CQ, 128), (C_CQ + 128, 128)]), rest_cq))

                def rest_ckv():
                    b0 = bk["ckv"][0]
                    rms_bc([b0], W, 128)
                    S.op("dve", lambda e: e.scalar_tensor_tensor(out=ckvn[:, 0:W], in0=bank(b0)[:, 0:W], scalar=gkv[:, 0:1], in1=rbc[:, 0:W], op0=ALU.mult, op1=ALU.mult),
                         reads=[bname(b0), "gkv", "rbc"], writes=["ckvn"])
                    for pr in range(3):
                        ub_ = tokbank()
                        S.op("pe", lambda e, pr=pr, ub_=ub_: e.matmul(bank(ub_)[:, 0:W], lhsT=wukv[:, pr * 128:(pr + 1) * 128], rhs=ckvn[:, 0:W], start=True, stop=True),
                             reads=["wukv", "ckvn"], writes=[bname(ub_)])
                        ka, kn_ = knb.next()
                        S.op("act", lambda e, ka=ka, ub_=ub_: e.activation(out=ka[:, 0:W], in_=bank(ub_)[:, 0:W], func=AF.Copy), reads=[bname(ub_)], writes=[kn_])
                        S.op("sp", lambda e, ka=ka, pr=pr: e.dma_start(out=kn_scr[pr * 128:(pr + 1) * 128, t0:t0 + W], in_=ka[:, 0:W]), reads=[kn_], writes=["kn_scr"], dma=True)
                    for t in range(nt):
                        tb = tokbank()
                        S.op("pe", lambda e, tb=tb, t=t: e.matmul(bank(tb)[:, 0:384], lhsT=ckvn[:, t * 128:(t + 1) * 128], rhs=wukv[:, 384:768], start=True, stop=True),
                             reads=["wukv", "ckvn"], writes=[bname(tb)])
                        va, vn = vtb.next()
                        S.op("dve", lambda e, va=va, tb=tb: e.tensor_copy(out=va, in_=bank(tb)[:, 0:384]), reads=[bname(tb)], writes=[vn])
                        S.op("sp", lambda e, va=va, t=t: e.dma_start(out=vm_scr[t0 + t * 128:t0 + (t + 1) * 128, :], in_=va), reads=[vn], writes=["vm_scr"], dma=True)
                tasks.append((mk_first("ckv", [(C_CKV, 128)]), rest_ckv))

                def rest_kr():
                    b = bk["kr"][0]
                    oa, on = qo.next()
                    rope(b, (0, 32), W, CM, SM, PmM, oa, on, rn + "m")
                    S.op("sp", lambda e: e.dma_start(out=kr_scr[:, t0:t0 + W], in_=oa[0:32, 0:W]), reads=[on], writes=["kr_scr"], dma=True)
                tasks.append((mk_first("kr", [(C_KR, 32)]), rest_kr))

                def rest_v():
                    for t in range(nt):
                        tb = tokbank()
                        for kc in range(8):
                            S.op("pe", lambda e, tb=tb, t=t, kc=kc: e.matmul(bank(tb)[:, 0:128], lhsT=xm[:, kc, t * 128:(t + 1) * 128], rhs=Win[:, kc, C_V:C_V + 128],
                                                                            start=(kc == 0), stop=(kc == 7)), reads=["Win", xmn], writes=[bname(tb)])
                        va, vn = vtb.next()
                        S.op("dve", lambda e, va=va, tb=tb: e.tensor_copy(out=va[:, 0:128], in_=bank(tb)[:, 0:128]), reads=[bname(tb)], writes=[vn])
                        S.op("sp", lambda e, va=va, t=t: e.dma_start(out=vs_scr[t0 + t * 128:t0 + (t + 1) * 128, :], in_=va[:, 0:128]), reads=[vn], writes=["vs_scr"], dma=True)
                tasks.append((lambda: None, rest_v))
                return tasks

            load_block(0)
            do_norm(0)
            for bi in range(len(bl)):
                if bi + 1 < len(bl):
                    load_block(bi + 1)
                run_pipeline(make_tasks(bi), 1)

        def phase_F(l):
            S.barrier()
            A.off = PERSIST
            last = l == DEPTH - 1
            M2 = A.alloc([64, 64], BF16)
            Yin = A.alloc([64, 256], BF16)
            Zs = A.alloc([64, 256], BF16)
            fouT = A.alloc([2, S_LAT], BF16)
            S.op("pool", lambda e: e.dma_start(out=M2, in_=m2d), writes=["M2"], dma=True)
            yv = y_scr[0:S_LAT].rearrange("(r c) (ri ch) -> ri r c ch", c=64, ri=2)
            for ri in range(2):
                for cq4 in range(4):
                    S.op("sp", lambda e, ri=ri, cq4=cq4: e.dma_start(out=Yin[ri * 64:(ri + 1) * 64, cq4 * 16:(cq4 + 1) * 16, :], in_=yv[ri, :, cq4 * 16:(cq4 + 1) * 16, :]),
                         writes=["Yin%d" % cq4], dma=True)
            Yf = Yin.rearrange("p c ch -> p (c ch)")
            Zf = Zs.rearrange("p c ch -> p (c ch)")
            for i in range(32):
                b = i % 4
                S.op("pe", lambda e, i=i, b=b: e.matmul(bank(b), lhsT=D1, rhs=Yf[:, i * 512:(i + 1) * 512], start=True, stop=True),
                     reads=["Yin%d" % (i // 8), "cm"], writes=[bname(b)])
                if i % 2 == 0:
                    S.op("act", lambda e, i=i, b=b: e.activation(out=Zf[:, i * 512:(i + 1) * 512], in_=bank(b), func=AF.Copy), reads=[bname(b)], writes=["Zs%d" % (i // 8)])
                else:
                    S.op("dve", lambda e, i=i, b=b: e.tensor_copy(out=Zf[:, i * 512:(i + 1) * 512], in_=bank(b)), reads=[bname(b)], writes=["Zs%d" % (i // 8)])
                if i % 8 == 7:
                    q4 = i // 8
                    S.op("sp", lambda e, q4=q4: e.dma_start(out=z_scr[:, q4 * 16:(q4 + 1) * 16, :], in_=Zs[:, q4 * 16:(q4 + 1) * 16, :]),
                         reads=["Zs%d" % q4], writes=["z_scr"], dma=True)
            Zin = Yin
            zv = z_scr.rearrange("(ri ka) c ch -> ri c ka ch", ri=2)
            for ri in range(2):
                for k4 in range(4):
                    S.op("sp", lambda e, ri=ri, k4=k4: e.dma_start(out=Zin[ri * 64:(ri + 1) * 64, k4 * 16:(k4 + 1) * 16, :], in_=zv[ri, :, k4 * 16:(k4 + 1) * 16, :]),
                         reads=["z_scr"], writes=["Zin%d" % k4] + ["Yin%d" % q for q in range(4)], dma=True)
            n = 0
            for ch in range(2):
                fv = fouT[:, ch, :].rearrange("p (kb ka) -> p ka kb", ka=64)
                for kag in range(8):
                    b = n % 4
                    n += 1
                    for a in range(8):
                        ka = kag * 8 + a
                        S.op("pe", lambda e, b=b, a=a, ka=ka, ch=ch: e.matmul(bank(b)[:, a * 64:(a + 1) * 64], lhsT=Zin[:, ka, ch * 128:(ch + 1) * 128], rhs=M2[:, ka, :],
                                                                             start=True, stop=True), reads=["Zin%d" % (ka // 16), "M2"] + ["Yin%d" % q for q in range(4)], writes=[bname(b)])
                    src = bank(b).rearrange("p (a kb) -> p a kb", a=8)
                    dst = fv[:, kag * 8:(kag + 1) * 8, :]
                    if kag % 2 == 0:
                        S.op("act", lambda e, src=src, dst=dst: e.activation(out=dst, in_=src, func=AF.Copy), reads=[bname(b)], writes=["fouT"])
                    else:
                        S.op("dve", lambda e, src=src, dst=dst: e.tensor_copy(out=dst, in_=src), reads=[bname(b)], writes=["fouT"])
            for ch in range(2):
                S.op("sp", lambda e, ch=ch: e.dma_start(out=mixT_scr[ch * 128:(ch + 1) * 128, 0:S_LAT], in_=fouT[:, ch, :]), reads=["fouT"], writes=["mixT_scr"], dma=True)
            if not last:
                D256 = A.alloc([2, 2, 256], BF16)
                yc = A.alloc([2, 512], BF16)
                fc = A.alloc([2, 256], BF16)
                S.op("pool", lambda e: e.dma_start(out=D256, in_=d256d), writes=["D256"], dma=True)
                S.op("sp", lambda e: e.dma_start(out=yc, in_=y_scr[S_LAT:T].rearrange("(t p) n -> p t n", p=128)), reads=["y_scr"], writes=["yc"], dma=True)
                for ch in range(2):
                    b = 4 + ch
                    k = 0
                    for t in range(2):
                        for ri in range(2):
                            S.op("pe", lambda e, b=b, t=t, ri=ri, ch=ch, k=k: e.matmul(bank(b)[:, 0:256], lhsT=yc[:, t, ri * 256 + ch * 128: ri * 256 + (ch + 1) * 128],
                                                                                    rhs=D256[:, t, ri, :], start=(k == 0), stop=(k == 3)), reads=["yc", "D256"], writes=[bname(b)])
                            k += 1
                    S.op("act", lambda e, b=b, ch=ch: e.activation(out=fc[:, ch, :], in_=bank(b)[:, 0:256], func=AF.Copy), reads=[bname(b)], writes=["fc"])
                    S.op("sp", lambda e, ch=ch: e.dma_start(out=mixT_scr[ch * 128:(ch + 1) * 128, S_LAT:T], in_=fc[:, ch, :]), reads=["fc"], writes=["mixT_scr"], dma=True)

        def run_pipeline(tasks, look):
            n = len(tasks)
            for i in range(min(look, n)):
                tasks[i][0]()
            for i in range(n):
                if i + look < n:
                    tasks[i + look][0]()
                tasks[i][1]()

        def attn_finish(ob, W, exp_ap, row0, t0, dnr, rcr, ob_ring):
            o = bank(ob)
            dn, dnn = dnr.next()
            rc, rcn = rcr.next()
            if exp_ap is not None:
                S.op("dve", lambda e: e.tensor_scalar(out=dn[64:128, 0:W], in0=o[64:128, 0:W], scalar1=exp_ap, scalar2=None, op0=ALU.add),
                     reads=[bname(ob), "exps"], writes=[dnn])
                S.op("dve", lambda e: e.reciprocal(out=rc[64:128, 0:W], in_=dn[64:128, 0:W]), reads=[dnn], writes=[rcn])
            else:
                S.op("dve", lambda e: e.reciprocal(out=rc[64:128, 0:W], in_=o[64:128, 0:W]), reads=[bname(ob)], writes=[rcn])
            oa, on = ob_ring.next()
            S.op("dve", lambda e: e.tensor_tensor(out=oa[0:64, 0:W], in0=o[0:64, 0:W], in1=rc[64:128, 0:W], op=ALU.mult), reads=[bname(ob), rcn], writes=[on])
            S.op("sp", lambda e: e.dma_start(out=mixT_scr[row0:row0 + 64, t0:t0 + W], in_=oa[0:64, 0:W]), reads=[on], writes=["mixT_scr"], dma=True)

        def phase_B1(l):
            S.barrier()
            A.off = PERSIST
            last = l == DEPTH - 1
            Ks = A.alloc([T], BF16)
            Vs = A.alloc([34, 2, 128], BF16)
            masks = A.alloc([6, 512], BF16)
            Qring = Ring(A, "Qs", 3, [T], BF16)
            pT = Ring(A, "pT", 4, [512], BF16)
            dnr = Ring(A, "dn", 2, [512], F32)
            rcr = Ring(A, "rc", 2, [512], F32)
            obr = Ring(A, "ob", 2, [512], BF16)
            S.op("pool", lambda e: e.dma_start(out=masks, in_=masksd), writes=["masks"], dma=True)
            S.op("sp", lambda e: e.dma_start(out=Ks, in_=ks_scr), writes=["Ks"], dma=True)
            S.op("dve", lambda e: e.memset(Vs, 1.0), writes=["Vs"])
            vv = vs_scr.rearrange("(blk p) (kvh d) -> p blk kvh d", p=128, kvh=2)
            vnames = []
            for g0 in range(0, 34, 12):
                g1 = min(34, g0 + 12)
                for kvh in range(2):
                    S.op("sp", lambda e, g0=g0, g1=g1, kvh=kvh: e.dma_start(out=Vs[:, g0:g1, kvh, 0:64], in_=vv[:, g0:g1, kvh, :]), reads=["Vs"], writes=["Vs%d_%d" % (g0, kvh)], dma=True)
                    vnames.append("Vs%d_%d" % (g0, kvh))
            bl = blocks(512, not last)
            Qs = []
            for c in range(3):
                Qa, Qn = Qring.next()
                S.op("sp", lambda e, Qa=Qa, c=c: e.dma_start(out=Qa, in_=qs_scr[c]), writes=[Qn], dma=True)
                Qs.append((Qa, Qn))
            tasks = []
            sb_i = 0
            ob_i = 0
            for c in range(3):
                Qa, Qn = Qs[c]
                for hh in range(2):
                    head = c + 3 * hh
                    r0 = hh * 64
                    for (t0, W) in bl:
                        if t0 < S_LAT:
                            qb_ = t0 // 512
                            kbs = [(kb, kb - 4 * qb_ + 1) for kb in range(4 * qb_ - 1, 4 * qb_ + 5) if 0 <= kb < 32] + [(32, None), (33, None)]
                        else:
                            kbs = [(32, None), (33, None)]
                        ob = 4 + ob_i % 2
                        ob_i += 1
                        for i, (kb, mi) in enumerate(kbs):
                            sbk = sb_i % 4
                            sb_i += 1
                            pa, pn = pT.next()

                            def first(sbk=sbk, kb=kb, Qa=Qa, Qn=Qn, r0=r0, t0=t0, W=W):
                                S.op("pe", lambda e: e.matmul(bank(sbk)[:, 0:W], lhsT=Ks[r0:r0 + 64, kb * 128:(kb + 1) * 128], rhs=Qa[r0:r0 + 64, t0:t0 + W],
                                                              start=True, stop=True), reads=["Ks", Qn], writes=[bname(sbk)])

                            def rest(sbk=sbk, kb=kb, mi=mi, pa=pa, pn=pn, ob=ob, i=i, nk=len(kbs), hh=hh, head=head, t0=t0, W=W):
                                S.op("act", lambda e: e.activation(out=pa[:, 0:W], in_=bank(sbk)[:, 0:W], func=AF.Exp, scale=SWA_SCALE), reads=[bname(sbk)], writes=[pn])
                                if mi is not None:
                                    S.op("dve", lambda e: e.tensor_tensor(out=pa[:, 0:W], in0=pa[:, 0:W], in1=masks[:, mi, 0:W], op=ALU.mult),
                                         reads=[pn, "masks"], writes=[pn])
                                S.op("pe", lambda e: e.matmul(bank(ob)[:, 0:W], lhsT=Vs[:, kb, hh, :], rhs=pa[:, 0:W], start=(i == 0), stop=(i == nk - 1)),
                                     reads=[pn, "Vs"] + vnames, writes=[bname(ob)])
                                if i == nk - 1:
                                    attn_finish(ob, W, exps[64:128, head:head + 1], 256 + head * 64, t0, dnr, rcr, obr)

                            tasks.append((first, rest))
            run_pipeline(tasks, 3)

        def phase_B2(l):
            S.barrier()
            A.off = PERSIST
            last = l == DEPTH - 1
            Qr = Ring(A, "Qm", 2, [T], BF16)
            Kr = Ring(A, "Km", 2, [T], BF16)
            Vr = Ring(A, "Vm", 2, [34, 128], BF16)
            pT = Ring(A, "pT2", 3, [2, 512], BF16)
            dnr = Ring(A, "dn2", 2, [512], F32)
            rcr = Ring(A, "rc2", 2, [512], F32)
            obr = Ring(A, "ob2", 2, [512], BF16)
            for va, vn in zip(Vr.bufs, Vr.names):
                S.op("dve", lambda e, va=va: e.memset(va, 1.0), writes=[vn])
            vv = vm_scr.rearrange("(blk p) (h d) -> p blk h d", p=128, h=6)
            bl = blocks(512, not last)
            hl = {}

            def load_head(hd):
                Qa, Qn = Qr.next()
                Ka, Kn = Kr.next()
                Va, Vn = Vr.next()
                S.op("sp", lambda e, Qa=Qa, hd=hd: e.dma_start(out=Qa[0:96, :], in_=qm_scr[hd]), writes=[Qn], dma=True)
                S.op("sp", lambda e, Ka=Ka, hd=hd: e.dma_start(out=Ka[0:64, :], in_=kn_scr[hd * 64:(hd + 1) * 64, :]), writes=[Kn], dma=True)
                S.op("sp", lambda e, Ka=Ka: e.dma_start(out=Ka[64:96, :], in_=kr_scr), writes=[Kn + "r"], dma=True)
                vparts = []
                for g0 in range(0, 34, 9):
                    g1 = min(34, g0 + 9)
                    S.op("sp", lambda e, Va=Va, g0=g0, g1=g1, hd=hd: e.dma_start(out=Va[:, g0:g1, 0:64], in_=vv[:, g0:g1, hd, :]), reads=[Vn], writes=[Vn + "_%d" % g0], dma=True)
                    vparts.append(Vn + "_%d" % g0)
                hl[hd] = (Qa, Qn, Ka, Kn, Va, Vn, vparts)

            tasks = []
            sp_i = 0
            ob_i = 0
            for hd in range(6):
                for bi, (t0, W) in enumerate(bl):
                    kbs = list(range(34)) if t0 < S_LAT else [32, 33]
                    ob = 4 + ob_i % 2
                    ob_i += 1
                    npairs = len(kbs) // 2
                    for pi in range(npairs):
                        sp_ = [0, 2, 6][sp_i % 3]
                        sp_i += 1
                        pa, pn = pT.next()
                        need_load = (bi == 0 and pi == 0)
                        prefetch = (bi == 1 and pi == 0)

                        def first(hd=hd, sp_=sp_, kb0=kbs[2 * pi], kb1=kbs[2 * pi + 1], t0=t0, W=W, need_load=need_load, prefetch=prefetch):
                            if need_load and hd not in hl:
                                load_head(hd)
                            if prefetch and hd + 1 < 6:
                                load_head(hd + 1)
                            Qa, Qn, Ka, Kn, Va, Vn, vparts = hl[hd]
                            for u, kb in enumerate((kb0, kb1)):
                                S.op("pe", lambda e, u=u, kb=kb: e.matmul(bank(sp_ + u)[:, 0:W], lhsT=Ka[0:96, kb * 128:(kb + 1) * 128], rhs=Qa[0:96, t0:t0 + W],
                                                                        start=True, stop=True), reads=[Kn, Kn + "r", Qn], writes=[bname(sp_ + u)])

                        def rest(hd=hd, sp_=sp_, kb0=kbs[2 * pi], kb1=kbs[2 * pi + 1], t0=t0, W=W, pa=pa, pn=pn, ob=ob, pi=pi, npairs=npairs):
                            Qa, Qn, Ka, Kn, Va, Vn, vparts = hl[hd]
                            src = pall[:, sp_ * 512:(sp_ + 2) * 512].rearrange("p (b w) -> p b w", b=2)[:, :, 0:W]
                            S.op("act", lambda e: e.activation(out=pa[:, :, 0:W], in_=src, func=AF.Exp, scale=MLA_SCALE),
                                 reads=[bname(sp_), bname(sp_ + 1)], writes=[pn])
                            for u, kb in enumerate((kb0, kb1)):
                                first_ = (pi == 0 and u == 0)
                                last_ = (pi == npairs - 1 and u == 1)
                                S.op("pe", lambda e, u=u, kb=kb, first_=first_, last_=last_: e.matmul(bank(ob)[:, 0:W], lhsT=Va[:, kb, :], rhs=pa[:, u, 0:W], start=first_, stop=last_),
                                     reads=[pn, Vn] + vparts, writes=[bname(ob)])
                            if pi == npairs - 1:
                                attn_finish(ob, W, None, 640 + hd * 64, t0, dnr, rcr, obr)

                        tasks.append((first, rest))
            run_pipeline(tasks, 2)

        def phase_C1(l):
            S.barrier()
            A.off = PERSIST
            last = l == DEPTH - 1
            Wo = A.alloc([8, D], BF16)
            mring = Ring(A, "mixT", 2, [8, 512], BF16)
            hring = Ring(A, "hC1", 8, [D], F32)
            tmpr = Ring(A, "tmpC1", 2, [512], F32)
            load_gate(l, 2)
            wv = w_out[l].rearrange("(k p) n -> p k n", p=128)
            for kc in range(0, 8, 2):
                S.op("pool", lambda e, kc=kc: e.dma_start(out=Wo[:, kc:kc + 2, :], in_=wv[:, kc:kc + 2, :]), writes=["Wo"], dma=True)
            src_lat = x if l == 0 else h_scr[0:S_LAT]
            src_ctx = ctx if l == 0 else h_scr[S_LAT:T]
            bl = blocks(512, not last)
            mv = mixT_scr.rearrange("(k p) t -> p k t", p=128)
            loaded = {}

            def load_block(bi):
                t0, W = bl[bi]
                ma, mn = mring.next()
                S.op("sp", lambda e, ma=ma: e.dma_start(out=ma[:, :, 0:W], in_=mv[:, :, t0:t0 + W]), reads=["mixT_scr"], writes=[mn], dma=True)
                tiles = []
                for t in range(W // 128):
                    ha, hn = hring.next()
                    tok = t0 + t * 128
                    src = src_lat[tok:tok + 128] if tok < S_LAT else src_ctx[tok - S_LAT:tok - S_LAT + 128]
                    S.op("sp", lambda e, ha=ha, src=src: e.dma_start(out=ha, in_=src), reads=["h_scr%d" % (tok // 128)], writes=[hn], dma=True)
                    tiles.append((ha, hn))
                loaded[bi] = (ma, mn, tiles)

            load_block(0)
            bi_ = [0]
            for bi, (t0, W) in enumerate(bl):
                if bi + 1 < len(bl):
                    load_block(bi + 1)
                ma, mn, tiles = loaded.pop(bi)
                j = 0 if t0 < S_LAT else 1
                for t in range(W // 128):
                    ha, hn = tiles[t]
                    for n in range(2):
                        b = bi_[0] % 4
                        bi_[0] += 1
                        for kc in range(8):
                            S.op("pe", lambda e, b=b, kc=kc, t=t, n=n, ma=ma: e.matmul(bank(b), lhsT=ma[:, kc, t * 128:(t + 1) * 128], rhs=Wo[:, kc, n * 512:(n + 1) * 512],
                                                                                    start=(kc == 0), stop=(kc == 7)), reads=[mn, "Wo"], writes=[bname(b)])
                        ta, tn = tmpr.next()
                        S.op("dve", lambda e, b=b, ta=ta, n=n: e.tensor_tensor(out=ta, in0=bank(b), in1=gbc[:, j, n * 512:(n + 1) * 512], op=ALU.mult),
                             reads=[bname(b), "gbc"], writes=[tn])
                        S.op("dve", lambda e, ta=ta, ha=ha, n=n: e.tensor_tensor(out=ha[:, n * 512:(n + 1) * 512], in0=ha[:, n * 512:(n + 1) * 512], in1=ta, op=ALU.add),
                             reads=[tn, hn], writes=[hn])
                    tok = t0 + t * 128
                    S.op("sp", lambda e, ha=ha, tok=tok: e.dma_start(out=h_scr[tok:tok + 128, :], in_=ha), reads=[hn], writes=["h_scr%d" % (tok // 128)], dma=True)

        def phase_C2(l):
            S.barrier()
            A.off = PERSIST
            last = l == DEPTH - 1
            W1 = A.alloc([8, DFF], BF16)
            W2 = A.alloc([32, D], BF16)
            hid = A.alloc([32, 256], BF16)
            hring = Ring(A, "hC2", 4, [D], F32)
            xhring = Ring(A, "xhC2", 3, [D], BF16)
            xmring = Ring(A, "xmC2", 2, [8, 256], BF16)
            ssb = A.alloc([12], F32)
            rr = Ring(A, "rr", 2, [256], F32)
            tmpr = Ring(A, "tmpC2", 2, [512], F32)
            fss = A.alloc([4], F32)
            load_gate(l, 5)
            wv1 = w_mlp1[l].rearrange("(k p) n -> p k n", p=128)
            wv2 = w_mlp2[l].rearrange("(k p) n -> p k n", p=128)
            for kc in range(8):
                S.op("pool", lambda e, kc=kc: e.dma_start(out=W1[:, kc, :], in_=wv1[:, kc, :]), writes=["W1_%d" % kc], dma=True)
            for kc in range(0, 32, 4):
                S.op("pool", lambda e, kc=kc: e.dma_start(out=W2[:, kc:kc + 4, :], in_=wv2[:, kc:kc + 4, :]), writes=["W2_%d" % (kc // 4)], dma=True)
            w1n = ["W1_%d" % k for k in range(8)]
            w2n = ["W2_%d" % k for k in range(8)]
            bl = blocks(256, not last)
            loaded = {}

            def load_block(bi):
                t0, W = bl[bi]
                tiles = []
                for t in range(W // 128):
                    ha, hn = hring.next()
                    tok = t0 + t * 128
                    S.op("sp", lambda e, ha=ha, tok=tok: e.dma_start(out=ha, in_=h_scr[tok:tok + 128, :]), reads=["h_scr%d" % (tok // 128)], writes=[hn], dma=True)
                    tiles.append((ha, hn))
                loaded[bi] = tiles

            load_block(0)
            ub = [0]
            ob = [0]
            xms = {}

            def do_norm(bi):
                t0, W = bl[bi]
                xm, xmn = xmring.next()
                norm_mod_T(loaded[bi], W // 128, 1, 0 if t0 < S_LAT else 1, xm, xmn, ssb, xhring, [0, 1])
                xms[bi] = (xm, xmn)

            do_norm(0)
            for bi, (t0, W) in enumerate(bl):
                if bi + 1 < len(bl):
                    load_block(bi + 1)
                tiles = loaded[bi]
                nt = W // 128
                j = 0 if t0 < S_LAT else 1
                xm, xmn = xms.pop(bi)
                for c in range(32):
                    b = 2 + ub[0] % 4
                    ub[0] += 1
                    for kc in range(8):
                        S.op("pe", lambda e, b=b, c=c, kc=kc: e.matmul(bank(b)[:, 0:W], lhsT=W1[:, kc, c * 128:(c + 1) * 128], rhs=xm[:, kc, 0:W], start=(kc == 0), stop=(kc == 7)),
                             reads=[xmn, w1n[kc]], writes=[bname(b)])
                    ra, rn_ = rr.next()
                    S.op("act", lambda e, b=b, ra=ra: e.activation(out=ra[:, 0:W], in_=bank(b)[:, 0:W], func=AF.Relu), reads=[bname(b)], writes=[rn_])
                    S.op("dve", lambda e, ra=ra, c=c: e.tensor_tensor(out=hid[:, c, 0:W], in0=ra[:, 0:W], in1=ra[:, 0:W], op=ALU.mult), reads=[rn_], writes=["hid%d" % c])
                hidn = ["hid%d" % c for c in range(32)]
                if bi + 1 < len(bl):
                    do_norm(bi + 1)
                loaded.pop(bi)
                for t in range(nt):
                    ha, hn = tiles[t]
                    for n in range(2):
                        b = 6 + ob[0] % 2
                        ob[0] += 1
                        for kc in range(32):
                            S.op("pe", lambda e, b=b, kc=kc, t=t, n=n: e.matmul(bank(b), lhsT=hid[:, kc, t * 128:(t + 1) * 128], rhs=W2[:, kc, n * 512:(n + 1) * 512],
                                                                             start=(kc == 0), stop=(kc == 31)), reads=[hidn[kc], w2n[kc // 4]], writes=[bname(b)])
                        ta, tn = tmpr.next()
                        S.op("dve", lambda e, b=b, ta=ta, n=n: e.tensor_tensor(out=ta, in0=bank(b), in1=gbc[:, j, n * 512:(n + 1) * 512], op=ALU.mult),
                             reads=[bname(b), "gbc"], writes=[tn])
                        S.op("dve", lambda e, ta=ta, ha=ha, n=n: e.tensor_tensor(out=ha[:, n * 512:(n + 1) * 512], in0=ha[:, n * 512:(n + 1) * 512], in1=ta, op=ALU.add),
                             reads=[tn, hn], writes=[hn])
                    tok = t0 + t * 128
                    if not last:
                        S.op("sp", lambda e, ha=ha, tok=tok: e.dma_start(out=h_scr[tok:tok + 128, :], in_=ha), reads=[hn], writes=["h_scr%d" % (tok // 128)], dma=True)
                    else:
                        oa, on = ha, hn
                        S.op("act", lambda e, ha=ha: e.activation(out=xhring.bufs[-1], in_=ha, func=AF.Square, accum_out=fss[:, 0:1]), reads=[hn], writes=["junk", "fss"])
                        S.op("act", lambda e: e.activation(out=fss[:, 1:2], in_=fss[:, 0:1], func=AF.Ln, scale=1.0 / D, bias=epst[:, 0:1]), reads=["fss", "eps"], writes=["fss"])
                        S.op("act", lambda e: e.activation(out=fss[:, 2:3], in_=fss[:, 1:2], func=AF.Exp, scale=-0.5), reads=["fss"], writes=["fss"])
                        S.op("dve", lambda e, ha=ha, oa=oa: e.scalar_tensor_tensor(out=oa, in0=ha, scalar=fss[:, 2:3], in1=fgbc, op0=ALU.mult, op1=ALU.mult),
                             reads=[hn, "fss", "fgbc"], writes=[hn])
                        S.op("sp", lambda e, oa=oa, tok=tok: e.dma_start(out=out[tok:tok + 128, :], in_=oa), reads=[on], dma=True, final=True)

        stop_after = [d for d in debug if d.startswith("stop:")]
        stop_after = stop_after[0][5:] if stop_after else None
        done = False
        for l in range(DEPTH):
            for nm, ph in (("mod", phase_mod), ("A", phase_A), ("F", phase_F), ("B1", phase_B1), ("B2", phase_B2), ("C1", phase_C1), ("C2", phase_C2)):
                ph(l)
                if stop_after == "%s%d" % (nm, l):
                    done = True
                    break
            if done:
                break
        if done:
            S.barrier()
            S.op("sp", lambda e: e.dma_start(out=out[0:128, :], in_=fgbc), dma=True, final=True)
        S.emit()
    return nc


def _host_inputs(inputs):
    f = lambda a: np.ascontiguousarray(np.asarray(a, dtype=np.float32))
    x = f(inputs["x"])
    c = f(inputs["c"])
    ctx = f(inputs["ctx"])
    c_ctx = f(inputs["c_ctx"])
    shared = {}
    shared["w_ada"] = f(inputs["w_ada"])
    shared["b_adaT"] = f(f(inputs["b_ada"]).reshape(DEPTH, 48, 128).transpose(0, 2, 1))
    shared["n1T"] = f(f(inputs["norm1_g"]).reshape(DEPTH, 8, 128).transpose(0, 2, 1))
    shared["n2T"] = f(f(inputs["norm2_g"]).reshape(DEPTH, 8, 128).transpose(0, 2, 1))
    shared["fg"] = f(inputs["final_norm_g"]).reshape(1, D)
    shared["w_in"] = f(f(inputs["w_in"])[:, :, _perm_in()])
    shared["w_f"] = f(inputs["w_fourier"])
    shared["sink"] = f(inputs["swa_sink"])
    shared["gqT"] = f(f(inputs["mla_q_norm"]).reshape(DEPTH, 2, 128).transpose(0, 2, 1))
    shared["gkvT"] = f(f(inputs["mla_kv_norm"]).reshape(DEPTH, 1, 128).transpose(0, 2, 1))
    shared["w_uq"] = f(inputs["w_uq"])
    shared["w_ukv"] = f(f(inputs["w_ukv"])[:, :, _perm_ukv()])
    shared["w_out"] = f(inputs["w_out"])
    shared["w_mlp1"] = f(inputs["w_mlp1"])
    shared["w_mlp2"] = f(inputs["w_mlp2"])
    shared.update(_consts())
    maps = []
    for b in range(8):
        m = dict(shared)
        m["x"] = x[b]
        m["ctx"] = ctx[b]
        cv = np.stack([c[b], c_ctx], 0)
        m["cT"] = f(cv.reshape(2, 8, 128).transpose(2, 1, 0))
        maps.append(m)
    return maps


_NC_CACHE = {}


def kernel(**inputs):
    maps = _host_inputs(inputs)
    if "nc" not in _NC_CACHE:
        _NC_CACHE["nc"] = build()
    res = run_bass_kernel_spmd(_NC_CACHE["nc"], maps, core_ids=list(range(8)))
    return np.stack([np.asarray(r["out"], dtype=np.float32) for r in res.results], 0)
```

```python
import numpy as np
import ml_dtypes
from contextlib import ExitStack
import concourse.bass as bass
import concourse.mybir as mybir
from concourse.bass_utils import run_bass_kernel_spmd

F32 = mybir.dt.float32
BF16 = mybir.dt.bfloat16
AF = mybir.ActivationFunctionType
ALU = mybir.AluOpType

D = 1024
S_LAT = 4096
L_CTX = 256
T = S_LAT + L_CTX
DEPTH = 2
DFF = 4096
D_IN = 1312
EPS = 1e-6
MLA_SCALE = 96.0 ** -0.5
SWA_SCALE = 0.125
C_F, C_Q, C_K, C_CQ, C_CKV, C_KR, C_V = 0, 256, 640, 768, 1024, 1152, 1184

ENGS = ["pe", "act", "dve", "pool", "sp"]
NDSEM = 8


class _Rec:
    def __getattr__(self, name):
        def f(*a, **k):
            self.call = (name, a, k)
            return self
        return f


class Sched:
    def __init__(self, nc):
        self.nc = nc
        self.ops = {e: [] for e in ENGS}
        self.res = {}
        self.known = {e: {} for e in ENGS}
        self.count = {e: 0 for e in ENGS}
        self.dma_n = {e: 0 for e in ENGS}
        self.pending = {e: {} for e in ENGS}
        self.out_dmas = []

    def _dep(self, eng, dep, waits):
        key, val = dep[:-1], dep[-1]
        if key == ("c", "pe") and eng == "pe":
            return
        if self.known[eng].get(key, 0) >= val:
            return
        self.known[eng][key] = val
        waits[key] = max(waits.get(key, 0), val)

    def op(self, eng, fn, reads=(), writes=(), dma=False, final=False):
        pb = [r for r in reads if r.startswith("pb")]
        if pb:
            reads = [r for r in reads if not r.startswith("pb")]
            writes = list(writes) + pb
        waits = {}
        for key, val in self.pending[eng].items():
            self._dep(eng, key + (val,), waits)
        self.pending[eng] = {}
        for r in reads:
            st = self.res.get(r)
            if st and st["w"]:
                self._dep(eng, st["w"], waits)
        for r in writes:
            st = self.res.get(r)
            if st:
                if st["w"]:
                    self._dep(eng, st["w"], waits)
                for d in st["r"]:
                    self._dep(eng, d, waits)
        if dma:
            n = self.dma_n[eng]
            slot = n % NDSEM
            val = 16 * (n // NDSEM + 1)
            self.dma_n[eng] += 1
            if n >= NDSEM:
                self._dep(eng, ("d", eng, slot, val - 16), waits)
            me = ("d", eng, slot, val)
            if final:
                self.out_dmas.append(me)
        else:
            self.count[eng] += 1
            me = ("c", eng, self.count[eng])
        rec = _Rec()
        fn(rec)
        self.ops[eng].append((rec.call, waits, me))
        for r in reads:
            self.res.setdefault(r, {"w": None, "r": []})["r"].append(me)
        for r in writes:
            self.res[r] = {"w": me, "r": []}
        return me

    def barrier(self):
        deps = {}
        for e in ENGS:
            if self.count[e] > 0:
                deps[("c", e)] = self.count[e]
            n = self.dma_n[e]
            for slot in range(min(n, NDSEM)):
                last = n - 1 - ((n - 1 - slot) % NDSEM)
                deps[("d", e, slot)] = 16 * (last // NDSEM + 1)
        for e in ENGS:
            for k, v in deps.items():
                self.pending[e][k] = max(self.pending[e].get(k, 0), v)
        self.res = {}

    def emit(self):
        nc = self.nc
        with ExitStack() as st:
            csem = {e: st.enter_context(nc.semaphore("c_" + e)) for e in ENGS}
            dsem = {(e, s): st.enter_context(nc.semaphore("d_%s%d" % (e, s)))
                    for e in ("sp", "pool", "act") for s in range(NDSEM)}
            block = st.enter_context(nc.Block())

            def semof(key):
                return csem[key[1]] if key[0] == "c" else dsem[(key[1], key[2])]

            def run(engname, eng):
                for fn, waits, me in self.ops[engname]:
                    for key, val in waits.items():
                        eng.wait_ge(semof(key), val)
                    ins = getattr(eng, fn[0])(*fn[1], **fn[2])
                    if me[0] == "c":
                        ins.then_inc(csem[engname], 1)
                    else:
                        ins.then_inc(dsem[(engname, me[2])], 16)
                if engname == "sp":
                    for me in self.out_dmas:
                        eng.wait_ge(dsem[(me[1], me[2])], me[3])

            @block.tensor
            def _(e):
                run("pe", e)

            @block.scalar
            def _(e):
                run("act", e)

            @block.vector
            def _(e):
                run("dve", e)

            @block.gpsimd
            def _(e):
                run("pool", e)

            @block.sync
            def _(e):
                run("sp", e)


class Arena:
    def __init__(self, ap_f32, nwords):
        self.t = ap_f32
        self.n = nwords
        self.off = 0

    def alloc(self, shape, dt):
        n = int(np.prod(shape))
        words = n if dt == F32 else (n + 1) // 2
        words = (words + 7) // 8 * 8
        assert self.off + words <= self.n, ("arena overflow", self.off, words, self.n)
        ap = self.t[:, self.off:self.off + words]
        self.off += words
        if dt != F32:
            ap = ap.bitcast(dt)
        ap = ap[:, 0:n]
        if len(shape) == 2:
            return ap.rearrange("p (a b) -> p a b", a=shape[0])
        if len(shape) == 3:
            return ap.rearrange("p (a b c) -> p a b c", a=shape[0], b=shape[1])
        return ap


class Ring:
    def __init__(self, arena, name, n, shape, dt):
        self.bufs = [arena.alloc(shape, dt) for _ in range(n)]
        self.names = ["%s#%d" % (name, i) for i in range(n)]
        self.i = 0

    def next(self):
        k = self.i % len(self.bufs)
        self.i += 1
        return self.bufs[k], self.names[k]


def _rope_tabs(dim):
    rows = S_LAT // 64
    r, col = np.meshgrid(np.arange(rows, dtype=np.float32), np.arange(64, dtype=np.float32), indexing="ij")
    r = r.reshape(-1)
    col = col.reshape(-1)
    nf = dim // 4
    inv = (np.float32(10000.0) ** (-np.arange(nf, dtype=np.float32) / np.float32(nf))).astype(np.float32)
    ang = np.concatenate([r[:, None] * inv[None, :], col[:, None] * inv[None, :]], axis=-1).astype(np.float32)
    c = np.cos(ang).astype(np.float32)
    s = np.sin(ang).astype(np.float32)
    half = dim // 2
    C2 = np.ones((dim, T), np.float32)
    S2 = np.zeros((dim, T), np.float32)
    C2[:half, :S_LAT] = c.T
    C2[half:, :S_LAT] = c.T
    S2[:half, :S_LAT] = -s.T
    S2[half:, :S_LAT] = s.T
    return C2, S2


def _consts():
    k = {}
    C2, S2 = _rope_tabs(64)
    ropeS = np.zeros((2, 128, T), np.float32)
    ropeS[0] = np.concatenate([C2, C2], 0)
    ropeS[1] = np.concatenate([S2, S2], 0)
    C2m, S2m = _rope_tabs(32)
    ropeM = np.zeros((2, 128, T), np.float32)
    ropeM[0] = 1.0
    for base in (0, 64):
        ropeM[0, base:base + 32] = C2m
        ropeM[1, base:base + 32] = S2m
    k["ropeS"] = ropeS
    k["ropeM"] = ropeM
    cm = np.zeros((5, 128, 128), np.float32)
    cm[0] = np.eye(128)
    cm[1] = 1.0
    for b0 in (0, 64):
        for d in range(32):
            cm[2, b0 + d, b0 + d + 32] = 1.0
            cm[2, b0 + d + 32, b0 + d] = 1.0
    for b0 in (0, 64):
        for d in range(16):
            cm[3, b0 + d, b0 + d + 16] = 1.0
            cm[3, b0 + d + 16, b0 + d] = 1.0
    a = np.arange(64)
    ang = 2 * np.pi * np.outer(a, a) / 64.0
    Cr, Sr = np.cos(ang) / 8.0, np.sin(ang) / 8.0
    cm[4, 0:64, 0:64] = Cr
    cm[4, 64:128, 0:64] = Sr
    cm[4, 0:64, 64:128] = -Sr
    cm[4, 64:128, 64:128] = Cr
    k["cmat"] = np.ascontiguousarray(cm.transpose(1, 0, 2))
    cf = np.zeros((3, 128, 128), np.float32)
    cf[0] = np.eye(128)
    for g in range(2):
        cf[1, g * 64:(g + 1) * 64, g * 64:(g + 1) * 64] = Cr
        cf[2, g * 64:(g + 1) * 64, g * 64:(g + 1) * 64] = -Sr
    k["cmatf"] = np.ascontiguousarray(cf.transpose(1, 0, 2))
    c = np.arange(64)[:, None, None]
    ka = np.arange(64)[None, :, None]
    kb = np.arange(64)[None, None, :]
    th = 2 * np.pi * c * (64 * kb + ka) / 4096.0
    m2 = np.zeros((128, 64, 64), np.float32)
    m2[0:64] = np.cos(th) / 8.0
    m2[64:128] = np.sin(th) / 8.0
    k["m2"] = m2
    tok = (np.arange(2)[None, :, None] * 128 + np.arange(128)[:, None, None]).astype(np.float64)
    kk = np.arange(256)[None, None, :]
    th = 2 * np.pi * tok * kk / 256.0
    d256 = np.zeros((128, 2, 2, 256), np.float32)
    d256[:, :, 0, :] = np.cos(th) / 16.0
    d256[:, :, 1, :] = np.sin(th) / 16.0
    k["d256"] = d256
    j = np.arange(128)[:, None, None]
    off = np.arange(-1, 5)[None, :, None]
    i = np.arange(512)[None, None, :]
    k["masks"] = (np.abs(128 * off + j - i) <= 128).astype(np.float32)
    return k


def _perm_in():
    idx = list(range(0, 256))
    for c in range(3):
        for h in (c, c + 3):
            idx += list(range(256 + h * 64, 256 + (h + 1) * 64))
    idx += list(range(640, 768))
    idx += list(range(896, 1152))
    idx += list(range(1152, 1280))
    idx += list(range(1280, 1312))
    idx += list(range(768, 896))
    return np.array(idx)


def _perm_ukv():
    kn, v = [], []
    for h in range(6):
        kn += list(range(h * 128, h * 128 + 64))
        v += list(range(h * 128 + 64, h * 128 + 128))
    return np.array(kn + v)


def build(debug=()):
    nc = bass.Bass("TRN2", target_bir_lowering=False)

    def din(name, shape, dt=F32):
        return nc.dram_tensor(name, list(shape), dt, kind="ExternalInput").ap()

    def dscr(name, shape, dt):
        kind = "ExternalOutput" if name in debug else "Internal"
        return nc.dram_tensor(name, list(shape), dt, kind=kind).ap()

    x = din("x", [S_LAT, D])
    ctx = din("ctx", [L_CTX, D])
    cT = din("cT", [128, 8, 2])
    w_ada = din("w_ada", [DEPTH, D, 6 * D])
    b_adaT = din("b_adaT", [DEPTH, 128, 48])
    n1T = din("n1T", [DEPTH, 128, 8])
    n2T = din("n2T", [DEPTH, 128, 8])
    fg = din("fg", [1, D])
    w_in = din("w_in", [DEPTH, D, D_IN])
    w_f = din("w_f", [DEPTH, 4, 64, 64])
    sink = din("sink", [DEPTH, 6])
    gqT = din("gqT", [DEPTH, 128, 2])
    gkvT = din("gkvT", [DEPTH, 128, 1])
    w_uq = din("w_uq", [DEPTH, 256, 576])
    w_ukv = din("w_ukv", [DEPTH, 128, 768])
    w_out = din("w_out", [DEPTH, D, D])
    w_mlp1 = din("w_mlp1", [DEPTH, D, DFF])
    w_mlp2 = din("w_mlp2", [DEPTH, DFF, D])
    ropeS = din("ropeS", [2, 128, T])
    ropeM = din("ropeM", [2, 128, T])
    cmat = din("cmat", [128, 5, 128])
    cmatf = din("cmatf", [128, 3, 128])
    m2d = din("m2", [128, 64, 64])
    d256d = din("d256", [128, 2, 2, 256])
    masksd = din("masks", [128, 6, 512])
    out = nc.dram_tensor("out", [S_LAT, D], F32, kind="ExternalOutput").ap()

    h_scr = dscr("h_scr", [T, D], F32)
    mod_scr = dscr("mod_scr", [DEPTH, 2, 6 * D], F32)
    y_scr = dscr("y_scr", [T, 512], BF16)
    z_scr = dscr("z_scr", [128, 64, 256], BF16)
    qs_scr = dscr("qs_scr", [3, 128, T], BF16)
    ks_scr = dscr("ks_scr", [128, T], BF16)
    vs_scr = dscr("vs_scr", [T, 128], BF16)
    qm_scr = dscr("qm_scr", [6, 96, T], BF16)
    kn_scr = dscr("kn_scr", [384, T], BF16)
    kr_scr = dscr("kr_scr", [32, T], BF16)
    vm_scr = dscr("vm_scr", [T, 384], BF16)
    mixT_scr = dscr("mixT_scr", [D, T], BF16)

    S = Sched(nc)
    SBW = 52736
    with ExitStack() as st:
        arena_t = st.enter_context(nc.sbuf_tensor("arena", [128, SBW], F32))
        pall = st.enter_context(nc.psum_tensor("pall", [128, 4096], F32))

        def bank(b, w=512):
            return pall[:, b * 512:b * 512 + w]

        def bname(b):
            return "pb%d" % b

        A = Arena(arena_t, SBW)
        cm = A.alloc([5, 128], BF16)
        ident, ones, PmS, PmM, D1 = (cm[:, i, :] for i in range(5))
        cmf = A.alloc([3, 128], F32)
        identf, CCre, CCim = (cmf[:, i, :] for i in range(3))
        epst = A.alloc([1], F32)
        modT = A.alloc([96], F32)
        amod = A.alloc([2, 2, 8], F32)
        smod = A.alloc([2, 2, 8], F32)
        nT = A.alloc([2, 8], F32)
        gq = A.alloc([2], F32)
        gkv = A.alloc([1], F32)
        exps = A.alloc([6], F32)
        gbc = A.alloc([2, D], F32)
        fgbc = A.alloc([D], F32)
        PERSIST = A.off

        S.op("pool", lambda e: e.dma_start(out=cm, in_=cmat), writes=["cm"], dma=True)
        S.op("sp", lambda e: e.dma_start(out=cmf, in_=cmatf), writes=["cmf"], dma=True)
        S.op("sp", lambda e: e.dma_start(out=fgbc, in_=fg.partition_broadcast(128)), writes=["fgbc"], dma=True)
        S.op("dve", lambda e: e.memset(epst, EPS), writes=["eps"])

        def blocks(width, with_ctx):
            bl = [(t0, width) for t0 in range(0, S_LAT, width)]
            if with_ctx:
                bl += [(t0, min(width, L_CTX)) for t0 in range(S_LAT, T, min(width, L_CTX))]
            return bl

        def phase_mod(l):
            S.barrier()
            A.off = PERSIST
            cTs = A.alloc([8, 2], F32)
            sil = A.alloc([8, 2], BF16)
            bT = A.alloc([48], F32)
            mrow = A.alloc([128], F32)
            wring = Ring(A, "wada", 2, [8, 1024], BF16)
            S.op("sp", lambda e: e.dma_start(out=cTs, in_=cT), writes=["cTs"], dma=True)
            S.op("sp", lambda e: e.dma_start(out=bT, in_=b_adaT[l]), writes=["bT"], dma=True)
            S.op("sp", lambda e: e.dma_start(out=nT[:, 0, :], in_=n1T[l]), writes=["nT"], dma=True)
            S.op("sp", lambda e: e.dma_start(out=nT[:, 1, :], in_=n2T[l]), writes=["nT"], dma=True)
            S.op("sp", lambda e: e.dma_start(out=gq, in_=gqT[l]), writes=["gq"], dma=True)
            S.op("sp", lambda e: e.dma_start(out=gkv, in_=gkvT[l]), writes=["gkv"], dma=True)
            S.op("sp", lambda e: e.dma_start(out=exps, in_=sink[l:l + 1, :].partition_broadcast(128)), writes=["exps"], dma=True)
            S.op("act", lambda e: e.activation(out=sil, in_=cTs, func=AF.Silu), reads=["cTs"], writes=["sil"])
            S.op("act", lambda e: e.activation(out=exps, in_=exps, func=AF.Exp), reads=["exps"], writes=["exps"])
            wv = w_ada[l].rearrange("(k p) n -> p k n", p=128)
            pb = bank(0, 96)
            for nb in range(6):
                wb, wn = wring.next()
                for kc in range(0, 8, 4):
                    S.op("pool", lambda e, wb=wb, nb=nb, kc=kc: e.dma_start(out=wb[:, kc:kc + 4, :], in_=wv[:, kc:kc + 4, nb * 1024:(nb + 1) * 1024]),
                         writes=[wn + "_%d" % (kc // 4)], dma=True)
                for cc in range(8):
                    chunk = nb * 8 + cc
                    for kc in range(8):
                        S.op("pe", lambda e, wb=wb, cc=cc, kc=kc, chunk=chunk: e.matmul(
                            pb[:, chunk * 2:chunk * 2 + 2], lhsT=wb[:, kc, cc * 128:(cc + 1) * 128], rhs=sil[:, kc, :],
                            start=(kc == 0), stop=(kc == 7)), reads=[wn + "_%d" % (kc // 4), "sil"], writes=[bname(0)])
            pbv = pb.rearrange("p (c j) -> p j c", j=2)
            for j in range(2):
                S.op("dve", lambda e, j=j: e.tensor_tensor(out=modT[:, j * 48:(j + 1) * 48], in0=pbv[:, j, :], in1=bT, op=ALU.add),
                     reads=[bname(0), "bT"], writes=["modT"])
            for ni, (s_sh, s_sc) in enumerate(((0, 1), (3, 4))):
                for j in range(2):
                    sc = modT[:, j * 48 + s_sc * 8: j * 48 + s_sc * 8 + 8]
                    shv = modT[:, j * 48 + s_sh * 8: j * 48 + s_sh * 8 + 8]
                    S.op("dve", lambda e, ni=ni, j=j, sc=sc: e.scalar_tensor_tensor(
                        out=amod[:, ni, j, :], in0=sc, scalar=1.0, in1=nT[:, ni, :], op0=ALU.add, op1=ALU.mult),
                        reads=["modT", "nT"], writes=["amod"])
                    S.op("dve", lambda e, ni=ni, j=j, shv=shv: e.tensor_copy(out=smod[:, ni, j, :], in_=shv),
                         reads=["modT"], writes=["smod"])
            S.op("pe", lambda e: e.transpose(out=bank(1, 128)[0:96, :], in_=modT, identity=identf), reads=["modT", "cmf"], writes=[bname(1)])
            S.op("dve", lambda e: e.tensor_copy(out=mrow[0:96, :], in_=bank(1, 128)[0:96, :]), reads=[bname(1)], writes=["mrow"])
            S.op("sp", lambda e: e.dma_start(out=mod_scr[l].rearrange("j (c p) -> (j c) p", p=128), in_=mrow[0:96, :]),
                 reads=["mrow"], writes=["mod_scr"], dma=True)

        def load_gate(l, sec):
            for j in range(2):
                S.op("sp", lambda e, j=j: e.dma_start(out=gbc[:, j, :], in_=mod_scr[l, j:j + 1, sec * D:(sec + 1) * D].partition_broadcast(128)),
                     reads=["mod_scr"], writes=["gbc"], dma=True)

        def norm_mod_T(htiles, nt, ni, j, xmT, xmn, ssb, xhring, tpbanks):
            junk = xhring.bufs[-1]
            for t in range(nt):
                ha, hn = htiles[t]
                S.op("act", lambda e, ha=ha, t=t: e.activation(out=junk, in_=ha, func=AF.Square, accum_out=ssb[:, t:t + 1]),
                     reads=[hn], writes=["junk", "ssb"])
            S.op("act", lambda e: e.activation(out=ssb[:, 4:4 + nt], in_=ssb[:, 0:nt], func=AF.Ln, scale=1.0 / D, bias=epst[:, 0:1]),
                 reads=["ssb", "eps"], writes=["ssb"])
            S.op("act", lambda e: e.activation(out=ssb[:, 8:8 + nt], in_=ssb[:, 4:4 + nt], func=AF.Exp, scale=-0.5),
                 reads=["ssb"], writes=["ssb"])
            for t in range(nt):
                ha, hn = htiles[t]
                xh = xhring.bufs[t % (len(xhring.bufs) - 1)]
                xn = xhring.names[t % (len(xhring.bufs) - 1)]
                S.op("act", lambda e, ha=ha, xh=xh, t=t: e.activation(out=xh, in_=ha, func=AF.Copy, scale=ssb[:, 8 + t:9 + t]),
                     reads=[hn, "ssb"], writes=[xn])
                tb = tpbanks[t % len(tpbanks)]
                tp = bank(tb).bitcast(BF16)
                for kc in range(8):
                    S.op("pe", lambda e, xh=xh, kc=kc, tp=tp: e.transpose(out=tp[:, kc * 128:(kc + 1) * 128], in_=xh[:, kc * 128:(kc + 1) * 128], identity=ident),
                         reads=[xn, "cm"], writes=[bname(tb)])
                tp3 = tp.rearrange("p (k c) -> p k c", k=8)
                dst = xmT[:, :, t * 128:(t + 1) * 128]
                S.op("dve", lambda e, tp3=tp3, dst=dst: e.tensor_tensor(out=dst, in0=tp3, in1=amod[:, ni, j, :].unsqueeze(2).to_broadcast([128, 8, 128]), op=ALU.mult),
                     reads=[bname(tb), "amod"], writes=[xmn])
                S.op("dve", lambda e, dst=dst: e.tensor_tensor(out=dst, in0=dst, in1=smod[:, ni, j, :].unsqueeze(2).to_broadcast([128, 8, 128]), op=ALU.add),
                     reads=[xmn, "smod"], writes=[xmn])

        def phase_A(l):
            S.barrier()
            A.off = PERSIST
            last = l == DEPTH - 1
            Win = A.alloc([8, D_IN], BF16)
            wuq = A.alloc([2, 576], BF16)
            wukv = A.alloc([768], BF16)
            wfbd = A.alloc([2, 128], F32)
            Abd = A.alloc([2, 256], BF16)
            hring = Ring(A, "hA", 8, [D], F32)
            xhring = Ring(A, "xhA", 3, [D], BF16)
            xmring = Ring(A, "xmA", 2, [8, 512], BF16)
            ssb = A.alloc([12], F32)
            rtab = Ring(A, "rtab", 2, [4, 512], F32)
            fT = A.alloc([2, 512], BF16)
            ysb = Ring(A, "ysb", 2, [512], BF16)
            qb = Ring(A, "qb", 3, [512], BF16)
            t1r = Ring(A, "t1", 3, [512], F32)
            t2r = Ring(A, "t2", 3, [512], F32)
            qo = Ring(A, "qo", 4, [512], BF16)
            sq = A.alloc([2, 512], BF16)
            lnv = A.alloc([512], F32)
            rbc = A.alloc([512], F32)
            cqn = A.alloc([2, 512], BF16)
            ckvn = A.alloc([512], BF16)
            knb = Ring(A, "knb", 2, [512], BF16)
            vtb = Ring(A, "vtb", 2, [384], BF16)

            wv = w_in[l].rearrange("(k p) n -> p k n", p=128)
            for kc in range(0, 8, 2):
                S.op("pool", lambda e, kc=kc: e.dma_start(out=Win[:, kc:kc + 2, :], in_=wv[:, kc:kc + 2, :]), writes=["Win"], dma=True)
            S.op("pool", lambda e: e.dma_start(out=wuq, in_=w_uq[l].rearrange("(k p) n -> p k n", p=128)), writes=["wuq"], dma=True)
            S.op("pool", lambda e: e.dma_start(out=wukv, in_=w_ukv[l]), writes=["wukv"], dma=True)
            S.op("dve", lambda e: e.memset(wfbd, 0.0), writes=["wfbd"])
            for g in range(4):
                jc, gl = g // 2, g % 2
                S.op("sp", lambda e, g=g, jc=jc, gl=gl: e.dma_start(out=wfbd[gl * 64:(gl + 1) * 64, jc, gl * 64:(gl + 1) * 64], in_=w_f[l, g]),
                     reads=["wfbd"], writes=["wfbd%d" % g], dma=True)
            for jc in range(2):
                for ri, Cm in enumerate((CCre, CCim)):
                    S.op("pe", lambda e, jc=jc, ri=ri, Cm=Cm: e.matmul(bank(4, 256)[:, ri * 128:(ri + 1) * 128], lhsT=Cm, rhs=wfbd[:, jc, :], start=True, stop=True),
                         reads=["cmf", "wfbd", "wfbd0", "wfbd1", "wfbd2", "wfbd3"], writes=[bname(4)])
                S.op("dve", lambda e, jc=jc: e.tensor_copy(out=Abd[:, jc, :], in_=bank(4, 256)), reads=[bname(4)], writes=["Abd"])

            src_lat = x if l == 0 else h_scr[0:S_LAT]
            src_ctx = ctx if l == 0 else h_scr[S_LAT:T]
            bl = blocks(512, True)
            hloaded = {}

            def load_block(bi):
                t0, W = bl[bi]
                tiles = []
                for t in range(W // 128):
                    ha, hn = hring.next()
                    tok = t0 + t * 128
                    src = src_lat[tok:tok + 128] if tok < S_LAT else src_ctx[tok - S_LAT:tok - S_LAT + 128]
                    S.op("sp", lambda e, ha=ha, src=src: e.dma_start(out=ha, in_=src), reads=["h_scr%d" % (tok // 128)], writes=[hn], dma=True)
                    tiles.append((ha, hn))
                rt, rn = rtab.next()
                S.op("sp", lambda e, rt=rt, t0=t0, W=W: e.dma_start(out=rt[:, 0:2, 0:W], in_=ropeS[:, :, t0:t0 + W].rearrange("a p w -> p a w")), writes=[rn], dma=True)
                S.op("sp", lambda e, rt=rt, t0=t0, W=W: e.dma_start(out=rt[:, 2:4, 0:W], in_=ropeM[:, :, t0:t0 + W].rearrange("a p w -> p a w")), writes=[rn + "m"], dma=True)
                hloaded[bi] = (tiles, rt, rn)

            auxc = [0]

            def rope(psb, rows, W, Ct, St, Pm, outap, outn, tabn):
                r0, r1 = rows
                q_b, q_n = qb.next()
                t1, t1n = t1r.next()
                t2, t2n = t2r.next()
                ps = bank(psb)
                ab = (1, 4)[auxc[0] % 2]
                auxc[0] += 1
                S.op("act", lambda e: e.activation(out=q_b[r0:r1, 0:W], in_=ps[r0:r1, 0:W], func=AF.Copy), reads=[bname(psb)], writes=[q_n])
                S.op("pe", lambda e: e.matmul(bank(ab)[r0:r1, 0:W], lhsT=Pm[r0:r1, r0:r1], rhs=q_b[r0:r1, 0:W], start=True, stop=True),
                     reads=[q_n, "cm"], writes=[bname(ab)])
                S.op("dve", lambda e: e.tensor_tensor(out=t1[r0:r1, 0:W], in0=ps[r0:r1, 0:W], in1=Ct[r0:r1, 0:W], op=ALU.mult),
                     reads=[bname(psb), tabn], writes=[t1n])
                S.op("dve", lambda e: e.tensor_tensor(out=t2[r0:r1, 0:W], in0=bank(ab)[r0:r1, 0:W], in1=St[r0:r1, 0:W], op=ALU.mult),
                     reads=[bname(ab), tabn], writes=[t2n])
                S.op("dve", lambda e: e.tensor_tensor(out=outap[r0:r1, 0:W], in0=t1[r0:r1, 0:W], in1=t2[r0:r1, 0:W], op=ALU.add),
                     reads=[t1n, t2n], writes=[outn])

            def rms_bc(psbanks, W, nfeat):
                n = len(psbanks)
                for jj, b in enumerate(psbanks):
                    S.op("act", lambda e, jj=jj, b=b: e.activation(out=sq[:, jj, 0:W], in_=bank(b)[:, 0:W], func=AF.Square), reads=[bname(b)], writes=["sq%d" % jj])
                ab = (1, 4)[auxc[0] % 2]
                auxc[0] += 1
                for jj in range(n):
                    S.op("pe", lambda e, jj=jj: e.matmul(bank(ab)[:, 0:W], lhsT=ones, rhs=sq[:, jj, 0:W], start=(jj == 0), stop=(jj == n - 1)),
                         reads=["sq%d" % jj, "cm"], writes=[bname(ab)])
                S.op("act", lambda e: e.activation(out=lnv[:, 0:W], in_=bank(ab)[:, 0:W], func=AF.Ln, scale=1.0 / nfeat, bias=epst[:, 0:1]),
                     reads=[bname(ab), "eps"], writes=["lnv"])
                S.op("act", lambda e: e.activation(out=rbc[:, 0:W], in_=lnv[:, 0:W], func=AF.Exp, scale=-0.5), reads=["lnv"], writes=["rbc"])

            ubanks = [2, 3, 5]
            ucnt = [0]

            def inproj(xm, xmn, col0, M, W, b=None):
                if b is None:
                    b = ubanks[ucnt[0] % 3]
                    ucnt[0] += 1
                for kc in range(8):
                    S.op("pe", lambda e, kc=kc: e.matmul(bank(b)[0:M, 0:W], lhsT=Win[:, kc, col0:col0 + M], rhs=xm[:, kc, 0:W], start=(kc == 0), stop=(kc == 7)),
                         reads=["Win", xmn], writes=[bname(b)])
                return b

            tokb = [6, 7]
            tokc = [0]

            def tokbank():
                b = tokb[tokc[0] % 2]
                tokc[0] += 1
                return b

            xms = {}

            def do_norm(bi):
                t0, W = bl[bi]
                tiles, rt, rn = hloaded[bi]
                xm, xmn = xmring.next()
                norm_mod_T(tiles, W // 128, 0, 0 if t0 < S_LAT else 1, xm, xmn, ssb, xhring, [0])
                xms[bi] = (xm, xmn)

            def make_tasks(bi):
                t0, W = bl[bi]
                tiles, rt, rn = hloaded.pop(bi)
                nt = W // 128
                xm, xmn = xms.pop(bi)
                CS, SS, CM, SM = rt[:, 0, :], rt[:, 1, :], rt[:, 2, :], rt[:, 3, :]
                tasks = []
                bk = {}

                def mk_first(key, cols):
                    def first():
                        bk[key] = [inproj(xm, xmn, c0, M, W) for (c0, M) in cols]
                    return first

                def rest_f(jc):
                    def rest():
                        b = bk[("f", jc)][0]
                        S.op("act", lambda e: e.activation(out=fT[:, jc, 0:W], in_=bank(b)[:, 0:W], func=AF.Copy), reads=[bname(b)], writes=["fT%d" % jc])
                        if jc == 1:
                            for t in range(nt):
                                tb = tokbank()
                                for jj in range(2):
                                    for ri in range(2):
                                        o = bank(tb)[:, ri * 256 + jj * 128: ri * 256 + (jj + 1) * 128]
                                        S.op("pe", lambda e, o=o, jj=jj, t=t, ri=ri: e.matmul(o, lhsT=fT[:, jj, t * 128:(t + 1) * 128], rhs=Abd[:, jj, ri * 128:(ri + 1) * 128],
                                                                                           start=True, stop=True), reads=["fT%d" % jj, "Abd"], writes=[bname(tb)])
                                yb, yn = ysb.next()
                                S.op("act", lambda e, yb=yb, tb=tb: e.activation(out=yb, in_=bank(tb), func=AF.Copy), reads=[bname(tb)], writes=[yn])
                                S.op("sp", lambda e, yb=yb, t=t: e.dma_start(out=y_scr[t0 + t * 128:t0 + (t + 1) * 128, :], in_=yb), reads=[yn], writes=["y_scr"], dma=True)
                    return rest
                for jc in range(2):
                    tasks.append((mk_first(("f", jc), [(C_F + jc * 128, 128)]), rest_f(jc)))

                def rest_q(c):
                    def rest():
                        b = bk[("q", c)][0]
                        oa, on = qo.next()
                        rope(b, (0, 128), W, CS, SS, PmS, oa, on, rn)
                        dst = qs_scr[c, :, t0:t0 + W] if c < 3 else ks_scr[:, t0:t0 + W]
                        S.op("sp", lambda e: e.dma_start(out=dst, in_=oa[:, 0:W]), reads=[on], writes=["qks_scr"], dma=True)
                    return rest
                for c in range(4):
                    tasks.append((mk_first(("q", c), [(C_Q + c * 128, 128)]), rest_q(c)))
                    if c == 1 and bi + 1 < len(bl):
                        tasks.append((lambda: None, lambda: do_norm(bi + 1)))

                def rest_cq():
                    b0, b1 = bk["cq"]
                    rms_bc([b0, b1], W, 256)
                    for jj, b in enumerate((b0, b1)):
                        S.op("dve", lambda e, jj=jj, b=b: e.scalar_tensor_tensor(out=cqn[:, jj, 0:W], in0=bank(b)[:, 0:W], scalar=gq[:, jj:jj + 1], in1=rbc[:, 0:W],
                                                                                 op0=ALU.mult, op1=ALU.mult), reads=[bname(b), "gq", "rbc"], writes=["cqn"])
                    for hd in range(6):
                        ub_ = tokbank()
                        for jj in range(2):
                            S.op("pe", lambda e, hd=hd, jj=jj, ub_=ub_: e.matmul(bank(ub_)[0:96, 0:W], lhsT=wuq[:, jj, hd * 96:(hd + 1) * 96], rhs=cqn[:, jj, 0:W], start=(jj == 0), stop=(jj == 1)),
                                 reads=["wuq", "cqn"], writes=[bname(ub_)])
                        oa, on = qo.next()
                        S.op("act", lambda e, oa=oa, ub_=ub_: e.activation(out=oa[0:64, 0:W], in_=bank(ub_)[0:64, 0:W], func=AF.Copy), reads=[bname(ub_)], writes=[on])
                        rope(ub_, (64, 96), W, CM, SM, PmM, oa, on, rn + "m")
                        S.op("sp", lambda e, oa=oa, hd=hd: e.dma_start(out=qm_scr[hd, :, t0:t0 + W], in_=oa[0:96, 0:W]), reads=[on], writes=["qm_scr"], dma=True)
                tasks.append((mk_first("cq", [(C_CQ, 128), (C_CQ + 128, 128)]), rest_cq))

                def rest_ckv():
                    b0 = bk["ckv"][0]
                    rms_bc([b0], W, 128)
                    S.op("dve", lambda e: e.scalar_tensor_tensor(out=ckvn[:, 0:W], in0=bank(b0)[:, 0:W], scalar=gkv[:, 0:1], in1=rbc[:, 0:W], op0=ALU.mult, op1=ALU.mult),
                         reads=[bname(b0), "gkv", "rbc"], writes=["ckvn"])
                    for pr in range(3):
                        ub_ = tokbank()
                        S.op("pe", lambda e, pr=pr, ub_=ub_: e.matmul(bank(ub_)[:, 0:W], lhsT=wukv[:, pr * 128:(pr + 1) * 128], rhs=ckvn[:, 0:W], start=True, stop=True),
                             reads=["wukv", "ckvn"], writes=[bname(ub_)])
                        ka, kn_ = knb.next()
                        S.op("act", lambda e, ka=ka, ub_=ub_: e.activation(out=ka[:, 0:W], in_=bank(ub_)[:, 0:W], func=AF.Copy), reads=[bname(ub_)], writes=[kn_])
                        S.op("sp", lambda e, ka=ka, pr=pr: e.dma_start(out=kn_scr[pr * 128:(pr + 1) * 128, t0:t0 + W], in_=ka[:, 0:W]), reads=[kn_], writes=["kn_scr"], dma=True)
                    for t in range(nt):
                        tb = tokbank()
                        S.op("pe", lambda e, tb=tb, t=t: e.matmul(bank(tb)[:, 0:384], lhsT=ckvn[:, t * 128:(t + 1) * 128], rhs=wukv[:, 384:768], start=True, stop=True),
                             reads=["wukv", "ckvn"], writes=[bname(tb)])
                        va, vn = vtb.next()
                        S.op("dve", lambda e, va=va, tb=tb: e.tensor_copy(out=va, in_=bank(tb)[:, 0:384]), reads=[bname(tb)], writes=[vn])
                        S.op("sp", lambda e, va=va, t=t: e.dma_start(out=vm_scr[t0 + t * 128:t0 + (t + 1) * 128, :], in_=va), reads=[vn], writes=["vm_scr"], dma=True)
                tasks.append((mk_first("ckv", [(C_CKV, 128)]), rest_ckv))

                def rest_kr():
                    b = bk["kr"][0]
                    oa, on = qo.next()
                    rope(b, (0, 32), W, CM, SM, PmM, oa, on, rn + "m")
                    S.op("sp", lambda e: e.dma_start(out=kr_scr[:, t0:t0 + W], in_=oa[0:32, 0:W]), reads=[on], writes=["kr_scr"], dma=True)
                tasks.append((mk_first("kr", [(C_KR, 32)]), rest_kr))

                def rest_v():
                    for t in range(nt):
                        tb = tokbank()
                        for kc in range(8):
                            S.op("pe", lambda e, tb=tb, t=t, kc=kc: e.matmul(bank(tb)[:, 0:128], lhsT=xm[:, kc, t * 128:(t + 1) * 128], rhs=Win[:, kc, C_V:C_V + 128],
                                                                            start=(kc == 0), stop=(kc == 7)), reads=["Win", xmn], writes=[bname(tb)])
                        va, vn = vtb.next()
                        S.op("dve", lambda e, va=va, tb=tb: e.tensor_copy(out=va[:, 0:128], in_=bank(tb)[:, 0:128]), reads=[bname(tb)], writes=[vn])
                        S.op("sp", lambda e, va=va, t=t: e.dma_start(out=vs_scr[t0 + t * 128:t0 + (t + 1) * 128, :], in_=va[:, 0:128]), reads=[vn], writes=["vs_scr"], dma=True)
                tasks.append((lambda: None, rest_v))
                return tasks

            load_block(0)
            do_norm(0)
            for bi in range(len(bl)):
                if bi + 1 < len(bl):
                    load_block(bi + 1)
                run_pipeline(make_tasks(bi), 1)

        def phase_F(l):
            S.barrier()
            A.off = PERSIST
            last = l == DEPTH - 1
            M2 = A.alloc([64, 64], BF16)
            Yin = A.alloc([64, 256], BF16)
            Zs = A.alloc([64, 256], BF16)
            fouT = A.alloc([2, S_LAT], BF16)
            S.op("pool", lambda e: e.dma_start(out=M2, in_=m2d), writes=["M2"], dma=True)
            yv = y_scr[0:S_LAT].rearrange("(r c) (ri ch) -> ri r c ch", c=64, ri=2)
            for ri in range(2):
                for cq4 in range(4):
                    S.op("sp", lambda e, ri=ri, cq4=cq4: e.dma_start(out=Yin[ri * 64:(ri + 1) * 64, cq4 * 16:(cq4 + 1) * 16, :], in_=yv[ri, :, cq4 * 16:(cq4 + 1) * 16, :]),
                         writes=["Yin%d" % cq4], dma=True)
            Yf = Yin.rearrange("p c ch -> p (c ch)")
            Zf = Zs.rearrange("p c ch -> p (c ch)")
            for i in range(32):
                b = i % 4
                S.op("pe", lambda e, i=i, b=b: e.matmul(bank(b), lhsT=D1, rhs=Yf[:, i * 512:(i + 1) * 512], start=True, stop=True),
                     reads=["Yin%d" % (i // 8), "cm"], writes=[bname(b)])
                if i % 2 == 0:
                    S.op("act", lambda e, i=i, b=b: e.activation(out=Zf[:, i * 512:(i + 1) * 512], in_=bank(b), func=AF.Copy), reads=[bname(b)], writes=["Zs%d" % (i // 8)])
                else:
                    S.op("dve", lambda e, i=i, b=b: e.tensor_copy(out=Zf[:, i * 512:(i + 1) * 512], in_=bank(b)), reads=[bname(b)], writes=["Zs%d" % (i // 8)])
                if i % 8 == 7:
                    q4 = i // 8
                    S.op("sp", lambda e, q4=q4: e.dma_start(out=z_scr[:, q4 * 16:(q4 + 1) * 16, :], in_=Zs[:, q4 * 16:(q4 + 1) * 16, :]),
                         reads=["Zs%d" % q4], writes=["z_scr"], dma=True)
            Zin = Yin
            zv = z_scr.rearrange("(ri ka) c ch -> ri c ka ch", ri=2)
            for ri in range(2):
                for k4 in range(4):
                    S.op("sp", lambda e, ri=ri, k4=k4: e.dma_start(out=Zin[ri * 64:(ri + 1) * 64, k4 * 16:(k4 + 1) * 16, :], in_=zv[ri, :, k4 * 16:(k4 + 1) * 16, :]),
                         reads=["z_scr"], writes=["Zin%d" % k4] + ["Yin%d" % q for q in range(4)], dma=True)
            n = 0
            for ch in range(2):
                fv = fouT[:, ch, :].rearrange("p (kb ka) -> p ka kb", ka=64)
                for kag in range(8):
                    b = n % 4
                    n += 1
                    for a in range(8):
                        ka = kag * 8 + a
                        S.op("pe", lambda e, b=b, a=a, ka=ka, ch=ch: e.matmul(bank(b)[:, a * 64:(a + 1) * 64], lhsT=Zin[:, ka, ch * 128:(ch + 1) * 128], rhs=M2[:, ka, :],
                                                                             start=True, stop=True), reads=["Zin%d" % (ka // 16), "M2"] + ["Yin%d" % q for q in range(4)], writes=[bname(b)])
                    src = bank(b).rearrange("p (a kb) -> p a kb", a=8)
                    dst = fv[:, kag * 8:(kag + 1) * 8, :]
                    if kag % 2 == 0:
                        S.op("act", lambda e, src=src, dst=dst: e.activation(out=dst, in_=src, func=AF.Copy), reads=[bname(b)], writes=["fouT"])
                    else:
                        S.op("dve", lambda e, src=src, dst=dst: e.tensor_copy(out=dst, in_=src), reads=[bname(b)], writes=["fouT"])
            for ch in range(2):
                S.op("sp", lambda e, ch=ch: e.dma_start(out=mixT_scr[ch * 128:(ch + 1) * 128, 0:S_LAT], in_=fouT[:, ch, :]), reads=["fouT"], writes=["mixT_scr"], dma=True)
            if not last:
                D256 = A.alloc([2, 2, 256], BF16)
                yc = A.alloc([2, 512], BF16)
                fc = A.alloc([2, 256], BF16)
                S.op("pool", lambda e: e.dma_start(out=D256, in_=d256d), writes=["D256"], dma=True)
                S.op("sp", lambda e: e.dma_start(out=yc, in_=y_scr[S_LAT:T].rearrange("(t p) n -> p t n", p=128)), reads=["y_scr"], writes=["yc"], dma=True)
                for ch in range(2):
                    b = 4 + ch
                    k = 0
                    for t in range(2):
                        for ri in range(2):
                            S.op("pe", lambda e, b=b, t=t, ri=ri, ch=ch, k=k: e.matmul(bank(b)[:, 0:256], lhsT=yc[:, t, ri * 256 + ch * 128: ri * 256 + (ch + 1) * 128],
                                                                                    rhs=D256[:, t, ri, :], start=(k == 0), stop=(k == 3)), reads=["yc", "D256"], writes=[bname(b)])
                            k += 1
                    S.op("act", lambda e, b=b, ch=ch: e.activation(out=fc[:, ch, :], in_=bank(b)[:, 0:256], func=AF.Copy), reads=[bname(b)], writes=["fc"])
                    S.op("sp", lambda e, ch=ch: e.dma_start(out=mixT_scr[ch * 128:(ch + 1) * 128, S_LAT:T], in_=fc[:, ch, :]), reads=["fc"], writes=["mixT_scr"], dma=True)

        def run_pipeline(tasks, look):
            n = len(tasks)
            for i in range(min(look, n)):
                tasks[i][0]()
            for i in range(n):
                if i + look < n:
                    tasks[i + look][0]()
                tasks[i][1]()

        def attn_finish(ob, W, exp_ap, row0, t0, dnr, rcr, ob_ring):
            o = bank(ob)
            dn, dnn = dnr.next()
            rc, rcn = rcr.next()
            if exp_ap is not None:
                S.op("dve", lambda e: e.tensor_scalar(out=dn[64:128, 0:W], in0=o[64:128, 0:W], scalar1=exp_ap, scalar2=None, op0=ALU.add),
                     reads=[bname(ob), "exps"], writes=[dnn])
                S.op("dve", lambda e: e.reciprocal(out=rc[64:128, 0:W], in_=dn[64:128, 0:W]), reads=[dnn], writes=[rcn])
            else:
                S.op("dve", lambda e: e.reciprocal(out=rc[64:128, 0:W], in_=o[64:128, 0:W]), reads=[bname(ob)], writes=[rcn])
            oa, on = ob_ring.next()
            S.op("dve", lambda e: e.tensor_tensor(out=oa[0:64, 0:W], in0=o[0:64, 0:W], in1=rc[64:128, 0:W], op=ALU.mult), reads=[bname(ob), rcn], writes=[on])
            S.op("sp", lambda e: e.dma_start(out=mixT_scr[row0:row0 + 64, t0:t0 + W], in_=oa[0:64, 0:W]), reads=[on], writes=["mixT_scr"], dma=True)

        def phase_B1(l):
            S.barrier()
            A.off = PERSIST
            last = l == DEPTH - 1
            Ks = A.alloc([T], BF16)
            Vs = A.alloc([34, 2, 128], BF16)
            masks = A.alloc([6, 512], BF16)
            Qring = Ring(A, "Qs", 3, [T], BF16)
            pT = Ring(A, "pT", 4, [512], BF16)
            dnr = Ring(A, "dn", 2, [512], F32)
            rcr = Ring(A, "rc", 2, [512], F32)
            obr = Ring(A, "ob", 2, [512], BF16)
            S.op("pool", lambda e: e.dma_start(out=masks, in_=masksd), writes=["masks"], dma=True)
            S.op("sp", lambda e: e.dma_start(out=Ks, in_=ks_scr), writes=["Ks"], dma=True)
            S.op("dve", lambda e: e.memset(Vs, 1.0), writes=["Vs"])
            vv = vs_scr.rearrange("(blk p) (kvh d) -> p blk kvh d", p=128, kvh=2)
            vnames = []
            for g0 in range(0, 34, 12):
                g1 = min(34, g0 + 12)
                for kvh in range(2):
                    S.op("sp", lambda e, g0=g0, g1=g1, kvh=kvh: e.dma_start(out=Vs[:, g0:g1, kvh, 0:64], in_=vv[:, g0:g1, kvh, :]), reads=["Vs"], writes=["Vs%d_%d" % (g0, kvh)], dma=True)
                    vnames.append("Vs%d_%d" % (g0, kvh))
            bl = blocks(512, not last)
            Qs = []
            for c in range(3):
                Qa, Qn = Qring.next()
                S.op("sp", lambda e, Qa=Qa, c=c: e.dma_start(out=Qa, in_=qs_scr[c]), writes=[Qn], dma=True)
                Qs.append((Qa, Qn))
            tasks = []
            sb_i = 0
            ob_i = 0
            for c in range(3):
                Qa, Qn = Qs[c]
                for hh in range(2):
                    head = c + 3 * hh
                    r0 = hh * 64
                    for (t0, W) in bl:
                        if t0 < S_LAT:
                            qb_ = t0 // 512
                            kbs = [(32, None), (33, None)] + [(kb, kb - 4 * qb_ + 1) for kb in range(4 * qb_ - 1, 4 * qb_ + 5) if 0 <= kb < 32]
                        else:
                            kbs = [(32, None), (33, None)]
                        ob = 4 + ob_i % 2
                        ob_i += 1
                        for i, (kb, mi) in enumerate(kbs):
                            sbk = sb_i % 4
                            sb_i += 1
                            pa, pn = pT.next()

                            c0, c1 = (0, W) if mi is None else ((0, 128), (0, 256), (0, 384), (128, 512), (256, 512), (384, 512))[mi]

                            def first(sbk=sbk, kb=kb, Qa=Qa, Qn=Qn, r0=r0, t0=t0, c0=c0, c1=c1):
                                S.op("pe", lambda e: e.matmul(bank(sbk)[:, c0:c1], lhsT=Ks[r0:r0 + 64, kb * 128:(kb + 1) * 128], rhs=Qa[r0:r0 + 64, t0 + c0:t0 + c1],
                                                              start=True, stop=True), reads=["Ks", Qn], writes=[bname(sbk)])

                            def rest(sbk=sbk, kb=kb, mi=mi, pa=pa, pn=pn, ob=ob, i=i, nk=len(kbs), hh=hh, head=head, t0=t0, W=W, c0=c0, c1=c1):
                                S.op("act", lambda e: e.activation(out=pa[:, c0:c1], in_=bank(sbk)[:, c0:c1], func=AF.Exp, scale=SWA_SCALE), reads=[bname(sbk)], writes=[pn])
                                if mi is not None:
                                    S.op("dve", lambda e: e.tensor_tensor(out=pa[:, c0:c1], in0=pa[:, c0:c1], in1=masks[:, mi, c0:c1], op=ALU.mult),
                                         reads=[pn, "masks"], writes=[pn])
                                S.op("pe", lambda e: e.matmul(bank(ob)[:, c0:c1], lhsT=Vs[:, kb, hh, :], rhs=pa[:, c0:c1], start=(i == 0), stop=(i == nk - 1),
                                                              skip_group_check=(i > 0)),
                                     reads=[pn, "Vs"] + vnames, writes=[bname(ob)])
                                if i == nk - 1:
                                    attn_finish(ob, W, exps[64:128, head:head + 1], 256 + head * 64, t0, dnr, rcr, obr)

                            tasks.append((first, rest))
            run_pipeline(tasks, 3)

        def phase_B2(l):
            S.barrier()
            A.off = PERSIST
            last = l == DEPTH - 1
            Qr = Ring(A, "Qm", 2, [T], BF16)
            Kr = Ring(A, "Km", 2, [T], BF16)
            Vr = Ring(A, "Vm", 2, [34, 128], BF16)
            pT = Ring(A, "pT2", 3, [2, 512], BF16)
            dnr = Ring(A, "dn2", 2, [512], F32)
            rcr = Ring(A, "rc2", 2, [512], F32)
            obr = Ring(A, "ob2", 2, [512], BF16)
            for va, vn in zip(Vr.bufs, Vr.names):
                S.op("dve", lambda e, va=va: e.memset(va, 1.0), writes=[vn])
            vv = vm_scr.rearrange("(blk p) (h d) -> p blk h d", p=128, h=6)
            bl = blocks(512, not last)
            hl = {}

            def load_head(hd):
                Qa, Qn = Qr.next()
                Ka, Kn = Kr.next()
                Va, Vn = Vr.next()
                S.op("sp", lambda e, Qa=Qa, hd=hd: e.dma_start(out=Qa[0:96, :], in_=qm_scr[hd]), writes=[Qn], dma=True)
                S.op("sp", lambda e, Ka=Ka, hd=hd: e.dma_start(out=Ka[0:64, :], in_=kn_scr[hd * 64:(hd + 1) * 64, :]), writes=[Kn], dma=True)
                S.op("sp", lambda e, Ka=Ka: e.dma_start(out=Ka[64:96, :], in_=kr_scr), writes=[Kn + "r"], dma=True)
                vparts = []
                for g0 in range(0, 34, 9):
                    g1 = min(34, g0 + 9)
                    S.op("sp", lambda e, Va=Va, g0=g0, g1=g1, hd=hd: e.dma_start(out=Va[:, g0:g1, 0:64], in_=vv[:, g0:g1, hd, :]), reads=[Vn], writes=[Vn + "_%d" % g0], dma=True)
                    vparts.append(Vn + "_%d" % g0)
                hl[hd] = (Qa, Qn, Ka, Kn, Va, Vn, vparts)

            tasks = []
            sp_i = 0
            ob_i = 0
            for hd in range(6):
                for bi, (t0, W) in enumerate(bl):
                    kbs = list(range(34)) if t0 < S_LAT else [32, 33]
                    ob = 4 + ob_i % 2
                    ob_i += 1
                    npairs = len(kbs) // 2
                    for pi in range(npairs):
                        sp_ = [0, 2, 6][sp_i % 3]
                        sp_i += 1
                        pa, pn = pT.next()
                        need_load = (bi == 0 and pi == 0)
                        prefetch = (bi == 1 and pi == 0)

                        def first(hd=hd, sp_=sp_, kb0=kbs[2 * pi], kb1=kbs[2 * pi + 1], t0=t0, W=W, need_load=need_load, prefetch=prefetch):
                            if need_load and hd not in hl:
                                load_head(hd)
                            if prefetch and hd + 1 < 6:
                                load_head(hd + 1)
                            Qa, Qn, Ka, Kn, Va, Vn, vparts = hl[hd]
                            for u, kb in enumerate((kb0, kb1)):
                                S.op("pe", lambda e, u=u, kb=kb: e.matmul(bank(sp_ + u)[:, 0:W], lhsT=Ka[0:96, kb * 128:(kb + 1) * 128], rhs=Qa[0:96, t0:t0 + W],
                                                                        start=True, stop=True), reads=[Kn, Kn + "r", Qn], writes=[bname(sp_ + u)])

                        def rest(hd=hd, sp_=sp_, kb0=kbs[2 * pi], kb1=kbs[2 * pi + 1], t0=t0, W=W, pa=pa, pn=pn, ob=ob, pi=pi, npairs=npairs):
                            Qa, Qn, Ka, Kn, Va, Vn, vparts = hl[hd]
                            src = pall[:, sp_ * 512:(sp_ + 2) * 512].rearrange("p (b w) -> p b w", b=2)[:, :, 0:W]
                            S.op("act", lambda e: e.activation(out=pa[:, :, 0:W], in_=src, func=AF.Exp, scale=MLA_SCALE),
                                 reads=[bname(sp_), bname(sp_ + 1)], writes=[pn])
                            for u, kb in enumerate((kb0, kb1)):
                                first_ = (pi == 0 and u == 0)
                                last_ = (pi == npairs - 1 and u == 1)
                                S.op("pe", lambda e, u=u, kb=kb, first_=first_, last_=last_: e.matmul(bank(ob)[:, 0:W], lhsT=Va[:, kb, :], rhs=pa[:, u, 0:W], start=first_, stop=last_),
                                     reads=[pn, Vn] + vparts, writes=[bname(ob)])
                            if pi == npairs - 1:
                                attn_finish(ob, W, None, 640 + hd * 64, t0, dnr, rcr, obr)

                        tasks.append((first, rest))
            run_pipeline(tasks, 2)

        def phase_C1(l):
            S.barrier()
            A.off = PERSIST
            last = l == DEPTH - 1
            Wo = A.alloc([8, D], BF16)
            mring = Ring(A, "mixT", 2, [8, 512], BF16)
            hring = Ring(A, "hC1", 8, [D], F32)
            tmpr = Ring(A, "tmpC1", 2, [512], F32)
            load_gate(l, 2)
            wv = w_out[l].rearrange("(k p) n -> p k n", p=128)
            for kc in range(0, 8, 2):
                S.op("pool", lambda e, kc=kc: e.dma_start(out=Wo[:, kc:kc + 2, :], in_=wv[:, kc:kc + 2, :]), writes=["Wo"], dma=True)
            src_lat = x if l == 0 else h_scr[0:S_LAT]
            src_ctx = ctx if l == 0 else h_scr[S_LAT:T]
            bl = blocks(512, not last)
            mv = mixT_scr.rearrange("(k p) t -> p k t", p=128)
            loaded = {}

            def load_block(bi):
                t0, W = bl[bi]
                ma, mn = mring.next()
                S.op("sp", lambda e, ma=ma: e.dma_start(out=ma[:, :, 0:W], in_=mv[:, :, t0:t0 + W]), reads=["mixT_scr"], writes=[mn], dma=True)
                tiles = []
                for t in range(W // 128):
                    ha, hn = hring.next()
                    tok = t0 + t * 128
                    src = src_lat[tok:tok + 128] if tok < S_LAT else src_ctx[tok - S_LAT:tok - S_LAT + 128]
                    S.op("sp", lambda e, ha=ha, src=src: e.dma_start(out=ha, in_=src), reads=["h_scr%d" % (tok // 128)], writes=[hn], dma=True)
                    tiles.append((ha, hn))
                loaded[bi] = (ma, mn, tiles)

            load_block(0)
            bi_ = [0]
            for bi, (t0, W) in enumerate(bl):
                if bi + 1 < len(bl):
                    load_block(bi + 1)
                ma, mn, tiles = loaded.pop(bi)
                j = 0 if t0 < S_LAT else 1
                for t in range(W // 128):
                    ha, hn = tiles[t]
                    for n in range(2):
                        b = bi_[0] % 4
                        bi_[0] += 1
                        for kc in range(8):
                            S.op("pe", lambda e, b=b, kc=kc, t=t, n=n, ma=ma: e.matmul(bank(b), lhsT=ma[:, kc, t * 128:(t + 1) * 128], rhs=Wo[:, kc, n * 512:(n + 1) * 512],
                                                                                    start=(kc == 0), stop=(kc == 7)), reads=[mn, "Wo"], writes=[bname(b)])
                        ta, tn = tmpr.next()
                        S.op("dve", lambda e, b=b, ta=ta, n=n: e.tensor_tensor(out=ta, in0=bank(b), in1=gbc[:, j, n * 512:(n + 1) * 512], op=ALU.mult),
                             reads=[bname(b), "gbc"], writes=[tn])
                        S.op("dve", lambda e, ta=ta, ha=ha, n=n: e.tensor_tensor(out=ha[:, n * 512:(n + 1) * 512], in0=ha[:, n * 512:(n + 1) * 512], in1=ta, op=ALU.add),
                             reads=[tn, hn], writes=[hn])
                    tok = t0 + t * 128
                    S.op("sp", lambda e, ha=ha, tok=tok: e.dma_start(out=h_scr[tok:tok + 128, :], in_=ha), reads=[hn], writes=["h_scr%d" % (tok // 128)], dma=True)

        def phase_C2(l):
            S.barrier()
            A.off = PERSIST
            last = l == DEPTH - 1
            W1 = A.alloc([8, DFF], BF16)
            W2 = A.alloc([32, D], BF16)
            hid = A.alloc([32, 256], BF16)
            hring = Ring(A, "hC2", 4, [D], F32)
            xhring = Ring(A, "xhC2", 3, [D], BF16)
            xmring = Ring(A, "xmC2", 2, [8, 256], BF16)
            ssb = A.alloc([12], F32)
            rr = Ring(A, "rr", 2, [256], F32)
            tmpr = Ring(A, "tmpC2", 2, [512], F32)
            fss = A.alloc([4], F32)
            load_gate(l, 5)
            wv1 = w_mlp1[l].rearrange("(k p) n -> p k n", p=128)
            wv2 = w_mlp2[l].rearrange("(k p) n -> p k n", p=128)
            for kc in range(8):
                S.op("pool", lambda e, kc=kc: e.dma_start(out=W1[:, kc, :], in_=wv1[:, kc, :]), writes=["W1_%d" % kc], dma=True)
            for kc in range(0, 32, 4):
                S.op("pool", lambda e, kc=kc: e.dma_start(out=W2[:, kc:kc + 4, :], in_=wv2[:, kc:kc + 4, :]), writes=["W2_%d" % (kc // 4)], dma=True)
            w1n = ["W1_%d" % k for k in range(8)]
            w2n = ["W2_%d" % k for k in range(8)]
            bl = blocks(256, not last)
            loaded = {}

            def load_block(bi):
                t0, W = bl[bi]
                tiles = []
                for t in range(W // 128):
                    ha, hn = hring.next()
                    tok = t0 + t * 128
                    S.op("sp", lambda e, ha=ha, tok=tok: e.dma_start(out=ha, in_=h_scr[tok:tok + 128, :]), reads=["h_scr%d" % (tok // 128)], writes=[hn], dma=True)
                    tiles.append((ha, hn))
                loaded[bi] = tiles

            load_block(0)
            ub = [0]
            ob = [0]
            xms = {}

            def do_norm(bi):
                t0, W = bl[bi]
                xm, xmn = xmring.next()
                norm_mod_T(loaded[bi], W // 128, 1, 0 if t0 < S_LAT else 1, xm, xmn, ssb, xhring, [0, 1])
                xms[bi] = (xm, xmn)

            do_norm(0)
            for bi, (t0, W) in enumerate(bl):
                if bi + 1 < len(bl):
                    load_block(bi + 1)
                tiles = loaded[bi]
                nt = W // 128
                j = 0 if t0 < S_LAT else 1
                xm, xmn = xms.pop(bi)
                for c in range(32):
                    b = 2 + ub[0] % 4
                    ub[0] += 1
                    for kc in range(8):
                        S.op("pe", lambda e, b=b, c=c, kc=kc: e.matmul(bank(b)[:, 0:W], lhsT=W1[:, kc, c * 128:(c + 1) * 128], rhs=xm[:, kc, 0:W], start=(kc == 0), stop=(kc == 7)),
                             reads=[xmn, w1n[kc]], writes=[bname(b)])
                    ra, rn_ = rr.next()
                    S.op("act", lambda e, b=b, ra=ra: e.activation(out=ra[:, 0:W], in_=bank(b)[:, 0:W], func=AF.Relu), reads=[bname(b)], writes=[rn_])
                    S.op("dve", lambda e, ra=ra, c=c: e.tensor_tensor(out=hid[:, c, 0:W], in0=ra[:, 0:W], in1=ra[:, 0:W], op=ALU.mult), reads=[rn_], writes=["hid%d" % c])
                hidn = ["hid%d" % c for c in range(32)]
                if bi + 1 < len(bl):
                    do_norm(bi + 1)
                loaded.pop(bi)
                for t in range(nt):
                    ha, hn = tiles[t]
                    for n in range(2):
                        b = 6 + ob[0] % 2
                        ob[0] += 1
                        for kc in range(32):
                            S.op("pe", lambda e, b=b, kc=kc, t=t, n=n: e.matmul(bank(b), lhsT=hid[:, kc, t * 128:(t + 1) * 128], rhs=W2[:, kc, n * 512:(n + 1) * 512],
                                                                             start=(kc == 0), stop=(kc == 31)), reads=[hidn[kc], w2n[kc // 4]], writes=[bname(b)])
                        ta, tn = tmpr.next()
                        S.op("dve", lambda e, b=b, ta=ta, n=n: e.tensor_tensor(out=ta, in0=bank(b), in1=gbc[:, j, n * 512:(n + 1) * 512], op=ALU.mult),
                             reads=[bname(b), "gbc"], writes=[tn])
                        S.op("dve", lambda e, ta=ta, ha=ha, n=n: e.tensor_tensor(out=ha[:, n * 512:(n + 1) * 512], in0=ha[:, n * 512:(n + 1) * 512], in1=ta, op=ALU.add),
                             reads=[tn, hn], writes=[hn])
                    tok = t0 + t * 128
                    if not last:
                        S.op("sp", lambda e, ha=ha, tok=tok: e.dma_start(out=h_scr[tok:tok + 128, :], in_=ha), reads=[hn], writes=["h_scr%d" % (tok // 128)], dma=True)
                    else:
                        oa, on = ha, hn
                        S.op("act", lambda e, ha=ha: e.activation(out=xhring.bufs[-1], in_=ha, func=AF.Square, accum_out=fss[:, 0:1]), reads=[hn], writes=["junk", "fss"])
                        S.op("act", lambda e: e.activation(out=fss[:, 1:2], in_=fss[:, 0:1], func=AF.Ln, scale=1.0 / D, bias=epst[:, 0:1]), reads=["fss", "eps"], writes=["fss"])
                        S.op("act", lambda e: e.activation(out=fss[:, 2:3], in_=fss[:, 1:2], func=AF.Exp, scale=-0.5), reads=["fss"], writes=["fss"])
                        S.op("dve", lambda e, ha=ha, oa=oa: e.scalar_tensor_tensor(out=oa, in0=ha, scalar=fss[:, 2:3], in1=fgbc, op0=ALU.mult, op1=ALU.mult),
                             reads=[hn, "fss", "fgbc"], writes=[hn])
                        S.op("sp", lambda e, oa=oa, tok=tok: e.dma_start(out=out[tok:tok + 128, :], in_=oa), reads=[on], dma=True, final=True)

        stop_after = [d for d in debug if d.startswith("stop:")]
        stop_after = stop_after[0][5:] if stop_after else None
        done = False
        for l in range(DEPTH):
            for nm, ph in (("mod", phase_mod), ("A", phase_A), ("F", phase_F), ("B1", phase_B1), ("B2", phase_B2), ("C1", phase_C1), ("C2", phase_C2)):
                ph(l)
                if stop_after == "%s%d" % (nm, l):
                    done = True
                    break
            if done:
                break
        if done:
            S.barrier()
            S.op("sp", lambda e: e.dma_start(out=out[0:128, :], in_=fgbc), dma=True, final=True)
        S.emit()
    return nc


def _host_inputs(inputs):
    f = lambda a: np.ascontiguousarray(np.asarray(a, dtype=np.float32))
    x = f(inputs["x"])
    c = f(inputs["c"])
    ctx = f(inputs["ctx"])
    c_ctx = f(inputs["c_ctx"])
    shared = {}
    shared["w_ada"] = f(inputs["w_ada"])
    shared["b_adaT"] = f(f(inputs["b_ada"]).reshape(DEPTH, 48, 128).transpose(0, 2, 1))
    shared["n1T"] = f(f(inputs["norm1_g"]).reshape(DEPTH, 8, 128).transpose(0, 2, 1))
    shared["n2T"] = f(f(inputs["norm2_g"]).reshape(DEPTH, 8, 128).transpose(0, 2, 1))
    shared["fg"] = f(inputs["final_norm_g"]).reshape(1, D)
    shared["w_in"] = f(f(inputs["w_in"])[:, :, _perm_in()])
    shared["w_f"] = f(inputs["w_fourier"])
    shared["sink"] = f(inputs["swa_sink"])
    shared["gqT"] = f(f(inputs["mla_q_norm"]).reshape(DEPTH, 2, 128).transpose(0, 2, 1))
    shared["gkvT"] = f(f(inputs["mla_kv_norm"]).reshape(DEPTH, 1, 128).transpose(0, 2, 1))
    shared["w_uq"] = f(inputs["w_uq"])
    shared["w_ukv"] = f(f(inputs["w_ukv"])[:, :, _perm_ukv()])
    shared["w_out"] = f(inputs["w_out"])
    shared["w_mlp1"] = f(inputs["w_mlp1"])
    shared["w_mlp2"] = f(inputs["w_mlp2"])
    shared.update(_consts())
    maps = []
    for b in range(8):
        m = dict(shared)
        m["x"] = x[b]
        m["ctx"] = ctx[b]
        cv = np.stack([c[b], c_ctx], 0)
        m["cT"] = f(cv.reshape(2, 8, 128).transpose(2, 1, 0))
        maps.append(m)
    return maps


_NC_CACHE = {}


def kernel(**inputs):
    maps = _host_inputs(inputs)
    if "nc" not in _NC_CACHE:
        _NC_CACHE["nc"] = build()
    res = run_bass_kernel_spmd(_NC_CACHE["nc"], maps, core_ids=list(range(8)))
    return np.stack([np.asarray(r["out"], dtype=np.float32) for r in res.results], 0)
```

```python
import numpy as np
import ml_dtypes
from contextlib import ExitStack
import concourse.bass as bass
import concourse.mybir as mybir
from concourse.bass_utils import run_bass_kernel_spmd

F32 = mybir.dt.float32
BF16 = mybir.dt.bfloat16
AF = mybir.ActivationFunctionType
ALU = mybir.AluOpType

D = 1024
S_LAT = 4096
L_CTX = 256
T = S_LAT + L_CTX
DEPTH = 2
DFF = 4096
D_IN = 1312
EPS = 1e-6
MLA_SCALE = 96.0 ** -0.5
SWA_SCALE = 0.125
C_F, C_Q, C_K, C_CQ, C_CKV, C_KR, C_V = 0, 256, 640, 768, 1024, 1152, 1184

ENGS = ["pe", "act", "dve", "pool", "sp"]
NDSEM = 8


class _Rec:
    def __getattr__(self, name):
        def f(*a, **k):
            self.call = (name, a, k)
            return self
        return f


class Sched:
    def __init__(self, nc):
        self.nc = nc
        self.ops = {e: [] for e in ENGS}
        self.res = {}
        self.known = {e: {} for e in ENGS}
        self.count = {e: 0 for e in ENGS}
        self.dma_n = {e: 0 for e in ENGS}
        self.pending = {e: {} for e in ENGS}
        self.out_dmas = []

    def _dep(self, eng, dep, waits):
        key, val = dep[:-1], dep[-1]
        if key == ("c", "pe") and eng == "pe":
            return
        if self.known[eng].get(key, 0) >= val:
            return
        self.known[eng][key] = val
        waits[key] = max(waits.get(key, 0), val)

    def op(self, eng, fn, reads=(), writes=(), dma=False, final=False):
        pb = [r for r in reads if r.startswith("pb")]
        if pb:
            reads = [r for r in reads if not r.startswith("pb")]
            writes = list(writes) + pb
        waits = {}
        for key, val in self.pending[eng].items():
            self._dep(eng, key + (val,), waits)
        self.pending[eng] = {}
        for r in reads:
            st = self.res.get(r)
            if st and st["w"]:
                self._dep(eng, st["w"], waits)
        for r in writes:
            st = self.res.get(r)
            if st:
                if st["w"]:
                    self._dep(eng, st["w"], waits)
                for d in st["r"]:
                    self._dep(eng, d, waits)
        if dma:
            n = self.dma_n[eng]
            slot = n % NDSEM
            val = 16 * (n // NDSEM + 1)
            self.dma_n[eng] += 1
            if n >= NDSEM:
                self._dep(eng, ("d", eng, slot, val - 16), waits)
            me = ("d", eng, slot, val)
            if final:
                self.out_dmas.append(me)
        else:
            self.count[eng] += 1
            me = ("c", eng, self.count[eng])
        rec = _Rec()
        fn(rec)
        self.ops[eng].append((rec.call, waits, me))
        for r in reads:
            self.res.setdefault(r, {"w": None, "r": []})["r"].append(me)
        for r in writes:
            self.res[r] = {"w": me, "r": []}
        return me

    def barrier(self):
        deps = {}
        for e in ENGS:
            if self.count[e] > 0:
                deps[("c", e)] = self.count[e]
            n = self.dma_n[e]
            for slot in range(min(n, NDSEM)):
                last = n - 1 - ((n - 1 - slot) % NDSEM)
                deps[("d", e, slot)] = 16 * (last // NDSEM + 1)
        for e in ENGS:
            for k, v in deps.items():
                self.pending[e][k] = max(self.pending[e].get(k, 0), v)
        self.res = {}

    def emit(self):
        nc = self.nc
        with ExitStack() as st:
            csem = {e: st.enter_context(nc.semaphore("c_" + e)) for e in ENGS}
            dsem = {(e, s): st.enter_context(nc.semaphore("d_%s%d" % (e, s)))
                    for e in ("sp", "pool", "act") for s in range(NDSEM)}
            block = st.enter_context(nc.Block())

            def semof(key):
                return csem[key[1]] if key[0] == "c" else dsem[(key[1], key[2])]

            def run(engname, eng):
                for fn, waits, me in self.ops[engname]:
                    for key, val in waits.items():
                        eng.wait_ge(semof(key), val)
                    ins = getattr(eng, fn[0])(*fn[1], **fn[2])
                    if me[0] == "c":
                        ins.then_inc(csem[engname], 1)
                    else:
                        ins.then_inc(dsem[(engname, me[2])], 16)
                if engname == "sp":
                    for me in self.out_dmas:
                        eng.wait_ge(dsem[(me[1], me[2])], me[3])

            @block.tensor
            def _(e):
                run("pe", e)

            @block.scalar
            def _(e):
                run("act", e)

            @block.vector
            def _(e):
                run("dve", e)

            @block.gpsimd
            def _(e):
                run("pool", e)

            @block.sync
            def _(e):
                run("sp", e)


class Arena:
    def __init__(self, ap_f32, nwords):
        self.t = ap_f32
        self.n = nwords
        self.off = 0

    def alloc(self, shape, dt):
        n = int(np.prod(shape))
        words = n if dt == F32 else (n + 1) // 2
        words = (words + 7) // 8 * 8
        assert self.off + words <= self.n, ("arena overflow", self.off, words, self.n)
        ap = self.t[:, self.off:self.off + words]
        self.off += words
        if dt != F32:
            ap = ap.bitcast(dt)
        ap = ap[:, 0:n]
        if len(shape) == 2:
            return ap.rearrange("p (a b) -> p a b", a=shape[0])
        if len(shape) == 3:
            return ap.rearrange("p (a b c) -> p a b c", a=shape[0], b=shape[1])
        return ap


class Ring:
    def __init__(self, arena, name, n, shape, dt):
        self.bufs = [arena.alloc(shape, dt) for _ in range(n)]
        self.names = ["%s#%d" % (name, i) for i in range(n)]
        self.i = 0

    def next(self):
        k = self.i % len(self.bufs)
        self.i += 1
        return self.bufs[k], self.names[k]


def _rope_tabs(dim):
    rows = S_LAT // 64
    r, col = np.meshgrid(np.arange(rows, dtype=np.float32), np.arange(64, dtype=np.float32), indexing="ij")
    r = r.reshape(-1)
    col = col.reshape(-1)
    nf = dim // 4
    inv = (np.float32(10000.0) ** (-np.arange(nf, dtype=np.float32) / np.float32(nf))).astype(np.float32)
    ang = np.concatenate([r[:, None] * inv[None, :], col[:, None] * inv[None, :]], axis=-1).astype(np.float32)
    c = np.cos(ang).astype(np.float32)
    s = np.sin(ang).astype(np.float32)
    half = dim // 2
    C2 = np.ones((dim, T), np.float32)
    S2 = np.zeros((dim, T), np.float32)
    C2[:half, :S_LAT] = c.T
    C2[half:, :S_LAT] = c.T
    S2[:half, :S_LAT] = -s.T
    S2[half:, :S_LAT] = s.T
    return C2, S2


def _consts():
    k = {}
    C2, S2 = _rope_tabs(64)
    ropeS = np.zeros((2, 128, T), np.float32)
    ropeS[0] = np.concatenate([C2, C2], 0)
    ropeS[1] = np.concatenate([S2, S2], 0)
    C2m, S2m = _rope_tabs(32)
    ropeM = np.zeros((2, 128, T), np.float32)
    ropeM[0] = 1.0
    for base in (0, 64):
        ropeM[0, base:base + 32] = C2m
        ropeM[1, base:base + 32] = S2m
    k["ropeS"] = ropeS
    k["ropeM"] = ropeM
    cm = np.zeros((5, 128, 128), np.float32)
    cm[0] = np.eye(128)
    cm[1] = 1.0
    for b0 in (0, 64):
        for d in range(32):
            cm[2, b0 + d, b0 + d + 32] = 1.0
            cm[2, b0 + d + 32, b0 + d] = 1.0
    for b0 in (0, 64):
        for d in range(16):
            cm[3, b0 + d, b0 + d + 16] = 1.0
            cm[3, b0 + d + 16, b0 + d] = 1.0
    a = np.arange(64)
    ang = 2 * np.pi * np.outer(a, a) / 64.0
    Cr, Sr = np.cos(ang) / 8.0, np.sin(ang) / 8.0
    cm[4, 0:64, 0:64] = Cr
    cm[4, 64:128, 0:64] = Sr
    cm[4, 0:64, 64:128] = -Sr
    cm[4, 64:128, 64:128] = Cr
    k["cmat"] = np.ascontiguousarray(cm.transpose(1, 0, 2))
    cf = np.zeros((3, 128, 128), np.float32)
    cf[0] = np.eye(128)
    for g in range(2):
        cf[1, g * 64:(g + 1) * 64, g * 64:(g + 1) * 64] = Cr
        cf[2, g * 64:(g + 1) * 64, g * 64:(g + 1) * 64] = -Sr
    k["cmatf"] = np.ascontiguousarray(cf.transpose(1, 0, 2))
    c = np.arange(64)[:, None, None]
    ka = np.arange(64)[None, :, None]
    kb = np.arange(64)[None, None, :]
    th = 2 * np.pi * c * (64 * kb + ka) / 4096.0
    m2 = np.zeros((128, 64, 64), np.float32)
    m2[0:64] = np.cos(th) / 8.0
    m2[64:128] = np.sin(th) / 8.0
    k["m2"] = m2
    tok = (np.arange(2)[None, :, None] * 128 + np.arange(128)[:, None, None]).astype(np.float64)
    kk = np.arange(256)[None, None, :]
    th = 2 * np.pi * tok * kk / 256.0
    d256 = np.zeros((128, 2, 2, 256), np.float32)
    d256[:, :, 0, :] = np.cos(th) / 16.0
    d256[:, :, 1, :] = np.sin(th) / 16.0
    k["d256"] = d256
    j = np.arange(128)[:, None, None]
    off = np.arange(-1, 5)[None, :, None]
    i = np.arange(512)[None, None, :]
    k["masks"] = (np.abs(128 * off + j - i) <= 128).astype(np.float32)
    return k


def _perm_in():
    idx = list(range(0, 256))
    for c in range(3):
        for h in (c, c + 3):
            idx += list(range(256 + h * 64, 256 + (h + 1) * 64))
    idx += list(range(640, 768))
    idx += list(range(896, 1152))
    idx += list(range(1152, 1280))
    idx += list(range(1280, 1312))
    idx += list(range(768, 896))
    return np.array(idx)


def _perm_ukv():
    kn, v = [], []
    for h in range(6):
        kn += list(range(h * 128, h * 128 + 64))
        v += list(range(h * 128 + 64, h * 128 + 128))
    return np.array(kn + v)


def build(debug=()):
    nc = bass.Bass("TRN2", target_bir_lowering=False)

    def din(name, shape, dt=F32):
        return nc.dram_tensor(name, list(shape), dt, kind="ExternalInput").ap()

    def dscr(name, shape, dt):
        kind = "ExternalOutput" if name in debug else "Internal"
        return nc.dram_tensor(name, list(shape), dt, kind=kind).ap()

    x = din("x", [S_LAT, D])
    ctx = din("ctx", [L_CTX, D])
    cT = din("cT", [128, 8, 2])
    w_ada = din("w_ada", [DEPTH, D, 6 * D])
    b_adaT = din("b_adaT", [DEPTH, 128, 48])
    n1T = din("n1T", [DEPTH, 128, 8])
    n2T = din("n2T", [DEPTH, 128, 8])
    fg = din("fg", [1, D])
    w_in = din("w_in", [DEPTH, D, D_IN])
    w_f = din("w_f", [DEPTH, 4, 64, 64])
    sink = din("sink", [DEPTH, 6])
    gqT = din("gqT", [DEPTH, 128, 2])
    gkvT = din("gkvT", [DEPTH, 128, 1])
    w_uq = din("w_uq", [DEPTH, 256, 576])
    w_ukv = din("w_ukv", [DEPTH, 128, 768])
    w_out = din("w_out", [DEPTH, D, D])
    w_mlp1 = din("w_mlp1", [DEPTH, D, DFF])
    w_mlp2 = din("w_mlp2", [DEPTH, DFF, D])
    ropeS = din("ropeS", [2, 128, T])
    ropeM = din("ropeM", [2, 128, T])
    cmat = din("cmat", [128, 5, 128])
    cmatf = din("cmatf", [128, 3, 128])
    m2d = din("m2", [128, 64, 64])
    d256d = din("d256", [128, 2, 2, 256])
    masksd = din("masks", [128, 6, 512])
    out = nc.dram_tensor("out", [S_LAT, D], F32, kind="ExternalOutput").ap()

    h_scr = dscr("h_scr", [T, D], F32)
    mod_scr = dscr("mod_scr", [DEPTH, 2, 6 * D], F32)
    y_scr = dscr("y_scr", [T, 512], BF16)
    z_scr = dscr("z_scr", [128, 64, 256], BF16)
    qs_scr = dscr("qs_scr", [3, 128, T], BF16)
    ks_scr = dscr("ks_scr", [128, T], BF16)
    vs_scr = dscr("vs_scr", [T, 128], BF16)
    qm_scr = dscr("qm_scr", [6, 96, T], BF16)
    kn_scr = dscr("kn_scr", [384, T], BF16)
    kr_scr = dscr("kr_scr", [32, T], BF16)
    vm_scr = dscr("vm_scr", [T, 384], BF16)
    mixT_scr = dscr("mixT_scr", [D, T], BF16)

    S = Sched(nc)
    SBW = 52736
    with ExitStack() as st:
        arena_t = st.enter_context(nc.sbuf_tensor("arena", [128, SBW], F32))
        pall = st.enter_context(nc.psum_tensor("pall", [128, 4096], F32))

        def bank(b, w=512):
            return pall[:, b * 512:b * 512 + w]

        def bname(b):
            return "pb%d" % b

        A = Arena(arena_t, SBW)
        cm = A.alloc([5, 128], BF16)
        ident, ones, PmS, PmM, D1 = (cm[:, i, :] for i in range(5))
        cmf = A.alloc([3, 128], F32)
        identf, CCre, CCim = (cmf[:, i, :] for i in range(3))
        epst = A.alloc([1], F32)
        modT = A.alloc([96], F32)
        amod = A.alloc([2, 2, 8], F32)
        smod = A.alloc([2, 2, 8], F32)
        nT = A.alloc([2, 8], F32)
        gq = A.alloc([2], F32)
        gkv = A.alloc([1], F32)
        exps = A.alloc([6], F32)
        gbc = A.alloc([2, D], F32)
        fgbc = A.alloc([D], F32)
        PERSIST = A.off

        S.op("pool", lambda e: e.dma_start(out=cm, in_=cmat), writes=["cm"], dma=True)
        S.op("sp", lambda e: e.dma_start(out=cmf, in_=cmatf), writes=["cmf"], dma=True)
        S.op("sp", lambda e: e.dma_start(out=fgbc, in_=fg.partition_broadcast(128)), writes=["fgbc"], dma=True)
        S.op("dve", lambda e: e.memset(epst, EPS), writes=["eps"])

        def blocks(width, with_ctx):
            bl = [(t0, width) for t0 in range(0, S_LAT, width)]
            if with_ctx:
                bl += [(t0, min(width, L_CTX)) for t0 in range(S_LAT, T, min(width, L_CTX))]
            return bl

        def phase_mod(l):
            S.barrier()
            A.off = PERSIST
            cTs = A.alloc([8, 2], F32)
            sil = A.alloc([8, 2], BF16)
            bT = A.alloc([48], F32)
            mrow = A.alloc([128], F32)
            wring = Ring(A, "wada", 2, [8, 1024], BF16)
            S.op("sp", lambda e: e.dma_start(out=cTs, in_=cT), writes=["cTs"], dma=True)
            S.op("sp", lambda e: e.dma_start(out=bT, in_=b_adaT[l]), writes=["bT"], dma=True)
            S.op("sp", lambda e: e.dma_start(out=nT[:, 0, :], in_=n1T[l]), writes=["nT"], dma=True)
            S.op("sp", lambda e: e.dma_start(out=nT[:, 1, :], in_=n2T[l]), writes=["nT"], dma=True)
            S.op("sp", lambda e: e.dma_start(out=gq, in_=gqT[l]), writes=["gq"], dma=True)
            S.op("sp", lambda e: e.dma_start(out=gkv, in_=gkvT[l]), writes=["gkv"], dma=True)
            S.op("sp", lambda e: e.dma_start(out=exps, in_=sink[l:l + 1, :].partition_broadcast(128)), writes=["exps"], dma=True)
            S.op("act", lambda e: e.activation(out=sil, in_=cTs, func=AF.Silu), reads=["cTs"], writes=["sil"])
            S.op("act", lambda e: e.activation(out=exps, in_=exps, func=AF.Exp), reads=["exps"], writes=["exps"])
            wv = w_ada[l].rearrange("(k p) n -> p k n", p=128)
            pb = bank(0, 96)
            for nb in range(6):
                wb, wn = wring.next()
                for kc in range(0, 8, 4):
                    S.op("pool", lambda e, wb=wb, nb=nb, kc=kc: e.dma_start(out=wb[:, kc:kc + 4, :], in_=wv[:, kc:kc + 4, nb * 1024:(nb + 1) * 1024]),
                         writes=[wn + "_%d" % (kc // 4)], dma=True)
                for cc in range(8):
                    chunk = nb * 8 + cc
                    for kc in range(8):
                        S.op("pe", lambda e, wb=wb, cc=cc, kc=kc, chunk=chunk: e.matmul(
                            pb[:, chunk * 2:chunk * 2 + 2], lhsT=wb[:, kc, cc * 128:(cc + 1) * 128], rhs=sil[:, kc, :],
                            start=(kc == 0), stop=(kc == 7)), reads=[wn + "_%d" % (kc // 4), "sil"], writes=[bname(0)])
            pbv = pb.rearrange("p (c j) -> p j c", j=2)
            for j in range(2):
                S.op("dve", lambda e, j=j: e.tensor_tensor(out=modT[:, j * 48:(j + 1) * 48], in0=pbv[:, j, :], in1=bT, op=ALU.add),
                     reads=[bname(0), "bT"], writes=["modT"])
            for ni, (s_sh, s_sc) in enumerate(((0, 1), (3, 4))):
                for j in range(2):
                    sc = modT[:, j * 48 + s_sc * 8: j * 48 + s_sc * 8 + 8]
                    shv = modT[:, j * 48 + s_sh * 8: j * 48 + s_sh * 8 + 8]
                    S.op("dve", lambda e, ni=ni, j=j, sc=sc: e.scalar_tensor_tensor(
                        out=amod[:, ni, j, :], in0=sc, scalar=1.0, in1=nT[:, ni, :], op0=ALU.add, op1=ALU.mult),
                        reads=["modT", "nT"], writes=["amod"])
                    S.op("dve", lambda e, ni=ni, j=j, shv=shv: e.tensor_copy(out=smod[:, ni, j, :], in_=shv),
                         reads=["modT"], writes=["smod"])
            S.op("pe", lambda e: e.transpose(out=bank(1, 128)[0:96, :], in_=modT, identity=identf), reads=["modT", "cmf"], writes=[bname(1)])
            S.op("dve", lambda e: e.tensor_copy(out=mrow[0:96, :], in_=bank(1, 128)[0:96, :]), reads=[bname(1)], writes=["mrow"])
            S.op("sp", lambda e: e.dma_start(out=mod_scr[l].rearrange("j (c p) -> (j c) p", p=128), in_=mrow[0:96, :]),
                 reads=["mrow"], writes=["mod_scr"], dma=True)

        def load_gate(l, sec):
            for j in range(2):
                S.op("sp", lambda e, j=j: e.dma_start(out=gbc[:, j, :], in_=mod_scr[l, j:j + 1, sec * D:(sec + 1) * D].partition_broadcast(128)),
                     reads=["mod_scr"], writes=["gbc"], dma=True)

        def norm_mod_T(htiles, nt, ni, j, xmT, xmn, ssb, xhring, tpbanks):
            junk = xhring.bufs[-1]
            for t in range(nt):
                ha, hn = htiles[t]
                S.op("act", lambda e, ha=ha, t=t: e.activation(out=junk, in_=ha, func=AF.Square, accum_out=ssb[:, t:t + 1]),
                     reads=[hn], writes=["junk", "ssb"])
            S.op("act", lambda e: e.activation(out=ssb[:, 4:4 + nt], in_=ssb[:, 0:nt], func=AF.Ln, scale=1.0 / D, bias=epst[:, 0:1]),
                 reads=["ssb", "eps"], writes=["ssb"])
            S.op("act", lambda e: e.activation(out=ssb[:, 8:8 + nt], in_=ssb[:, 4:4 + nt], func=AF.Exp, scale=-0.5),
                 reads=["ssb"], writes=["ssb"])
            for t in range(nt):
                ha, hn = htiles[t]
                xh = xhring.bufs[t % (len(xhring.bufs) - 1)]
                xn = xhring.names[t % (len(xhring.bufs) - 1)]
                S.op("act", lambda e, ha=ha, xh=xh, t=t: e.activation(out=xh, in_=ha, func=AF.Copy, scale=ssb[:, 8 + t:9 + t]),
                     reads=[hn, "ssb"], writes=[xn])
                tb = tpbanks[t % len(tpbanks)]
                tp = bank(tb).bitcast(BF16)
                for kc in range(8):
                    S.op("pe", lambda e, xh=xh, kc=kc, tp=tp: e.transpose(out=tp[:, kc * 128:(kc + 1) * 128], in_=xh[:, kc * 128:(kc + 1) * 128], identity=ident),
                         reads=[xn, "cm"], writes=[bname(tb)])
                tp3 = tp.rearrange("p (k c) -> p k c", k=8)
                dst = xmT[:, :, t * 128:(t + 1) * 128]
                S.op("dve", lambda e, tp3=tp3, dst=dst: e.tensor_tensor(out=dst, in0=tp3, in1=amod[:, ni, j, :].unsqueeze(2).to_broadcast([128, 8, 128]), op=ALU.mult),
                     reads=[bname(tb), "amod"], writes=[xmn])
                S.op("dve", lambda e, dst=dst: e.tensor_tensor(out=dst, in0=dst, in1=smod[:, ni, j, :].unsqueeze(2).to_broadcast([128, 8, 128]), op=ALU.add),
                     reads=[xmn, "smod"], writes=[xmn])

        def phase_A(l):
            S.barrier()
            A.off = PERSIST
            last = l == DEPTH - 1
            Win = A.alloc([8, D_IN], BF16)
            wuq = A.alloc([2, 576], BF16)
            wukv = A.alloc([768], BF16)
            wfbd = A.alloc([2, 128], F32)
            Abd = A.alloc([2, 256], BF16)
            hring = Ring(A, "hA", 8, [D], F32)
            xhring = Ring(A, "xhA", 3, [D], BF16)
            xmring = Ring(A, "xmA", 2, [8, 512], BF16)
            ssb = A.alloc([12], F32)
            rtab = Ring(A, "rtab", 2, [4, 512], F32)
            fT = A.alloc([2, 512], BF16)
            ysb = Ring(A, "ysb", 2, [512], BF16)
            qb = Ring(A, "qb", 3, [512], BF16)
            t1r = Ring(A, "t1", 3, [512], F32)
            t2r = Ring(A, "t2", 3, [512], F32)
            qo = Ring(A, "qo", 4, [512], BF16)
            sq = A.alloc([2, 512], BF16)
            lnv = A.alloc([512], F32)
            rbc = A.alloc([512], F32)
            cqn = A.alloc([2, 512], BF16)
            ckvn = A.alloc([512], BF16)
            knb = Ring(A, "knb", 2, [512], BF16)
            vtb = Ring(A, "vtb", 2, [384], BF16)

            wv = w_in[l].rearrange("(k p) n -> p k n", p=128)
            for kc in range(0, 8, 2):
                S.op("pool", lambda e, kc=kc: e.dma_start(out=Win[:, kc:kc + 2, :], in_=wv[:, kc:kc + 2, :]), writes=["Win"], dma=True)
            S.op("pool", lambda e: e.dma_start(out=wuq, in_=w_uq[l].rearrange("(k p) n -> p k n", p=128)), writes=["wuq"], dma=True)
            S.op("pool", lambda e: e.dma_start(out=wukv, in_=w_ukv[l]), writes=["wukv"], dma=True)
            S.op("dve", lambda e: e.memset(wfbd, 0.0), writes=["wfbd"])
            for g in range(4):
                jc, gl = g // 2, g % 2
                S.op("sp", lambda e, g=g, jc=jc, gl=gl: e.dma_start(out=wfbd[gl * 64:(gl + 1) * 64, jc, gl * 64:(gl + 1) * 64], in_=w_f[l, g]),
                     reads=["wfbd"], writes=["wfbd%d" % g], dma=True)
            for jc in range(2):
                for ri, Cm in enumerate((CCre, CCim)):
                    S.op("pe", lambda e, jc=jc, ri=ri, Cm=Cm: e.matmul(bank(4, 256)[:, ri * 128:(ri + 1) * 128], lhsT=Cm, rhs=wfbd[:, jc, :], start=True, stop=True),
                         reads=["cmf", "wfbd", "wfbd0", "wfbd1", "wfbd2", "wfbd3"], writes=[bname(4)])
                S.op("dve", lambda e, jc=jc: e.tensor_copy(out=Abd[:, jc, :], in_=bank(4, 256)), reads=[bname(4)], writes=["Abd"])

            src_lat = x if l == 0 else h_scr[0:S_LAT]
            src_ctx = ctx if l == 0 else h_scr[S_LAT:T]
            bl = blocks(512, True)
            hloaded = {}

            def load_block(bi):
                t0, W = bl[bi]
                tiles = []
                for t in range(W // 128):
                    ha, hn = hring.next()
                    tok = t0 + t * 128
                    src = src_lat[tok:tok + 128] if tok < S_LAT else src_ctx[tok - S_LAT:tok - S_LAT + 128]
                    S.op("sp", lambda e, ha=ha, src=src: e.dma_start(out=ha, in_=src), reads=["h_scr%d" % (tok // 128)], writes=[hn], dma=True)
                    tiles.append((ha, hn))
                rt, rn = rtab.next()
                S.op("sp", lambda e, rt=rt, t0=t0, W=W: e.dma_start(out=rt[:, 0:2, 0:W], in_=ropeS[:, :, t0:t0 + W].rearrange("a p w -> p a w")), writes=[rn], dma=True)
                S.op("sp", lambda e, rt=rt, t0=t0, W=W: e.dma_start(out=rt[:, 2:4, 0:W], in_=ropeM[:, :, t0:t0 + W].rearrange("a p w -> p a w")), writes=[rn + "m"], dma=True)
                hloaded[bi] = (tiles, rt, rn)

            auxc = [0]

            def rope(psb, rows, W, Ct, St, Pm, outap, outn, tabn):
                r0, r1 = rows
                q_b, q_n = qb.next()
                t1, t1n = t1r.next()
                t2, t2n = t2r.next()
                ps = bank(psb)
                ab = (1, 4)[auxc[0] % 2]
                auxc[0] += 1
                S.op("act", lambda e: e.activation(out=q_b[r0:r1, 0:W], in_=ps[r0:r1, 0:W], func=AF.Copy), reads=[bname(psb)], writes=[q_n])
                S.op("pe", lambda e: e.matmul(bank(ab)[r0:r1, 0:W], lhsT=Pm[r0:r1, r0:r1], rhs=q_b[r0:r1, 0:W], start=True, stop=True),
                     reads=[q_n, "cm"], writes=[bname(ab)])
                S.op("dve", lambda e: e.tensor_tensor(out=t1[r0:r1, 0:W], in0=ps[r0:r1, 0:W], in1=Ct[r0:r1, 0:W], op=ALU.mult),
                     reads=[bname(psb), tabn], writes=[t1n])
                S.op("dve", lambda e: e.tensor_tensor(out=t2[r0:r1, 0:W], in0=bank(ab)[r0:r1, 0:W], in1=St[r0:r1, 0:W], op=ALU.mult),
                     reads=[bname(ab), tabn], writes=[t2n])
                S.op("dve", lambda e: e.tensor_tensor(out=outap[r0:r1, 0:W], in0=t1[r0:r1, 0:W], in1=t2[r0:r1, 0:W], op=ALU.add),
                     reads=[t1n, t2n], writes=[outn])

            def rms_bc(psbanks, W, nfeat):
                n = len(psbanks)
                for jj, b in enumerate(psbanks):
                    S.op("act", lambda e, jj=jj, b=b: e.activation(out=sq[:, jj, 0:W], in_=bank(b)[:, 0:W], func=AF.Square), reads=[bname(b)], writes=["sq%d" % jj])
                ab = (1, 4)[auxc[0] % 2]
                auxc[0] += 1
                for jj in range(n):
                    S.op("pe", lambda e, jj=jj: e.matmul(bank(ab)[:, 0:W], lhsT=ones, rhs=sq[:, jj, 0:W], start=(jj == 0), stop=(jj == n - 1)),
                         reads=["sq%d" % jj, "cm"], writes=[bname(ab)])
                S.op("act", lambda e: e.activation(out=lnv[:, 0:W], in_=bank(ab)[:, 0:W], func=AF.Ln, scale=1.0 / nfeat, bias=epst[:, 0:1]),
                     reads=[bname(ab), "eps"], writes=["lnv"])
                S.op("act", lambda e: e.activation(out=rbc[:, 0:W], in_=lnv[:, 0:W], func=AF.Exp, scale=-0.5), reads=["lnv"], writes=["rbc"])

            ubanks = [2, 3, 5]
            ucnt = [0]

            def inproj(xm, xmn, col0, M, W, b=None):
                if b is None:
                    b = ubanks[ucnt[0] % 3]
                    ucnt[0] += 1
                for kc in range(8):
                    S.op("pe", lambda e, kc=kc: e.matmul(bank(b)[0:M, 0:W], lhsT=Win[:, kc, col0:col0 + M], rhs=xm[:, kc, 0:W], start=(kc == 0), stop=(kc == 7)),
                         reads=["Win", xmn], writes=[bname(b)])
                return b

            tokb = [6, 7]
            tokc = [0]

            def tokbank():
                b = tokb[tokc[0] % 2]
                tokc[0] += 1
                return b

            xms = {}

            def do_norm(bi):
                t0, W = bl[bi]
                tiles, rt, rn = hloaded[bi]
                xm, xmn = xmring.next()
                norm_mod_T(tiles, W // 128, 0, 0 if t0 < S_LAT else 1, xm, xmn, ssb, xhring, [0])
                xms[bi] = (xm, xmn)

            def make_tasks(bi):
                t0, W = bl[bi]
                tiles, rt, rn = hloaded.pop(bi)
                nt = W // 128
                xm, xmn = xms.pop(bi)
                CS, SS, CM, SM = rt[:, 0, :], rt[:, 1, :], rt[:, 2, :], rt[:, 3, :]
                tasks = []
                bk = {}

                def mk_first(key, cols):
                    def first():
                        bk[key] = [inproj(xm, xmn, c0, M, W) for (c0, M) in cols]
                    return first

                def rest_f(jc):
                    def rest():
                        b = bk[("f", jc)][0]
                        S.op("act", lambda e: e.activation(out=fT[:, jc, 0:W], in_=bank(b)[:, 0:W], func=AF.Copy), reads=[bname(b)], writes=["fT%d" % jc])
                        if jc == 1:
                            for t in range(nt):
                                tb = tokbank()
                                for jj in range(2):
                                    for ri in range(2):
                                        o = bank(tb)[:, ri * 256 + jj * 128: ri * 256 + (jj + 1) * 128]
                                        S.op("pe", lambda e, o=o, jj=jj, t=t, ri=ri: e.matmul(o, lhsT=fT[:, jj, t * 128:(t + 1) * 128], rhs=Abd[:, jj, ri * 128:(ri + 1) * 128],
                                                                                           start=True, stop=True), reads=["fT%d" % jj, "Abd"], writes=[bname(tb)])
                                yb, yn = ysb.next()
                                S.op("act", lambda e, yb=yb, tb=tb: e.activation(out=yb, in_=bank(tb), func=AF.Copy), reads=[bname(tb)], writes=[yn])
                                S.op("sp", lambda e, yb=yb, t=t: e.dma_start(out=y_scr[t0 + t * 128:t0 + (t + 1) * 128, :], in_=yb), reads=[yn], dma=True)
                    return rest
                for jc in range(2):
                    tasks.append((mk_first(("f", jc), [(C_F + jc * 128, 128)]), rest_f(jc)))

                def rest_q(c):
                    def rest():
                        b = bk[("q", c)][0]
                        oa, on = qo.next()
                        rope(b, (0, 128), W, CS, SS, PmS, oa, on, rn)
                        dst = qs_scr[c, :, t0:t0 + W] if c < 3 else ks_scr[:, t0:t0 + W]
                        S.op("sp", lambda e: e.dma_start(out=dst, in_=oa[:, 0:W]), reads=[on], dma=True)
                    return rest
                for c in range(4):
                    tasks.append((mk_first(("q", c), [(C_Q + c * 128, 128)]), rest_q(c)))
                    if c == 1 and bi + 1 < len(bl):
                        tasks.append((lambda: None, lambda: do_norm(bi + 1)))

                def rest_cq():
                    b0, b1 = bk["cq"]
                    rms_bc([b0, b1], W, 256)
                    for jj, b in enumerate((b0, b1)):
                        S.op("dve", lambda e, jj=jj, b=b: e.scalar_tensor_tensor(out=cqn[:, jj, 0:W], in0=bank(b)[:, 0:W], scalar=gq[:, jj:jj + 1], in1=rbc[:, 0:W],
                                                                                 op0=ALU.mult, op1=ALU.mult), reads=[bname(b), "gq", "rbc"], writes=["cqn"])
                    for hd in range(6):
                        ub_ = tokbank()
                        for jj in range(2):
                            S.op("pe", lambda e, hd=hd, jj=jj, ub_=ub_: e.matmul(bank(ub_)[0:96, 0:W], lhsT=wuq[:, jj, hd * 96:(hd + 1) * 96], rhs=cqn[:, jj, 0:W], start=(jj == 0), stop=(jj == 1)),
                                 reads=["wuq", "cqn"], writes=[bname(ub_)])
                        oa, on = qo.next()
                        S.op("act", lambda e, oa=oa, ub_=ub_: e.activation(out=oa[0:64, 0:W], in_=bank(ub_)[0:64, 0:W], func=AF.Copy), reads=[bname(ub_)], writes=[on])
                        rope(ub_, (64, 96), W, CM, SM, PmM, oa, on, rn + "m")
                        S.op("sp", lambda e, oa=oa, hd=hd: e.dma_start(out=qm_scr[hd, :, t0:t0 + W], in_=oa[0:96, 0:W]), reads=[on], dma=True)
                tasks.append((mk_first("cq", [(C_CQ, 128), (C_CQ + 128, 128)]), rest_cq))

                def rest_ckv():
                    b0 = bk["ckv"][0]
                    rms_bc([b0], W, 128)
                    S.op("dve", lambda e: e.scalar_tensor_tensor(out=ckvn[:, 0:W], in0=bank(b0)[:, 0:W], scalar=gkv[:, 0:1], in1=rbc[:, 0:W], op0=ALU.mult, op1=ALU.mult),
                         reads=[bname(b0), "gkv", "rbc"], writes=["ckvn"])
                    for pr in range(3):
                        ub_ = tokbank()
                        S.op("pe", lambda e, pr=pr, ub_=ub_: e.matmul(bank(ub_)[:, 0:W], lhsT=wukv[:, pr * 128:(pr + 1) * 128], rhs=ckvn[:, 0:W], start=True, stop=True),
                             reads=["wukv", "ckvn"], writes=[bname(ub_)])
                        ka, kn_ = knb.next()
                        S.op("act", lambda e, ka=ka, ub_=ub_: e.activation(out=ka[:, 0:W], in_=bank(ub_)[:, 0:W], func=AF.Copy), reads=[bname(ub_)], writes=[kn_])
                        S.op("sp", lambda e, ka=ka, pr=pr: e.dma_start(out=kn_scr[pr * 128:(pr + 1) * 128, t0:t0 + W], in_=ka[:, 0:W]), reads=[kn_], dma=True)
                    for t in range(nt):
                        tb = tokbank()
                        S.op("pe", lambda e, tb=tb, t=t: e.matmul(bank(tb)[:, 0:384], lhsT=ckvn[:, t * 128:(t + 1) * 128], rhs=wukv[:, 384:768], start=True, stop=True),
                             reads=["wukv", "ckvn"], writes=[bname(tb)])
                        va, vn = vtb.next()
                        S.op("dve", lambda e, va=va, tb=tb: e.tensor_copy(out=va, in_=bank(tb)[:, 0:384]), reads=[bname(tb)], writes=[vn])
                        S.op("sp", lambda e, va=va, t=t: e.dma_start(out=vm_scr[t0 + t * 128:t0 + (t + 1) * 128, :], in_=va), reads=[vn], dma=True)
                tasks.append((mk_first("ckv", [(C_CKV, 128)]), rest_ckv))

                def rest_kr():
                    b = bk["kr"][0]
                    oa, on = qo.next()
                    rope(b, (0, 32), W, CM, SM, PmM, oa, on, rn + "m")
                    S.op("sp", lambda e: e.dma_start(out=kr_scr[:, t0:t0 + W], in_=oa[0:32, 0:W]), reads=[on], dma=True)
                tasks.append((mk_first("kr", [(C_KR, 32)]), rest_kr))

                def rest_v():
                    for t in range(nt):
                        tb = tokbank()
                        for kc in range(8):
                            S.op("pe", lambda e, tb=tb, t=t, kc=kc: e.matmul(bank(tb)[:, 0:128], lhsT=xm[:, kc, t * 128:(t + 1) * 128], rhs=Win[:, kc, C_V:C_V + 128],
                                                                            start=(kc == 0), stop=(kc == 7)), reads=["Win", xmn], writes=[bname(tb)])
                        va, vn = vtb.next()
                        S.op("dve", lambda e, va=va, tb=tb: e.tensor_copy(out=va[:, 0:128], in_=bank(tb)[:, 0:128]), reads=[bname(tb)], writes=[vn])
                        S.op("sp", lambda e, va=va, t=t: e.dma_start(out=vs_scr[t0 + t * 128:t0 + (t + 1) * 128, :], in_=va[:, 0:128]), reads=[vn], dma=True)
                tasks.append((lambda: None, rest_v))
                return tasks

            load_block(0)
            do_norm(0)
            for bi in range(len(bl)):
                if bi + 1 < len(bl):
                    load_block(bi + 1)
                run_pipeline(make_tasks(bi), 1)

        def fourier_parts(l):
            last = l == DEPTH - 1
            M2 = A.alloc([64, 64], BF16)
            Yin = A.alloc([64, 256], BF16)
            Zs = A.alloc([64, 256], BF16)
            fouT = A.alloc([2, S_LAT], BF16)
            parts = {}

            def setup():
                S.op("pool", lambda e: e.dma_start(out=M2, in_=m2d), writes=["M2"], dma=True)
                for ri in range(2):
                    for cq4 in range(4):
                        S.op("sp", lambda e, ri=ri, cq4=cq4: e.dma_start(out=Yin[ri * 64:(ri + 1) * 64, cq4 * 16:(cq4 + 1) * 16, :], in_=yv[ri, :, cq4 * 16:(cq4 + 1) * 16, :]),
                             writes=["Yin%d" % cq4], dma=True)
            yv = y_scr[0:S_LAT].rearrange("(r c) (ri ch) -> ri r c ch", c=64, ri=2)
            Yf = Yin.rearrange("p c ch -> p (c ch)")
            Zf = Zs.rearrange("p c ch -> p (c ch)")
            Zin = Yin
            zv = z_scr.rearrange("(ri ka) c ch -> ri c ka ch", ri=2)

            def stage1(i0, i1):
              for i in range(i0, i1):
                b = 6 + i % 2
                S.op("pe", lambda e, i=i, b=b: e.matmul(bank(b), lhsT=D1, rhs=Yf[:, i * 512:(i + 1) * 512], start=True, stop=True),
                     reads=["Yin%d" % (i // 8), "cm"], writes=[bname(b)])
                S.op("dve", lambda e, i=i, b=b: e.tensor_copy(out=Zf[:, i * 512:(i + 1) * 512], in_=bank(b)), reads=[bname(b)], writes=["Zs%d" % (i // 8)])
                if i % 8 == 7:
                    q4 = i // 8
                    S.op("sp", lambda e, q4=q4: e.dma_start(out=z_scr[:, q4 * 16:(q4 + 1) * 16, :], in_=Zs[:, q4 * 16:(q4 + 1) * 16, :]),
                         reads=["Zs%d" % q4], writes=["z_scr%d" % q4], dma=True)
            def stage2_load():
                for ri in range(2):
                    for k4 in range(4):
                        S.op("sp", lambda e, ri=ri, k4=k4: e.dma_start(out=Zin[ri * 64:(ri + 1) * 64, k4 * 16:(k4 + 1) * 16, :], in_=zv[ri, :, k4 * 16:(k4 + 1) * 16, :]),
                             reads=["z_scr%d" % q for q in range(4)], writes=["Zin%d" % k4] + ["Yin%d" % q for q in range(4)], dma=True)

            def stage2(ch, kag0, kag1):
                fv = fouT[:, ch, :].rearrange("p (kb ka) -> p ka kb", ka=64)
                for kag in range(kag0, kag1):
                    b = 6 + kag % 2
                    for a in range(8):
                        ka = kag * 8 + a
                        S.op("pe", lambda e, b=b, a=a, ka=ka, ch=ch: e.matmul(bank(b)[:, a * 64:(a + 1) * 64], lhsT=Zin[:, ka, ch * 128:(ch + 1) * 128], rhs=M2[:, ka, :],
                                                                             start=True, stop=True), reads=["Zin%d" % (ka // 16), "M2"] + ["Yin%d" % q for q in range(4)], writes=[bname(b)])
                    src = bank(b).rearrange("p (a kb) -> p a kb", a=8)
                    dst = fv[:, kag * 8:(kag + 1) * 8, :]
                    S.op("dve", lambda e, src=src, dst=dst: e.tensor_copy(out=dst, in_=src), reads=[bname(b)], writes=["fouT%d" % ch])
                if kag1 == 8:
                    S.op("sp", lambda e, ch=ch: e.dma_start(out=mixT_scr[ch * 128:(ch + 1) * 128, 0:S_LAT], in_=fouT[:, ch, :]), reads=["fouT%d" % ch], dma=True)

            if not last:
                D256 = A.alloc([2, 2, 256], BF16)
                yc = A.alloc([2, 512], BF16)
                fc = A.alloc([2, 256], BF16)

            def ctxpart():
                if last:
                    return
                S.op("pool", lambda e: e.dma_start(out=D256, in_=d256d), writes=["D256"], dma=True)
                S.op("sp", lambda e: e.dma_start(out=yc, in_=y_scr[S_LAT:T].rearrange("(t p) n -> p t n", p=128)), reads=["y_scr"], writes=["yc"], dma=True)
                for ch in range(2):
                    b = 6 + ch
                    k = 0
                    for t in range(2):
                        for ri in range(2):
                            S.op("pe", lambda e, b=b, t=t, ri=ri, ch=ch, k=k: e.matmul(bank(b)[:, 0:256], lhsT=yc[:, t, ri * 256 + ch * 128: ri * 256 + (ch + 1) * 128],
                                                                                    rhs=D256[:, t, ri, :], start=(k == 0), stop=(k == 3)), reads=["yc", "D256"], writes=[bname(b)])
                            k += 1
                    S.op("act", lambda e, b=b, ch=ch: e.activation(out=fc[:, ch, :], in_=bank(b)[:, 0:256], func=AF.Copy), reads=[bname(b)], writes=["fc"])
                    S.op("sp", lambda e, ch=ch: e.dma_start(out=mixT_scr[ch * 128:(ch + 1) * 128, S_LAT:T], in_=fc[:, ch, :]), reads=["fc"], dma=True)
            return setup, stage1, stage2_load, stage2, ctxpart

        def run_pipeline(tasks, look):
            n = len(tasks)
            for i in range(min(look, n)):
                tasks[i][0]()
            for i in range(n):
                if i + look < n:
                    tasks[i + look][0]()
                tasks[i][1]()

        def attn_finish(ob, W, exp_ap, row0, t0, dnr, rcr, ob_ring):
            o = bank(ob)
            dn, dnn = dnr.next()
            rc, rcn = rcr.next()
            if exp_ap is not None:
                S.op("act", lambda e: e.activation(out=dn[64:128, 0:W], in_=o[64:128, 0:W], func=AF.Ln, bias=exp_ap, scale=1.0),
                     reads=[bname(ob), "exps"], writes=[dnn])
                S.op("act", lambda e: e.activation(out=rc[64:128, 0:W], in_=dn[64:128, 0:W], func=AF.Exp, scale=-1.0), reads=[dnn], writes=[rcn])
            else:
                S.op("dve", lambda e: e.reciprocal(out=rc[64:128, 0:W], in_=o[64:128, 0:W]), reads=[bname(ob)], writes=[rcn])
            oa, on = ob_ring.next()
            S.op("dve", lambda e: e.tensor_tensor(out=oa[0:64, 0:W], in0=o[0:64, 0:W], in1=rc[64:128, 0:W], op=ALU.mult), reads=[bname(ob), rcn], writes=[on])
            S.op("sp", lambda e: e.dma_start(out=mixT_scr[row0:row0 + 64, t0:t0 + W], in_=oa[0:64, 0:W]), reads=[on], dma=True)

        def phase_B1(l):
            S.barrier()
            A.off = PERSIST
            last = l == DEPTH - 1
            Ks = A.alloc([T], BF16)
            Vs = A.alloc([34, 2, 128], BF16)
            masks = A.alloc([6, 512], BF16)
            Qring = Ring(A, "Qs", 3, [T], BF16)
            pT = Ring(A, "pT", 4, [512], BF16)
            dnr = Ring(A, "dn", 2, [512], F32)
            rcr = Ring(A, "rc", 2, [512], F32)
            obr = Ring(A, "ob", 2, [512], BF16)
            S.op("pool", lambda e: e.dma_start(out=masks, in_=masksd), writes=["masks"], dma=True)
            S.op("sp", lambda e: e.dma_start(out=Ks, in_=ks_scr), writes=["Ks"], dma=True)
            S.op("dve", lambda e: e.memset(Vs, 1.0), writes=["Vs"])
            vv = vs_scr.rearrange("(blk p) (kvh d) -> p blk kvh d", p=128, kvh=2)
            vnames = []
            for g0 in range(0, 34, 12):
                g1 = min(34, g0 + 12)
                for kvh in range(2):
                    S.op("sp", lambda e, g0=g0, g1=g1, kvh=kvh: e.dma_start(out=Vs[:, g0:g1, kvh, 0:64], in_=vv[:, g0:g1, kvh, :]), reads=["Vs"], writes=["Vs%d_%d" % (g0, kvh)], dma=True)
                    vnames.append("Vs%d_%d" % (g0, kvh))
            bl = blocks(512, not last)
            Qs = []
            for c in range(3):
                Qa, Qn = Qring.next()
                S.op("sp", lambda e, Qa=Qa, c=c: e.dma_start(out=Qa, in_=qs_scr[c]), writes=[Qn], dma=True)
                Qs.append((Qa, Qn))
            f_setup, f_stage1, f_s2load, f_stage2, f_ctx = fourier_parts(l)
            f_setup()
            extra = {40: lambda: f_stage1(0, 8), 56: lambda: f_stage1(8, 16), 72: lambda: f_stage1(16, 24),
                     88: lambda: (f_stage1(24, 32), f_s2load()),
                     160: lambda: f_stage2(0, 0, 4), 180: lambda: f_stage2(0, 4, 8), 200: lambda: f_stage2(1, 0, 4), 220: lambda: f_stage2(1, 4, 8),
                     240: f_ctx}
            tasks = []
            nextra = [0]
            sb_i = 0
            ob_i = 0
            for c in range(3):
                Qa, Qn = Qs[c]
                for hh in range(2):
                    head = c + 3 * hh
                    r0 = hh * 64
                    for (t0, W) in bl:
                        if t0 < S_LAT:
                            qb_ = t0 // 512
                            kbs = [(32, None)] + [(kb, kb - 4 * qb_ + 1) for kb in range(4 * qb_ - 1, 4 * qb_ + 5) if 0 <= kb < 32] + [(33, None)]
                        else:
                            kbs = [(32, None), (33, None)]
                        ob = 4 + ob_i % 2
                        ob_i += 1
                        for i, (kb, mi) in enumerate(kbs):
                            sbk = sb_i % 4
                            sb_i += 1
                            pa, pn = pT.next()

                            c0, c1 = (0, W) if mi is None else ((0, 128), (0, 256), (0, 384), (128, 512), (256, 512), (384, 512))[mi]

                            def first(sbk=sbk, kb=kb, Qa=Qa, Qn=Qn, r0=r0, t0=t0, c0=c0, c1=c1):
                                S.op("pe", lambda e: e.matmul(bank(sbk)[:, c0:c1], lhsT=Ks[r0:r0 + 64, kb * 128:(kb + 1) * 128], rhs=Qa[r0:r0 + 64, t0 + c0:t0 + c1],
                                                              start=True, stop=True), reads=["Ks", Qn], writes=[bname(sbk)])

                            def rest(sbk=sbk, kb=kb, mi=mi, pa=pa, pn=pn, ob=ob, i=i, nk=len(kbs), hh=hh, head=head, t0=t0, W=W, c0=c0, c1=c1):
                                S.op("act", lambda e: e.activation(out=pa[:, c0:c1], in_=bank(sbk)[:, c0:c1], func=AF.Exp, scale=SWA_SCALE), reads=[bname(sbk)], writes=[pn])
                                if mi is not None:
                                    S.op("dve", lambda e: e.tensor_tensor(out=pa[:, c0:c1], in0=pa[:, c0:c1], in1=masks[:, mi, c0:c1], op=ALU.mult),
                                         reads=[pn, "masks"], writes=[pn])
                                S.op("pe", lambda e: e.matmul(bank(ob)[:, c0:c1], lhsT=Vs[:, kb, hh, :], rhs=pa[:, c0:c1], start=(i == 0), stop=(i == nk - 1),
                                                              skip_group_check=(0 < i < nk - 1)),
                                     reads=[pn, "Vs"] + vnames, writes=[bname(ob)])
                                if i == nk - 1:
                                    attn_finish(ob, W, exps[64:128, head:head + 1], 256 + head * 64, t0, dnr, rcr, obr)

                            tasks.append((first, rest))
                            if len([t for t in tasks if t[0] is not None]) and (len(tasks) - nextra[0]) in extra:
                                tasks.append((lambda: None, extra.pop(len(tasks) - nextra[0])))
                                nextra[0] += 1
            assert not extra, extra
            run_pipeline(tasks, 3)

        def phase_B2(l):
            S.barrier()
            A.off = PERSIST
            last = l == DEPTH - 1
            Qr = Ring(A, "Qm", 2, [T], BF16)
            Kr = Ring(A, "Km", 2, [T], BF16)
            Vr = Ring(A, "Vm", 2, [34, 128], BF16)
            pT = Ring(A, "pT2", 3, [2, 512], BF16)
            dnr = Ring(A, "dn2", 2, [512], F32)
            rcr = Ring(A, "rc2", 2, [512], F32)
            obr = Ring(A, "ob2", 2, [512], BF16)
            for va, vn in zip(Vr.bufs, Vr.names):
                S.op("dve", lambda e, va=va: e.memset(va, 1.0), writes=[vn])
            vv = vm_scr.rearrange("(blk p) (h d) -> p blk h d", p=128, h=6)
            bl = blocks(512, not last)
            hl = {}

            def load_head(hd):
                Qa, Qn = Qr.next()
                Ka, Kn = Kr.next()
                Va, Vn = Vr.next()
                S.op("sp", lambda e, Qa=Qa, hd=hd: e.dma_start(out=Qa[0:96, :], in_=qm_scr[hd]), writes=[Qn], dma=True)
                S.op("sp", lambda e, Ka=Ka, hd=hd: e.dma_start(out=Ka[0:64, :], in_=kn_scr[hd * 64:(hd + 1) * 64, :]), writes=[Kn], dma=True)
                S.op("sp", lambda e, Ka=Ka: e.dma_start(out=Ka[64:96, :], in_=kr_scr), writes=[Kn + "r"], dma=True)
                vparts = []
                for g0 in range(0, 34, 9):
                    g1 = min(34, g0 + 9)
                    S.op("sp", lambda e, Va=Va, g0=g0, g1=g1, hd=hd: e.dma_start(out=Va[:, g0:g1, 0:64], in_=vv[:, g0:g1, hd, :]), reads=[Vn], writes=[Vn + "_%d" % g0], dma=True)
                    vparts.append(Vn + "_%d" % g0)
                hl[hd] = (Qa, Qn, Ka, Kn, Va, Vn, vparts)

            tasks = []
            sp_i = 0
            ob_i = 0
            for hd in range(6):
                for bi, (t0, W) in enumerate(bl):
                    kbs = list(range(34)) if t0 < S_LAT else [32, 33]
                    ob = 4 + ob_i % 2
                    ob_i += 1
                    npairs = len(kbs) // 2
                    for pi in range(npairs):
                        sp_ = [0, 2, 6][sp_i % 3]
                        sp_i += 1
                        pa, pn = pT.next()
                        need_load = (bi == 0 and pi == 0)
                        prefetch = (bi == 1 and pi == 0)

                        def first(hd=hd, sp_=sp_, kb0=kbs[2 * pi], kb1=kbs[2 * pi + 1], t0=t0, W=W, need_load=need_load, prefetch=prefetch):
                            if need_load and hd not in hl:
                                load_head(hd)
                            if prefetch and hd + 1 < 6:
                                load_head(hd + 1)
                            Qa, Qn, Ka, Kn, Va, Vn, vparts = hl[hd]
                            for u, kb in enumerate((kb0, kb1)):
                                S.op("pe", lambda e, u=u, kb=kb: e.matmul(bank(sp_ + u)[:, 0:W], lhsT=Ka[0:96, kb * 128:(kb + 1) * 128], rhs=Qa[0:96, t0:t0 + W],
                                                                        start=True, stop=True), reads=[Kn, Kn + "r", Qn], writes=[bname(sp_ + u)])

                        def rest(hd=hd, sp_=sp_, kb0=kbs[2 * pi], kb1=kbs[2 * pi + 1], t0=t0, W=W, pa=pa, pn=pn, ob=ob, pi=pi, npairs=npairs):
                            Qa, Qn, Ka, Kn, Va, Vn, vparts = hl[hd]
                            src = pall[:, sp_ * 512:(sp_ + 2) * 512].rearrange("p (b w) -> p b w", b=2)[:, :, 0:W]
                            S.op("act", lambda e: e.activation(out=pa[:, :, 0:W], in_=src, func=AF.Exp, scale=MLA_SCALE),
                                 reads=[bname(sp_), bname(sp_ + 1)], writes=[pn])
                            for u, kb in enumerate((kb0, kb1)):
                                first_ = (pi == 0 and u == 0)
                                last_ = (pi == npairs - 1 and u == 1)
                                S.op("pe", lambda e, u=u, kb=kb, first_=first_, last_=last_: e.matmul(bank(ob)[:, 0:W], lhsT=Va[:, kb, :], rhs=pa[:, u, 0:W], start=first_, stop=last_),
                                     reads=[pn, Vn] + vparts, writes=[bname(ob)])
                            if pi == npairs - 1:
                                attn_finish(ob, W, None, 640 + hd * 64, t0, dnr, rcr, obr)

                        tasks.append((first, rest))
            run_pipeline(tasks, 2)

        def phase_C1(l):
            S.barrier()
            A.off = PERSIST
            last = l == DEPTH - 1
            Wo = A.alloc([8, D], BF16)
            mring = Ring(A, "mixT", 2, [8, 512], BF16)
            hring = Ring(A, "hC1", 8, [D], F32)
            tmpr = Ring(A, "tmpC1", 2, [512], F32)
            load_gate(l, 2)
            wv = w_out[l].rearrange("(k p) n -> p k n", p=128)
            for kc in range(0, 8, 2):
                S.op("pool", lambda e, kc=kc: e.dma_start(out=Wo[:, kc:kc + 2, :], in_=wv[:, kc:kc + 2, :]), writes=["Wo"], dma=True)
            src_lat = x if l == 0 else h_scr[0:S_LAT]
            src_ctx = ctx if l == 0 else h_scr[S_LAT:T]
            bl = blocks(512, not last)
            mv = mixT_scr.rearrange("(k p) t -> p k t", p=128)
            loaded = {}

            def load_block(bi):
                t0, W = bl[bi]
                ma, mn = mring.next()
                S.op("sp", lambda e, ma=ma: e.dma_start(out=ma[:, :, 0:W], in_=mv[:, :, t0:t0 + W]), reads=["mixT_scr"], writes=[mn], dma=True)
                tiles = []
                for t in range(W // 128):
                    ha, hn = hring.next()
                    tok = t0 + t * 128
                    src = src_lat[tok:tok + 128] if tok < S_LAT else src_ctx[tok - S_LAT:tok - S_LAT + 128]
                    S.op("sp", lambda e, ha=ha, src=src: e.dma_start(out=ha, in_=src), reads=["h_scr%d" % (tok // 128)], writes=[hn], dma=True)
                    tiles.append((ha, hn))
                loaded[bi] = (ma, mn, tiles)

            load_block(0)
            bi_ = [0]
            for bi, (t0, W) in enumerate(bl):
                if bi + 1 < len(bl):
                    load_block(bi + 1)
                ma, mn, tiles = loaded.pop(bi)
                j = 0 if t0 < S_LAT else 1
                for t in range(W // 128):
                    ha, hn = tiles[t]
                    for n in range(2):
                        b = bi_[0] % 4
                        bi_[0] += 1
                        for kc in range(8):
                            S.op("pe", lambda e, b=b, kc=kc, t=t, n=n, ma=ma: e.matmul(bank(b), lhsT=ma[:, kc, t * 128:(t + 1) * 128], rhs=Wo[:, kc, n * 512:(n + 1) * 512],
                                                                                    start=(kc == 0), stop=(kc == 7)), reads=[mn, "Wo"], writes=[bname(b)])
                        ta, tn = tmpr.next()
                        S.op("dve", lambda e, b=b, ta=ta, n=n: e.tensor_tensor(out=ta, in0=bank(b), in1=gbc[:, j, n * 512:(n + 1) * 512], op=ALU.mult),
                             reads=[bname(b), "gbc"], writes=[tn])
                        S.op("dve", lambda e, ta=ta, ha=ha, n=n: e.tensor_tensor(out=ha[:, n * 512:(n + 1) * 512], in0=ha[:, n * 512:(n + 1) * 512], in1=ta, op=ALU.add),
                             reads=[tn, hn], writes=[hn])
                    tok = t0 + t * 128
                    S.op("sp", lambda e, ha=ha, tok=tok: e.dma_start(out=h_scr[tok:tok + 128, :], in_=ha), reads=[hn], writes=["h_scr%d" % (tok // 128)], dma=True)

        def phase_C2(l):
            S.barrier()
            A.off = PERSIST
            last = l == DEPTH - 1
            W1 = A.alloc([8, DFF], BF16)
            W2 = A.alloc([32, D], BF16)
            hid = A.alloc([32, 256], BF16)
            hring = Ring(A, "hC2", 4, [D], F32)
            xhring = Ring(A, "xhC2", 3, [D], BF16)
            xmring = Ring(A, "xmC2", 2, [8, 256], BF16)
            ssb = A.alloc([12], F32)
            rr = Ring(A, "rr", 2, [256], F32)
            tmpr = Ring(A, "tmpC2", 2, [512], F32)
            fss = A.alloc([4], F32)
            load_gate(l, 5)
            wv1 = w_mlp1[l].rearrange("(k p) n -> p k n", p=128)
            wv2 = w_mlp2[l].rearrange("(k p) n -> p k n", p=128)
            for kc in range(8):
                S.op("pool", lambda e, kc=kc: e.dma_start(out=W1[:, kc, :], in_=wv1[:, kc, :]), writes=["W1_%d" % kc], dma=True)
            for kc in range(0, 32, 4):
                S.op("pool", lambda e, kc=kc: e.dma_start(out=W2[:, kc:kc + 4, :], in_=wv2[:, kc:kc + 4, :]), writes=["W2_%d" % (kc // 4)], dma=True)
            w1n = ["W1_%d" % k for k in range(8)]
            w2n = ["W2_%d" % k for k in range(8)]
            bl = blocks(256, not last)
            loaded = {}

            def load_block(bi):
                t0, W = bl[bi]
                tiles = []
                for t in range(W // 128):
                    ha, hn = hring.next()
                    tok = t0 + t * 128
                    S.op("sp", lambda e, ha=ha, tok=tok: e.dma_start(out=ha, in_=h_scr[tok:tok + 128, :]), reads=["h_scr%d" % (tok // 128)], writes=[hn], dma=True)
                    tiles.append((ha, hn))
                loaded[bi] = tiles

            load_block(0)
            ub = [0]
            ob = [0]
            xms = {}

            def do_norm(bi):
                t0, W = bl[bi]
                xm, xmn = xmring.next()
                norm_mod_T(loaded[bi], W // 128, 1, 0 if t0 < S_LAT else 1, xm, xmn, ssb, xhring, [0, 1])
                xms[bi] = (xm, xmn)

            do_norm(0)
            for bi, (t0, W) in enumerate(bl):
                if bi + 1 < len(bl):
                    load_block(bi + 1)
                tiles = loaded[bi]
                nt = W // 128
                j = 0 if t0 < S_LAT else 1
                xm, xmn = xms.pop(bi)
                for c in range(32):
                    b = 2 + ub[0] % 4
                    ub[0] += 1
                    for kc in range(8):
                        S.op("pe", lambda e, b=b, c=c, kc=kc: e.matmul(bank(b)[:, 0:W], lhsT=W1[:, kc, c * 128:(c + 1) * 128], rhs=xm[:, kc, 0:W], start=(kc == 0), stop=(kc == 7)),
                             reads=[xmn, w1n[kc]], writes=[bname(b)])
                    ra, rn_ = rr.next()
                    S.op("act", lambda e, b=b, ra=ra: e.activation(out=ra[:, 0:W], in_=bank(b)[:, 0:W], func=AF.Relu), reads=[bname(b)], writes=[rn_])
                    S.op("dve", lambda e, ra=ra, c=c: e.tensor_tensor(out=hid[:, c, 0:W], in0=ra[:, 0:W], in1=ra[:, 0:W], op=ALU.mult), reads=[rn_], writes=["hid%d" % c])
                hidn = ["hid%d" % c for c in range(32)]
                if bi + 1 < len(bl):
                    do_norm(bi + 1)
                loaded.pop(bi)
                for t in range(nt):
                    ha, hn = tiles[t]
                    for n in range(2):
                        b = 6 + ob[0] % 2
                        ob[0] += 1
                        for kc in range(32):
                            S.op("pe", lambda e, b=b, kc=kc, t=t, n=n: e.matmul(bank(b), lhsT=hid[:, kc, t * 128:(t + 1) * 128], rhs=W2[:, kc, n * 512:(n + 1) * 512],
                                                                             start=(kc == 0), stop=(kc == 31)), reads=[hidn[kc], w2n[kc // 4]], writes=[bname(b)])
                        ta, tn = tmpr.next()
                        S.op("dve", lambda e, b=b, ta=ta, n=n: e.tensor_tensor(out=ta, in0=bank(b), in1=gbc[:, j, n * 512:(n + 1) * 512], op=ALU.mult),
                             reads=[bname(b), "gbc"], writes=[tn])
                        S.op("dve", lambda e, ta=ta, ha=ha, n=n: e.tensor_tensor(out=ha[:, n * 512:(n + 1) * 512], in0=ha[:, n * 512:(n + 1) * 512], in1=ta, op=ALU.add),
                             reads=[tn, hn], writes=[hn])
                    tok = t0 + t * 128
                    if not last:
                        S.op("sp", lambda e, ha=ha, tok=tok: e.dma_start(out=h_scr[tok:tok + 128, :], in_=ha), reads=[hn], writes=["h_scr%d" % (tok // 128)], dma=True)
                    else:
                        oa, on = ha, hn
                        S.op("act", lambda e, ha=ha: e.activation(out=xhring.bufs[-1], in_=ha, func=AF.Square, accum_out=fss[:, 0:1]), reads=[hn], writes=["junk", "fss"])
                        S.op("act", lambda e: e.activation(out=fss[:, 1:2], in_=fss[:, 0:1], func=AF.Ln, scale=1.0 / D, bias=epst[:, 0:1]), reads=["fss", "eps"], writes=["fss"])
                        S.op("act", lambda e: e.activation(out=fss[:, 2:3], in_=fss[:, 1:2], func=AF.Exp, scale=-0.5), reads=["fss"], writes=["fss"])
                        S.op("dve", lambda e, ha=ha, oa=oa: e.scalar_tensor_tensor(out=oa, in0=ha, scalar=fss[:, 2:3], in1=fgbc, op0=ALU.mult, op1=ALU.mult),
                             reads=[hn, "fss", "fgbc"], writes=[hn])
                        S.op("sp", lambda e, oa=oa, tok=tok: e.dma_start(out=out[tok:tok + 128, :], in_=oa), reads=[on], dma=True, final=True)

        stop_after = [d for d in debug if d.startswith("stop:")]
        stop_after = stop_after[0][5:] if stop_after else None
        done = False
        for l in range(DEPTH):
            for nm, ph in (("mod", phase_mod), ("A", phase_A), ("B1", phase_B1), ("B2", phase_B2), ("C1", phase_C1), ("C2", phase_C2)):
                ph(l)
                if stop_after == "%s%d" % (nm, l):
                    done = True
                    break
            if done:
                break
        if done:
            S.barrier()
            S.op("sp", lambda e: e.dma_start(out=out[0:128, :], in_=fgbc), dma=True, final=True)
        S.emit()
    return nc


def _host_inputs(inputs):
    f = lambda a: np.ascontiguousarray(np.asarray(a, dtype=np.float32))
    x = f(inputs["x"])
    c = f(inputs["c"])
    ctx = f(inputs["ctx"])
    c_ctx = f(inputs["c_ctx"])
    shared = {}
    shared["w_ada"] = f(inputs["w_ada"])
    shared["b_adaT"] = f(f(inputs["b_ada"]).reshape(DEPTH, 48, 128).transpose(0, 2, 1))
    shared["n1T"] = f(f(inputs["norm1_g"]).reshape(DEPTH, 8, 128).transpose(0, 2, 1))
    shared["n2T"] = f(f(inputs["norm2_g"]).reshape(DEPTH, 8, 128).transpose(0, 2, 1))
    shared["fg"] = f(inputs["final_norm_g"]).reshape(1, D)
    shared["w_in"] = f(f(inputs["w_in"])[:, :, _perm_in()])
    shared["w_f"] = f(inputs["w_fourier"])
    shared["sink"] = f(inputs["swa_sink"])
    shared["gqT"] = f(f(inputs["mla_q_norm"]).reshape(DEPTH, 2, 128).transpose(0, 2, 1))
    shared["gkvT"] = f(f(inputs["mla_kv_norm"]).reshape(DEPTH, 1, 128).transpose(0, 2, 1))
    shared["w_uq"] = f(inputs["w_uq"])
    shared["w_ukv"] = f(f(inputs["w_ukv"])[:, :, _perm_ukv()])
    shared["w_out"] = f(inputs["w_out"])
    shared["w_mlp1"] = f(inputs["w_mlp1"])
    shared["w_mlp2"] = f(inputs["w_mlp2"])
    shared.update(_consts())
    maps = []
    for b in range(8):
        m = dict(shared)
        m["x"] = x[b]
        m["ctx"] = ctx[b]
        cv = np.stack([c[b], c_ctx], 0)
        m["cT"] = f(cv.reshape(2, 8, 128).transpose(2, 1, 0))
        maps.append(m)
    return maps


_NC_CACHE = {}


def kernel(**inputs):
    maps = _host_inputs(inputs)
    if "nc" not in _NC_CACHE:
        _NC_CACHE["nc"] = build()
    res = run_bass_kernel_spmd(_NC_CACHE["nc"], maps, core_ids=list(range(8)))
    return np.stack([np.asarray(r["out"], dtype=np.float32) for r in res.results], 0)
```
